# Optimizing a Trainium2 kernel written in Bass

```python
import jax, jax.numpy as jnp
from jax import lax
import numpy as np

D_MODEL = 2048
BATCH = 1
SEQ = 8192
DEPTH = 4

CHUNK = 64
N_BRANCH = 4
BRANCH_W = 512
RMS_EPS = 1e-6
GN_EPS = 1e-5
ROPE_THETA = 10000.0

RET_HEADS = 4
RET_DK = 128
RET_DV = 128
SSM_HEADS = 8
SSM_HEADDIM = 64
SSM_GROUPS = 2
SSM_STATE = 128
SSM_CONV = 4
SSM_HPG = SSM_HEADS // SSM_GROUPS
SSM_XBC = SSM_HEADS * SSM_HEADDIM + 2 * SSM_GROUPS * SSM_STATE
MLA_HEADS = 4
MLA_Q_RANK = 512
MLA_KV_RANK = 256
MLA_NOPE = 128
MLA_ROPE = 64
MLA_V = 128
MLA_QK = MLA_NOPE + MLA_ROPE
Q_BLOCK = 128
RWKV_HEADS = 8
RWKV_HEAD = 64
RWKV_W = RWKV_HEADS * RWKV_HEAD
RWKV_W_LORA = 64
RWKV_A_LORA = 64
RWKV_V_LORA = 32
RWKV_G_LORA = 128
RWKV_GN_EPS = 64e-5
RWKV_COLS = 3 * RWKV_W + RWKV_W_LORA + RWKV_A_LORA + RWKV_G_LORA
D_FF = -(-8 * D_MODEL // (3 * 256)) * 256

IN_SIZES = (
    RET_HEADS * RET_DK, RET_HEADS * RET_DK, RET_HEADS * RET_DV, RET_HEADS * RET_DV,
    SSM_HEADS * SSM_HEADDIM, SSM_XBC, SSM_HEADS,
    MLA_Q_RANK, MLA_KV_RANK, MLA_ROPE,
    RWKV_COLS,
    N_BRANCH * D_MODEL,
)
N_IN = sum(IN_SIZES)

kernel_name = "hybrid_ret_ssd_mla_rwkv7_gated_trunk"


def _split(x, sizes):
    idx = np.cumsum(np.array(sizes))[:-1].tolist()
    return jnp.split(x, idx, axis=-1)


def rms_norm(x, w, eps=RMS_EPS):
    xf = x.astype(jnp.float32)
    y = xf * lax.rsqrt(jnp.mean(xf * xf, axis=-1, keepdims=True) + eps)
    return (y * w.astype(jnp.float32)).astype(x.dtype)


def head_layer_norm(x, eps):
    xf = x.astype(jnp.float32)
    xc = xf - jnp.mean(xf, axis=-1, keepdims=True)
    return xc * lax.rsqrt(jnp.mean(xc * xc, axis=-1, keepdims=True) + eps)


def rope_tables(positions, dim):
    inv = 1.0 / (ROPE_THETA ** (jnp.arange(0, dim, 2, dtype=jnp.float32) / dim))
    ang = positions.astype(jnp.float32)[..., None] * inv
    return jnp.cos(ang), jnp.sin(ang)


def apply_rope(x, cos, sin):
    half = x.shape[-1] // 2
    x1 = x[..., :half].astype(jnp.float32)
    x2 = x[..., half:].astype(jnp.float32)
    c = cos[:, :, None, :]
    s = sin[:, :, None, :]
    return jnp.concatenate([x1 * c - x2 * s, x2 * c + x1 * s], axis=-1).astype(x.dtype)


def retention_mixer(q, k, v, g, cos, sin, gn_w):
    B, S, _ = q.shape
    NC = S // CHUNK
    H, DK, DV = RET_HEADS, RET_DK, RET_DV
    f32 = jnp.float32
    q = apply_rope(q.reshape(B, S, H, DK), cos, sin).astype(f32)
    k = apply_rope(k.reshape(B, S, H, DK), cos, sin).astype(f32) * (DK ** -0.5)
    v = v.reshape(B, S, H, DV).astype(f32)
    log_gamma = jnp.log1p(-jnp.exp2(-5.0 - jnp.arange(H, dtype=f32)))
    pos = jnp.arange(CHUNK, dtype=f32)
    intra_decay = jnp.exp(log_gamma[:, None, None] * jnp.abs(pos[:, None] - pos[None, :]))
    q_decay = jnp.exp(log_gamma[:, None] * (pos + 1.0))
    k_decay = jnp.exp(log_gamma[:, None] * (CHUNK - 1.0 - pos))
    chunk_decay = jnp.exp(log_gamma * CHUNK)[None, :, None, None]
    qc = q.reshape(B, NC, CHUNK, H, DK)
    kc = k.reshape(B, NC, CHUNK, H, DK)
    vc = v.reshape(B, NC, CHUNK, H, DV)
    scores = jnp.einsum('bclhd,bcmhd->bchlm', qc, kc) * intra_decay
    intra = jnp.einsum('bchlm,bcmhe->bclhe', scores, vc)
    kv = jnp.einsum('bcmhd,hm,bcmhe->bchde', kc, k_decay, vc)

    def step(state, kv_c):
        return chunk_decay * state + kv_c, state

    _, s_prev = lax.scan(step, jnp.zeros((B, H, DK, DV), f32), jnp.moveaxis(kv, 1, 0))
    s_prev = jnp.moveaxis(s_prev, 0, 1)
    inter = jnp.einsum('bclhd,bchde,hl->bclhe', qc, s_prev, q_decay)
    o = head_layer_norm((intra + inter).reshape(B, S, H, DV), GN_EPS) * gn_w.astype(f32).reshape(H, DV)
    return (jax.nn.silu(g.astype(f32)) * o.reshape(B, S, H * DV)).astype(g.dtype)


def causal_depthwise_conv(x, w, b):
    K, C = w.shape
    y = lax.conv_general_dilated(x, w.astype(x.dtype)[:, None, :], window_strides=(1,),
                                 padding=[(K - 1, 0)], dimension_numbers=('NWC', 'WIO', 'NWC'),
                                 feature_group_count=C)
    return y + b.astype(x.dtype)


def ssd_chunked(x, a_dt, bm, cm):
    B, S, G, HG, P = x.shape
    N = bm.shape[-1]
    NC = S // CHUNK
    f32 = jnp.float32
    xc = x.reshape(B, NC, CHUNK, G, HG, P)
    ac = a_dt.reshape(B, NC, CHUNK, G, HG)
    bc = bm.reshape(B, NC, CHUNK, G, N)
    cc = cm.reshape(B, NC, CHUNK, G, N)
    a_cs = jnp.cumsum(ac, axis=2)
    causal = jnp.tril(jnp.ones((CHUNK, CHUNK), dtype=bool))[:, :, None, None]
    seg = a_cs[:, :, :, None] - a_cs[:, :, None, :]
    decay_lm = jnp.exp(jnp.where(causal, seg, -jnp.inf))
    cb = jnp.einsum('bclgn,bcmgn->bclmg', cc, bc)
    y_diag = jnp.einsum('bclmgh,bcmghp->bclghp', cb[..., None] * decay_lm, xc)
    decay_to_end = jnp.exp(a_cs[:, :, -1:] - a_cs)
    states = jnp.einsum('bclgn,bclgh,bclghp->bcghpn', bc, decay_to_end, xc)
    chunk_decay = jnp.exp(a_cs[:, :, -1])[..., None, None]

    def step(state, inp):
        dec, st = inp
        return state * dec + st, state

    _, s_prev = lax.scan(step, jnp.zeros((B, G, HG, P, N), f32),
                         (jnp.moveaxis(chunk_decay, 1, 0).astype(f32), jnp.moveaxis(states, 1, 0).astype(f32)))
    s_prev = jnp.moveaxis(s_prev, 0, 1)
    y_off = jnp.einsum('bclgn,bcghpn,bclgh->bclghp', cc, s_prev, jnp.exp(a_cs))
    return (y_diag + y_off).reshape(B, S, G, HG, P)


def mamba2_mixer(z, xbc, dt, conv_w, conv_b, dt_bias, a_log, d_skip, norm_w):
    B, S, _ = z.shape
    G, HG, P, N = SSM_GROUPS, SSM_HPG, SSM_HEADDIM, SSM_STATE
    f32 = jnp.float32
    xbc = jax.nn.silu(causal_depthwise_conv(xbc, conv_w, conv_b))
    xs, bm, cm = _split(xbc, (SSM_HEADS * P, G * N, G * N))
    x = xs.reshape(B, S, G, HG, P).astype(f32)
    bm = bm.reshape(B, S, G, N).astype(f32)
    cm = cm.reshape(B, S, G, N).astype(f32)
    dt = jax.nn.softplus(dt.astype(f32) + dt_bias.astype(f32)).reshape(B, S, G, HG)
    a = -jnp.exp(a_log.astype(f32)).reshape(G, HG)
    y = ssd_chunked(x * dt[..., None], dt * a, bm, cm)
    y = y + x * d_skip.astype(f32).reshape(G, HG, 1)
    y = y.reshape(B, S, G, HG * P) * jax.nn.silu(z.astype(f32)).reshape(B, S, G, HG * P)
    y = rms_norm(y, norm_w.reshape(G, HG * P))
    return y.reshape(B, S, G * HG * P).astype(z.dtype)


def block_causal_attention(q, k, v):
    B, S, H, D = q.shape
    Dv = v.shape[-1]
    NB = S // Q_BLOCK
    scale = D ** -0.5
    qb = q.reshape(B, NB, Q_BLOCK, H, D).transpose(1, 0, 3, 2, 4)
    kt = k.transpose(0, 2, 1, 3)
    vt = v.transpose(0, 2, 1, 3)
    key_chunk = jnp.arange(S) // CHUNK

    def one_block(args):
        q_blk, blk = args
        q_chunk = (blk * Q_BLOCK + jnp.arange(Q_BLOCK)) // CHUNK
        s = jnp.einsum('bhqd,bhkd->bhqk', q_blk, kt).astype(jnp.float32) * scale
        s = jnp.where(key_chunk[None, :] <= q_chunk[:, None], s, -jnp.inf)
        p = jax.nn.softmax(s, axis=-1).astype(vt.dtype)
        return jnp.einsum('bhqk,bhkd->bhqd', p, vt)

    o = lax.map(one_block, (qb, jnp.arange(NB)))
    return o.transpose(1, 0, 3, 2, 4).reshape(B, S, H * Dv)


def mla_mixer(cq, ckv, k_rope, cos, sin, q_a_norm_w, w_qb, kv_a_norm_w, w_kvb, q_norm_w, k_norm_w):
    B, S, _ = cq.shape
    H = MLA_HEADS
    q = (rms_norm(cq, q_a_norm_w) @ w_qb).reshape(B, S, H, MLA_QK)
    kv = (rms_norm(ckv, kv_a_norm_w) @ w_kvb).reshape(B, S, H, MLA_NOPE + MLA_V)
    k_nope, v = kv[..., :MLA_NOPE], kv[..., MLA_NOPE:]
    k = jnp.concatenate([k_nope, jnp.broadcast_to(k_rope[:, :, None, :], (B, S, H, MLA_ROPE)).astype(k_nope.dtype)], axis=-1)
    q = rms_norm(q, q_norm_w)
    k = rms_norm(k, k_norm_w)
    q = jnp.concatenate([q[..., :MLA_NOPE], apply_rope(q[..., MLA_NOPE:], cos, sin)], axis=-1)
    k = jnp.concatenate([k[..., :MLA_NOPE], apply_rope(k[..., MLA_NOPE:], cos, sin)], axis=-1)
    return block_causal_attention(q, k, v)


def rwkv7_mixer(p, mu, w0, w2, a0, a2, g2, k_k, k_a, r_k, ln_w, ln_b, v_first, v_res):
    B, S, _ = p.shape
    H, N = RWKV_HEADS, RWKV_HEAD
    f32 = jnp.float32
    p = p.astype(f32)
    p_prev = jnp.pad(p[:, :-1], ((0, 0), (1, 0), (0, 0)))
    pm = p + (p_prev - p) * mu.astype(f32)
    r, k, v, wl, al, gl = _split(pm, (RWKV_W, RWKV_W, RWKV_W, RWKV_W_LORA, RWKV_A_LORA, RWKV_G_LORA))
    w = -jax.nn.softplus(-(w0.astype(f32) + jnp.tanh(wl) @ w2.astype(f32))) - 0.5
    decay = jnp.exp(-jnp.exp(w))
    a = jax.nn.sigmoid(a0.astype(f32) + al @ a2.astype(f32))
    g = jax.nn.sigmoid(gl) @ g2.astype(f32)
    if v_res is None:
        v_first = v
    else:
        v0, v1, v2 = v_res
        v = v + (v_first - v) * jax.nn.sigmoid(v0.astype(f32) + (v @ v1.astype(f32)) @ v2.astype(f32))
    rh = r.reshape(B, S, H, N)
    kh = k.reshape(B, S, H, N)
    vh = v.reshape(B, S, H, N)
    ah = a.reshape(B, S, H, N)
    dh = decay.reshape(B, S, H, N)
    kk = kh * k_k.astype(f32).reshape(H, N)
    kk = kk / jnp.maximum(jnp.sqrt(jnp.sum(kk * kk, axis=-1, keepdims=True)), 1e-12)
    kh = kh * (1.0 + (ah - 1.0) * k_a.astype(f32).reshape(H, N))

    def step(state, inp):
        r_t, w_t, k_t, v_t, a_t, b_t = inp
        sa = jnp.einsum('bhij,bhj->bhi', state, a_t)
        state = state * w_t[:, :, None, :] + sa[..., None] * b_t[:, :, None, :] + v_t[..., None] * k_t[:, :, None, :]
        return state, jnp.einsum('bhij,bhj->bhi', state, r_t)

    xs = (jnp.moveaxis(rh, 1, 0), jnp.moveaxis(dh, 1, 0), jnp.moveaxis(kh, 1, 0),
          jnp.moveaxis(vh, 1, 0), jnp.moveaxis(-kk, 1, 0), jnp.moveaxis(kk * ah, 1, 0))
    _, ys = lax.scan(step, jnp.zeros((B, H, N, N), f32), xs)
    y = jnp.moveaxis(ys, 0, 1)
    y = head_layer_norm(y, RWKV_GN_EPS) * ln_w.astype(f32).reshape(H, N) + ln_b.astype(f32).reshape(H, N)
    y = y + jnp.sum(rh * kh * r_k.astype(f32), axis=-1, keepdims=True) * vh
    return y.reshape(B, S, H * N) * g, v_first


def setup_inputs(seed: int = 0) -> dict:
    key = jax.random.key(seed)
    k = jax.random.split(key, 40)
    f32 = jnp.float32

    def nrm(i, shape, scale):
        return jax.random.normal(k[i], shape, f32) * scale

    def gain(i, shape):
        return 1.0 + nrm(i, shape, 0.02)

    x = jax.random.normal(k[0], (BATCH, SEQ, D_MODEL), f32)
    offset = jax.random.randint(k[1], (BATCH, 1), 0, 4096, dtype=jnp.int32)
    positions = (offset + jnp.arange(SEQ, dtype=jnp.int32)[None, :]).astype(jnp.int32)
    u = jax.random.uniform(k[6], (DEPTH, SSM_HEADS), f32)
    dt0 = jnp.exp(u * (jnp.log(0.1) - jnp.log(0.001)) + jnp.log(0.001))
    return {
        "x": x,
        "positions": positions,
        "norm1_w": gain(2, (DEPTH, D_MODEL)),
        "w_in": nrm(3, (DEPTH, D_MODEL, N_IN), D_MODEL ** -0.5),
        "ret_gn_w": gain(4, (DEPTH, RET_HEADS * RET_DV)),
        "ssm_conv_w": nrm(5, (DEPTH, SSM_CONV, SSM_XBC), SSM_CONV ** -0.5),
        "ssm_conv_b": nrm(7, (DEPTH, SSM_XBC), 0.02),
        "ssm_dt_bias": dt0 + jnp.log(-jnp.expm1(-dt0)),
        "ssm_a_log": jnp.log(jax.random.uniform(k[8], (DEPTH, SSM_HEADS), f32, 1.0, 16.0)),
        "ssm_d": 1.0 + nrm(9, (DEPTH, SSM_HEADS), 0.1),
        "ssm_norm_w": gain(10, (DEPTH, SSM_HEADS * SSM_HEADDIM)),
        "mla_q_a_norm_w": gain(11, (DEPTH, MLA_Q_RANK)),
        "mla_w_qb": nrm(12, (DEPTH, MLA_Q_RANK, MLA_HEADS * MLA_QK), MLA_Q_RANK ** -0.5),
        "mla_kv_a_norm_w": gain(13, (DEPTH, MLA_KV_RANK)),
        "mla_w_kvb": nrm(14, (DEPTH, MLA_KV_RANK, MLA_HEADS * (MLA_NOPE + MLA_V)), MLA_KV_RANK ** -0.5),
        "mla_q_norm_w": gain(15, (DEPTH, MLA_QK)),
        "mla_k_norm_w": gain(16, (DEPTH, MLA_QK)),
        "rwkv_mu": jax.random.uniform(k[17], (DEPTH, RWKV_COLS), f32),
        "rwkv_w0": jax.random.uniform(k[18], (DEPTH, RWKV_W), f32, -6.0, -1.0),
        "rwkv_w2": nrm(19, (DEPTH, RWKV_W_LORA, RWKV_W), 0.1),
        "rwkv_a0": nrm(20, (DEPTH, RWKV_W), 0.1),
        "rwkv_a2": nrm(21, (DEPTH, RWKV_A_LORA, RWKV_W), RWKV_A_LORA ** -0.5),
        "rwkv_g2": nrm(22, (DEPTH, RWKV_G_LORA, RWKV_W), RWKV_G_LORA ** -0.5),
        "rwkv_v0": nrm(23, (DEPTH - 1, RWKV_W), 0.1),
        "rwkv_v1": nrm(24, (DEPTH - 1, RWKV_W, RWKV_V_LORA), RWKV_W ** -0.5),
        "rwkv_v2": nrm(25, (DEPTH - 1, RWKV_V_LORA, RWKV_W), 0.1),
        "rwkv_k_k": 0.85 + nrm(26, (DEPTH, RWKV_W), 0.02),
        "rwkv_k_a": gain(27, (DEPTH, RWKV_W)),
        "rwkv_r_k": nrm(28, (DEPTH, RWKV_HEADS, RWKV_HEAD), 0.1),
        "rwkv_ln_w": gain(29, (DEPTH, RWKV_W)),
        "rwkv_ln_b": nrm(30, (DEPTH, RWKV_W), 0.02),
        "w_branch": nrm(31, (DEPTH, N_BRANCH, BRANCH_W, D_MODEL), BRANCH_W ** -0.5),
        "w_out": nrm(32, (DEPTH, D_MODEL, D_MODEL), D_MODEL ** -0.5),
        "norm2_w": gain(33, (DEPTH, D_MODEL)),
        "ffn_w_gu": nrm(34, (DEPTH, D_MODEL, 2 * D_FF), D_MODEL ** -0.5),
        "ffn_w_down": nrm(35, (DEPTH, D_FF, D_MODEL), D_FF ** -0.5),
    }


def reference(x, positions, norm1_w, w_in, ret_gn_w, ssm_conv_w, ssm_conv_b, ssm_dt_bias, ssm_a_log, ssm_d,
              ssm_norm_w, mla_q_a_norm_w, mla_w_qb, mla_kv_a_norm_w, mla_w_kvb, mla_q_norm_w, mla_k_norm_w,
              rwkv_mu, rwkv_w0, rwkv_w2, rwkv_a0, rwkv_a2, rwkv_g2, rwkv_v0, rwkv_v1, rwkv_v2, rwkv_k_k, rwkv_k_a,
              rwkv_r_k, rwkv_ln_w, rwkv_ln_b, w_branch, w_out, norm2_w, ffn_w_gu, ffn_w_down):
    B, S, D = x.shape
    cos_ret, sin_ret = rope_tables(positions, RET_DK)
    cos_mla, sin_mla = rope_tables(positions, MLA_ROPE)
    v_first = None
    for l in range(DEPTH):
        h = rms_norm(x, norm1_w[l])
        pin = h @ w_in[l]
        (rq, rk, rv, rg, sz, sxbc, sdt, cq, ckv, krope, rw, gate_pre) = _split(pin, IN_SIZES)
        o_a = retention_mixer(rq, rk, rv, rg, cos_ret, sin_ret, ret_gn_w[l])
        o_b = mamba2_mixer(sz, sxbc, sdt, ssm_conv_w[l], ssm_conv_b[l], ssm_dt_bias[l], ssm_a_log[l],
                           ssm_d[l], ssm_norm_w[l])
        o_c = mla_mixer(cq, ckv, krope, cos_mla, sin_mla, mla_q_a_norm_w[l], mla_w_qb[l], mla_kv_a_norm_w[l],
                        mla_w_kvb[l], mla_q_norm_w[l], mla_k_norm_w[l])
        v_res = None if l == 0 else (rwkv_v0[l - 1], rwkv_v1[l - 1], rwkv_v2[l - 1])
        o_d, v_first = rwkv7_mixer(rw, rwkv_mu[l], rwkv_w0[l], rwkv_w2[l], rwkv_a0[l], rwkv_a2[l], rwkv_g2[l],
                                   rwkv_k_k[l], rwkv_k_a[l], rwkv_r_k[l], rwkv_ln_w[l], rwkv_ln_b[l], v_first, v_res)
        o = jnp.stack([o_a.astype(x.dtype), o_b.astype(x.dtype), o_c.astype(x.dtype), o_d.astype(x.dtype)], axis=2)
        br = jnp.einsum('bsnc,ncd->bsnd', o, w_branch[l])
        gates = jax.nn.sigmoid(gate_pre.reshape(B, S, N_BRANCH, D))
        merged = jnp.sum(gates * br, axis=2)
        x = x + (merged @ w_out[l]).astype(x.dtype)
        h2 = rms_norm(x, norm2_w[l])
        gt, up = jnp.split(h2 @ ffn_w_gu[l], 2, axis=-1)
        x = x + ((jax.nn.silu(gt) * up) @ ffn_w_down[l]).astype(x.dtype)
    return x
```

```python
import contextlib
import math
import numpy as np
import ml_dtypes
import concourse.bass as bass
import concourse.mybir as mybir
from concourse.bass_utils import run_bass_kernel_spmd

F32 = mybir.dt.float32
BF16 = mybir.dt.bfloat16
I32 = mybir.dt.int32
AF = mybir.ActivationFunctionType
ALU = mybir.AluOpType

NCORES = 8
D_MODEL = 2048
KC = D_MODEL // 128
D_FF = 5632
FFT = D_FF // 128
RMS_EPS = 1e-6
GN_EPS = 1e-5
RWKV_GN_EPS = 64e-5
N_IN = 14408


class Cfg:
    def __init__(self, S=8192, depth=4):
        self.S = S
        self.depth = depth
        self.TPC = S // NCORES
        self.NT = min(512, self.TPC)
        self.NP = self.TPC // self.NT


class Tn:
    def __init__(self, name, t):
        self.name = name
        self.t = t

    def __getitem__(self, k):
        return self.t[k]


class Sched:
    ND = 6
    SAME = {"pe": False, "act": True, "dve": True, "pool": True}

    def __init__(self, nc, es):
        self.nc = nc
        self.es = es
        self.eng = dict(pe=nc.tensor, act=nc.scalar, dve=nc.vector, pool=nc.gpsimd, sp=nc.sync)
        self.csem = {e: es.enter_context(nc.semaphore("c_" + e)) for e in ("pe", "act", "dve", "pool")}
        self.ccnt = {e: 0 for e in self.csem}
        self.waited = {e: {} for e in self.eng}
        self.dsem = {q: [es.enter_context(nc.semaphore("d_%s%d" % (q, i))) for i in range(self.ND)]
                     for q in ("sp", "pool", "act")}
        self.dcnt = {q: [0] * self.ND for q in self.dsem}
        self.drr = {q: 0 for q in self.dsem}
        self.state = {}
        self.scopes = []
        self.out_tokens = []
        self.n_instr = 0

    def sb(self, name, shape, dtype):
        es = self.scopes[-1] if self.scopes else self.es
        return Tn(name, es.enter_context(self.nc.sbuf_tensor(name, list(shape), dtype)))

    def push_scope(self):
        self.scopes.append(contextlib.ExitStack())

    def pop_scope(self):
        self.barrier()
        self.scopes.pop().close()

    def barrier(self):
        toks = [("c_" + e, self.csem[e], self.ccnt[e], "x") for e in self.csem if self.ccnt[e] > 0]
        for q in self.dsem:
            for i in range(self.ND):
                if self.dcnt[q][i] > 0:
                    toks.append(("d_%s%d" % (q, i), self.dsem[q][i], self.dcnt[q][i] * 16, "dma"))
        for e in self.eng:
            self._emit_waits(e, toks)

    def ps(self, name, shape, dtype=F32):
        return Tn(name, self.es.enter_context(self.nc.psum_tensor(name, list(shape), dtype)))

    def dram(self, name, shape, dtype, kind):
        return Tn(name, self.nc.dram_tensor(name, list(shape), dtype, kind=kind).ap())

    @staticmethod
    def _rk(r):
        if isinstance(r, tuple):
            return (r[0].name, r[1])
        return (r.name, "*")

    def _conf(self, key):
        name, k = key
        if k == "*":
            return [kk for kk in self.state if kk[0] == name]
        out = []
        if (name, k) in self.state:
            out.append((name, k))
        if (name, "*") in self.state:
            out.append((name, "*"))
        return out

    def _collect(self, reads, writes):
        toks = []
        for r in reads:
            for kk in self._conf(self._rk(r)):
                w = self.state[kk][0]
                if w is not None:
                    toks.append(w)
        for r in writes:
            for kk in self._conf(self._rk(r)):
                w, rds = self.state[kk]
                if w is not None:
                    toks.append(w)
                toks.extend(rds)
        return toks

    def _emit_waits(self, e, toks):
        best = {}
        for (sn, sh, val, pe) in toks:
            if pe == e and e in self.SAME and not self.SAME[e]:
                continue
            if self.waited[e].get(sn, 0) >= val:
                continue
            if sn not in best or best[sn][1] < val:
                best[sn] = (sh, val)
        for sn, (sh, val) in best.items():
            self.eng[e].wait_ge(sh, val)
            self.waited[e][sn] = val
            self.n_instr += 1

    def _update(self, tok, reads, writes):
        for r in writes:
            key = self._rk(r)
            if key[1] == "*":
                for kk in [kk for kk in self.state if kk[0] == key[0]]:
                    del self.state[kk]
            self.state[key] = [tok, []]
        for r in reads:
            key = self._rk(r)
            if key not in self.state:
                self.state[key] = [None, []]
            rds = self.state[key][1]
            rds[:] = [t for t in rds if t[0] != tok[0]]
            rds.append(tok)

    def op(self, e, fn, r=(), w=()):
        toks = self._collect(r, w)
        self._emit_waits(e, toks)
        ins = fn()
        self.ccnt[e] += 1
        ins.then_inc(self.csem[e], 1)
        tok = ("c_" + e, self.csem[e], self.ccnt[e], e)
        self._update(tok, r, w)
        self.n_instr += 1
        return tok

    def dma(self, q, out, in_, r=(), w=(), is_output=False):
        toks = self._collect(r, w)
        i = self.drr[q]
        self.drr[q] = (i + 1) % self.ND
        sn = "d_%s%d" % (q, i)
        sh = self.dsem[q][i]
        if self.dcnt[q][i] > 0:
            toks.append((sn, sh, self.dcnt[q][i] * 16, "dma"))
        self._emit_waits(q, toks)
        self.eng[q].dma_start(out=out, in_=in_).then_inc(sh, 16)
        self.dcnt[q][i] += 1
        tok = (sn, sh, self.dcnt[q][i] * 16, "dma")
        self._update(tok, r, w)
        if is_output:
            self.out_tokens.append(tok)
        self.n_instr += 1
        return tok

    def finish(self):
        self._emit_waits("sp", self.out_tokens)
        toks = [("c_" + e, self.csem[e], self.ccnt[e], e) for e in self.csem if self.ccnt[e] > 0]
        self._emit_waits("sp", toks)


class PsumPool:
    def __init__(self, s, n, width=512):
        self.tiles = [s.ps("ps%d" % i, [128, width]) for i in range(n)]
        self.i = 0

    def get(self):
        t = self.tiles[self.i]
        self.i = (self.i + 1) % len(self.tiles)
        return t


class Rot:
    def __init__(self, tiles):
        self.tiles = tiles
        self.i = 0

    def get(self):
        t = self.tiles[self.i]
        self.i = (self.i + 1) % len(self.tiles)
        return t


def mk_consts(s, nc):
    c = {}
    c["ones_f"] = s.sb("ones_f", [128, 128], F32)
    s.op("pool", lambda: nc.gpsimd.memset(c["ones_f"][:], 1.0), w=[c["ones_f"]])
    c["ones_b"] = s.sb("ones_b", [128, 128], BF16)
    s.op("pool", lambda: nc.gpsimd.memset(c["ones_b"][:], 1.0), w=[c["ones_b"]])
    return c


def rstd_from_psum(s, nc, ps, P, N, scale, bias, out_t, tmp_t):
    s.op("act", lambda: nc.scalar.activation(out=tmp_t[0:P, 0:N], in_=ps[0:P, 0:N], func=AF.Sqrt,
                                             scale=scale, bias=bias), r=[ps], w=[tmp_t])
    s.op("dve", lambda: nc.vector.reciprocal(out=out_t[0:P, 0:N], in_=tmp_t[0:P, 0:N]), r=[tmp_t], w=[out_t])


def rope_tables(s, nc, pos_bc_i, TPC, inv_col, sgn_col, P, cos_t, sins_t, tmp):
    posf, ang, t, n, r = tmp
    INV2PI = 1.0 / (2.0 * math.pi)
    MAGIC = 12582912.0
    C1 = 6.28125
    C2 = 2.0 * math.pi - 6.28125
    PI_LO = 3.1415925
    s.op("dve", lambda: nc.vector.tensor_copy(out=posf[0:P, :], in_=pos_bc_i[0:P, :]), r=[pos_bc_i], w=[posf])
    s.op("dve", lambda: nc.vector.tensor_scalar(out=ang[0:P, :], in0=posf[0:P, :], scalar1=inv_col[0:P, 0:1],
                                                scalar2=None, op0=ALU.mult), r=[posf, inv_col], w=[ang])
    for which in ("sin", "cos"):
        off = 0.0 if which == "sin" else 0.25
        s.op("dve", lambda: nc.vector.tensor_scalar(out=t[0:P, :], in0=ang[0:P, :], scalar1=INV2PI, scalar2=off,
                                                    op0=ALU.mult, op1=ALU.add), r=[ang], w=[t])
        s.op("dve", lambda: nc.vector.tensor_scalar(out=n[0:P, :], in0=t[0:P, :], scalar1=MAGIC, scalar2=None,
                                                    op0=ALU.add), r=[t], w=[n])
        s.op("dve", lambda: nc.vector.tensor_scalar(out=n[0:P, :], in0=n[0:P, :], scalar1=MAGIC, scalar2=None,
                                                    op0=ALU.subtract), r=[n], w=[n])
        s.op("dve", lambda: nc.vector.scalar_tensor_tensor(out=r[0:P, :], in0=n[0:P, :], scalar=-C1, in1=ang[0:P, :],
                                                           op0=ALU.mult, op1=ALU.add), r=[n, ang], w=[r])
        s.op("dve", lambda: nc.vector.scalar_tensor_tensor(out=t[0:P, :], in0=n[0:P, :], scalar=-C2, in1=r[0:P, :],
                                                           op0=ALU.mult, op1=ALU.add), r=[n, r], w=[t])
        if which == "cos":
            s.op("dve", lambda: nc.vector.tensor_scalar(out=t[0:P, :], in0=t[0:P, :], scalar1=math.pi / 2, scalar2=None,
                                                        op0=ALU.add), r=[t], w=[t])
        s.op("dve", lambda: nc.vector.tensor_scalar(out=r[0:P, :], in0=t[0:P, :], scalar1=-PI_LO, scalar2=PI_LO,
                                                    op0=ALU.max, op1=ALU.min), r=[t], w=[r])
        if which == "sin":
            s.op("act", lambda: nc.scalar.activation(out=t[0:P, :], in_=r[0:P, :], func=AF.Sin), r=[r], w=[t])
            s.op("dve", lambda: nc.vector.tensor_scalar(out=sins_t[0:P, :], in0=t[0:P, :], scalar1=sgn_col[0:P, 0:1],
                                                        scalar2=None, op0=ALU.mult), r=[t, sgn_col], w=[sins_t])
        else:
            s.op("act", lambda: nc.scalar.activation(out=cos_t[0:P, :], in_=r[0:P, :], func=AF.Sin), r=[r], w=[cos_t])


def compute_hT(s, nc, cn, pp, x_src, nw, hT, NT, sq_rot, rstd_t, tmp_t, x_resident):
    ps = pp.get()
    for c in range(KC):
        xa, xd = x_src(c)
        sq = sq_rot.get()
        s.op("act", lambda: nc.scalar.activation(out=sq[:, 0:NT], in_=xa, func=AF.Square), r=xd, w=[sq])
        s.op("pe", lambda: nc.tensor.matmul(ps[:, 0:NT], cn["ones_f"][:, :], sq[:, 0:NT], start=(c == 0), stop=(c == KC - 1)),
             r=[sq, cn["ones_f"]], w=[ps])
    rstd_from_psum(s, nc, ps, 128, NT, 1.0 / D_MODEL, RMS_EPS, rstd_t, tmp_t)
    for c in range(KC):
        xa, xd = x_src(c)
        s.op("dve", lambda: nc.vector.scalar_tensor_tensor(out=hT[:, c, 0:NT], in0=xa, scalar=nw[:, c:c + 1],
                                                           in1=rstd_t[:, 0:NT], op0=ALU.mult, op1=ALU.mult),
             r=xd + [nw, rstd_t], w=[(hT, c)])


def planA():
    tiles = []
    for h in range(4):
        for kind in ("rq", "rqp", "rk", "rkp", "rv"):
            tiles.append((kind, h, 128))
    for i in range(8):
        tiles.append(("sx", i, 128))
    tiles.append(("sdt", 0, 8))
    for i in range(4):
        tiles.append(("cq", i, 128))
    for i in range(2):
        tiles.append(("ckv", i, 128))
    tiles.append(("kr", 0, 64))
    tiles.append(("krp", 0, 64))
    for i in range(14):
        tiles.append(("rw", i, 128))
    groups = []
    cur = []
    curw = 0
    off = 0
    for t in tiles:
        if curw + t[2] > 512:
            groups.append((off - curw, curw, cur))
            cur = []
            curw = 0
        cur.append((t[0], t[1], t[2], curw))
        curw += t[2]
        off += t[2]
    groups.append((off - curw, curw, cur))
    return tiles, groups, off


def colsA():
    idx = []
    p64 = (np.arange(128) + 64) % 128
    p32 = (np.arange(64) + 32) % 64
    for h in range(4):
        q0 = 0 + h * 128
        k0 = 512 + h * 128
        v0 = 1024 + h * 128
        idx += list(q0 + np.arange(128)) + list(q0 + p64) + list(k0 + np.arange(128)) + list(k0 + p64) + list(v0 + np.arange(128))
    idx += list(2560 + np.arange(1024))
    idx += list(3584 + np.arange(8))
    idx += list(3592 + np.arange(512))
    idx += list(4104 + np.arange(256))
    idx += list(4360 + np.arange(64)) + list(4360 + p32)
    idx += list(4424 + np.arange(1792))
    return np.array(idx, dtype=np.int64)


def build_A(cfg):
    TPC, NT, NP = cfg.TPC, cfg.NT, cfg.NP
    tiles, groups, NCA = planA()
    nc = bass.Bass("TRN2", target_bir_lowering=False)
    es = contextlib.ExitStack()
    with es:
        s = Sched(nc, es)
        IN, OUT = "ExternalInput", "ExternalOutput"
        xT = s.dram("xT", [128, KC, TPC], F32, IN)
        pos = s.dram("pos", [1, TPC], I32, IN)
        n1w = s.dram("n1w", [128, KC], F32, IN)
        wA = s.dram("wA", [128, KC, NCA], F32, IN)
        ropec = s.dram("ropec", [128, 4], F32, IN)
        wqb = s.dram("wqb", [128, 4, 1024], F32, IN)
        wkvb = s.dram("wkvb", [128, 2, 1024], F32, IN)
        mlaw = s.dram("mlaw", [128, 16], F32, IN)
        o_rq = s.dram("o_rq", [4, 128, TPC], F32, OUT)
        o_rk = s.dram("o_rk", [4, 128, TPC], F32, OUT)
        o_rv = s.dram("o_rv", [4, 128, TPC], F32, OUT)
        o_sx = s.dram("o_sx", [8, 128, TPC], F32, OUT)
        o_sdt = s.dram("o_sdt", [8, TPC], F32, OUT)
        o_rw = s.dram("o_rw", [14, 128, TPC], F32, OUT)
        o_mqn = s.dram("o_mqn", [4, 128, TPC], BF16, OUT)
        o_mqr = s.dram("o_mqr", [4, 64, TPC], BF16, OUT)
        o_mkn = s.dram("o_mkn", [4, 128, TPC], BF16, OUT)
        o_mkr = s.dram("o_mkr", [4, 64, TPC], BF16, OUT)
        o_mv = s.dram("o_mv", [TPC, 512], BF16, OUT)

        cn = mk_consts(s, nc)
        pp = PsumPool(s, 8)
        n1w_t = s.sb("n1w_t", [128, KC], F32)
        s.dma("sp", n1w_t[:], n1w[:], r=[n1w], w=[n1w_t])
        ropec_t = s.sb("ropec_t", [128, 4], F32)
        s.dma("sp", ropec_t[:], ropec[:], r=[ropec], w=[ropec_t])
        mlaw_t = s.sb("mlaw_t", [128, 16], F32)
        s.dma("sp", mlaw_t[:], mlaw[:], r=[mlaw], w=[mlaw_t])
        wqb_t = s.sb("wqb_t", [128, 4, 1024], BF16)
        for c in range(4):
            s.dma("pool", wqb_t[:, c, :], wqb[:, c, :], r=[wqb], w=[wqb_t])
        wkvb_t = s.sb("wkvb_t", [128, 2, 1024], BF16)
        for c in range(2):
            s.dma("pool", wkvb_t[:, c, :], wkvb[:, c, :], r=[wkvb], w=[wkvb_t])

        pos_i = s.sb("pos_i", [128, TPC], I32)
        s.dma("sp", pos_i[:], pos[0:1, :].partition_broadcast(128), r=[pos], w=[pos_i])
        tmp5 = [s.sb("rt%d" % i, [128, TPC], F32) for i in range(5)]
        cos_r = s.sb("cos_r", [128, TPC], F32)
        sin_r = s.sb("sin_r", [128, TPC], F32)
        cos_m = s.sb("cos_m", [64, TPC], F32)
        sin_m = s.sb("sin_m", [64, TPC], F32)
        inv_r = Tn("ropec_t", ropec_t.t[:, 0:1])
        sgn_r = Tn("ropec_t", ropec_t.t[:, 1:2])
        inv_m = Tn("ropec_t", ropec_t.t[:, 2:3])
        sgn_m = Tn("ropec_t", ropec_t.t[:, 3:4])
        rope_tables(s, nc, pos_i, TPC, inv_r, sgn_r, 128, cos_r, sin_r, tmp5)
        rope_tables(s, nc, pos_i, TPC, inv_m, sgn_m, 64, cos_m, sin_m, tmp5)

        hT = s.sb("hT", [128, KC, NT], BF16)
        xrot = Rot([s.sb("xb%d" % i, [128, NT], F32) for i in range(4)])
        sqrot = Rot([s.sb("sq%d" % i, [128, NT], F32) for i in range(2)])
        rstd_t = s.sb("rstd_t", [128, NT], F32)
        rtmp = s.sb("rtmp", [128, NT], F32)
        wrot = Rot([s.sb("wb%d" % i, [128, KC, 512], BF16) for i in range(2)])
        stg = Rot([s.sb("stg%d" % i, [128, NT], F32) for i in range(4)])
        t1rot = Rot([s.sb("t1_%d" % i, [128, NT], F32) for i in range(2)])
        cqT = s.sb("cqT", [128, 4, NT], F32)
        ckvT = s.sb("ckvT", [128, 2, NT], F32)
        krT = s.sb("krT", [64, 2, NT], F32)
        cqn = s.sb("cqn", [128, 4, NT], BF16)
        ckvn = s.sb("ckvn", [128, 2, NT], BF16)
        bstg = Rot([s.sb("bstg%d" % i, [128, NT], BF16) for i in range(3)])
        vstg = Rot([s.sb("vstg%d" % i, [128, 512], BF16) for i in range(2)])
        rs2 = s.sb("rs2", [128, NT], F32)

        for p in range(NP):
            t0 = p * NT
            tsl = slice(t0, t0 + NT)

            def x_src(c):
                xb = xrot.get()
                s.dma("sp", xb[:, 0:NT], xT[:, c, tsl], r=[xT], w=[xb])
                return xb[:, 0:NT], [xb]
            compute_hT(s, nc, cn, pp, x_src, n1w_t, hT, NT, sqrot, rstd_t, rtmp, False)

            pend = None
            for (c0, gw, gt) in groups:
                wb = wrot.get()
                s.dma("pool", wb[:, :, 0:gw], wA[:, :, c0:c0 + gw], r=[wA], w=[wb])
                for (kind, idx, ncol, lo) in gt:
                    ps = pp.get()
                    for c in range(KC):
                        s.op("pe", lambda: nc.tensor.matmul(ps[0:ncol, 0:NT], wb[:, c, lo:lo + ncol], hT[:, c, 0:NT],
                                                            start=(c == 0), stop=(c == KC - 1)),
                             r=[wb, (hT, c)], w=[ps])
                    if kind in ("rq", "rk"):
                        sc = 1.0 if kind == "rq" else 128.0 ** -0.5
                        t1 = t1rot.get()
                        s.op("dve", lambda: nc.vector.scalar_tensor_tensor(out=t1[:, 0:NT], in0=ps[:, 0:NT], scalar=sc,
                                                                           in1=cos_r[:, tsl], op0=ALU.mult, op1=ALU.mult),
                             r=[ps, cos_r], w=[t1])
                        pend = t1
                    elif kind in ("rqp", "rkp"):
                        sc = 1.0 if kind == "rqp" else 128.0 ** -0.5
                        t2 = t1rot.get()
                        s.op("dve", lambda: nc.vector.scalar_tensor_tensor(out=t2[:, 0:NT], in0=ps[:, 0:NT], scalar=sc,
                                                                           in1=sin_r[:, tsl], op0=ALU.mult, op1=ALU.mult),
                             r=[ps, sin_r], w=[t2])
                        st = stg.get()
                        s.op("pool", lambda: nc.gpsimd.tensor_tensor(out=st[:, 0:NT], in0=pend[:, 0:NT], in1=t2[:, 0:NT],
                                                                     op=ALU.add), r=[pend, t2], w=[st])
                        dst = o_rq if kind == "rqp" else o_rk
                        s.dma("sp", dst[idx, :, tsl], st[:, 0:NT], r=[st], w=[(dst, (idx, p))], is_output=True)
                    elif kind in ("rv", "sx", "rw", "sdt"):
                        st = stg.get()
                        s.op("act", lambda: nc.scalar.copy(out=st[0:ncol, 0:NT], in_=ps[0:ncol, 0:NT]), r=[ps], w=[st])
                        if kind == "sdt":
                            s.dma("sp", o_sdt[:, tsl], st[0:8, 0:NT], r=[st], w=[(o_sdt, p)], is_output=True)
                        else:
                            dst = {"rv": o_rv, "sx": o_sx, "rw": o_rw}[kind]
                            s.dma("sp", dst[idx, :, tsl], st[:, 0:NT], r=[st], w=[(dst, (idx, p))], is_output=True)
                    elif kind == "cq":
                        s.op("act", lambda: nc.scalar.copy(out=cqT[:, idx, 0:NT], in_=ps[:, 0:NT]), r=[ps], w=[(cqT, idx)])
                    elif kind == "ckv":
                        s.op("act", lambda: nc.scalar.copy(out=ckvT[:, idx, 0:NT], in_=ps[:, 0:NT]), r=[ps], w=[(ckvT, idx)])
                    elif kind in ("kr", "krp"):
                        j = 0 if kind == "kr" else 1
                        s.op("act", lambda: nc.scalar.copy(out=krT[:, j, 0:NT], in_=ps[0:64, 0:NT]), r=[ps], w=[(krT, j)])

            def rms_feat(src, nch, P, wcol0, dst, nfeat):
                psn = pp.get()
                for c in range(nch):
                    sq = sqrot.get()
                    s.op("act", lambda: nc.scalar.activation(out=sq[0:P, 0:NT], in_=src[0:P, c, 0:NT], func=AF.Square),
                         r=[(src, c)], w=[sq])
                    s.op("pe", lambda: nc.tensor.matmul(psn[:, 0:NT], cn["ones_f"][0:P, :], sq[0:P, 0:NT],
                                                        start=(c == 0), stop=(c == nch - 1)), r=[sq, cn["ones_f"]], w=[psn])
                rstd_from_psum(s, nc, psn, 128, NT, 1.0 / nfeat, RMS_EPS, rs2, rtmp)
                for c in range(nch):
                    s.op("dve", lambda: nc.vector.scalar_tensor_tensor(out=dst[:, c, 0:NT], in0=src[:, c, 0:NT],
                                                                       scalar=mlaw_t[:, wcol0 + c:wcol0 + c + 1],
                                                                       in1=rs2[:, 0:NT], op0=ALU.mult, op1=ALU.mult),
                         r=[(src, c), mlaw_t, rs2], w=[(dst, c)])
            rms_feat(cqT, 4, 128, 0, cqn, 512)
            rms_feat(ckvT, 2, 128, 4, ckvn, 256)

            def head_qk(is_q, h):
                if is_q:
                    wt, nkc, src = wqb_t, 4, cqn
                    cn0, cr0, cp0 = h * 256, h * 256 + 128, h * 256 + 192
                    wn, wr, wp = 6, 7, 8
                else:
                    wt, nkc, src = wkvb_t, 2, ckvn
                    cn0 = h * 128
                    wn, wr, wp = 9, 10, 11
                psn = pp.get()
                for c in range(nkc):
                    s.op("pe", lambda: nc.tensor.matmul(psn[:, 0:NT], wt[:, c, cn0:cn0 + 128], src[:, c, 0:NT],
                                                        start=(c == 0), stop=(c == nkc - 1)), r=[wt, (src, c)], w=[psn])
                nope = stg.get()
                s.op("act", lambda: nc.scalar.copy(out=nope[:, 0:NT], in_=psn[:, 0:NT]), r=[psn], w=[nope])
                if is_q:
                    psr = pp.get()
                    psp = pp.get()
                    for (pst, c00) in ((psr, cr0), (psp, cp0)):
                        for c in range(nkc):
                            s.op("pe", lambda: nc.tensor.matmul(pst[0:64, 0:NT], wt[:, c, c00:c00 + 64], src[:, c, 0:NT],
                                                                start=(c == 0), stop=(c == nkc - 1)), r=[wt, (src, c)], w=[pst])
                    rr = stg.get()
                    s.op("act", lambda: nc.scalar.copy(out=rr[0:64, 0:NT], in_=psr[0:64, 0:NT]), r=[psr], w=[rr])
                    rp = stg.get()
                    s.op("act", lambda: nc.scalar.copy(out=rp[0:64, 0:NT], in_=psp[0:64, 0:NT]), r=[psp], w=[rp])
                    rr_ap, rp_ap, rdeps = rr[0:64, 0:NT], rp[0:64, 0:NT], [rr, rp]
                else:
                    rr_ap, rp_ap, rdeps = krT[:, 0, 0:NT], krT[:, 1, 0:NT], [(krT, 0), (krT, 1)]
                pss = pp.get()
                sq = sqrot.get()
                s.op("act", lambda: nc.scalar.activation(out=sq[:, 0:NT], in_=nope[:, 0:NT], func=AF.Square), r=[nope], w=[sq])
                s.op("pe", lambda: nc.tensor.matmul(pss[:, 0:NT], cn["ones_f"][:, :], sq[:, 0:NT], start=True, stop=False),
                     r=[sq, cn["ones_f"]], w=[pss])
                sq2 = sqrot.get()
                s.op("act", lambda: nc.scalar.activation(out=sq2[0:64, 0:NT], in_=rr_ap, func=AF.Square), r=rdeps, w=[sq2])
                s.op("pe", lambda: nc.tensor.matmul(pss[:, 0:NT], cn["ones_f"][0:64, :], sq2[0:64, 0:NT], start=False, stop=True),
                     r=[sq2, cn["ones_f"]], w=[pss])
                if is_q:
                    rstd_from_psum(s, nc, pss, 128, NT, 1.0, 192.0 * RMS_EPS, rs2, rtmp)
                else:
                    rstd_from_psum(s, nc, pss, 128, NT, 1.0 / 192.0, RMS_EPS, rs2, rtmp)
                ob = bstg.get()
                s.op("dve", lambda: nc.vector.scalar_tensor_tensor(out=ob[:, 0:NT], in0=nope[:, 0:NT], scalar=mlaw_t[:, wn:wn + 1],
                                                                   in1=rs2[:, 0:NT], op0=ALU.mult, op1=ALU.mult),
                     r=[nope, mlaw_t, rs2], w=[ob])
                dstn = o_mqn if is_q else o_mkn
                s.dma("sp", dstn[h, :, tsl], ob[:, 0:NT], r=[ob], w=[(dstn, (h, p))], is_output=True)
                t1 = t1rot.get()
                s.op("dve", lambda: nc.vector.scalar_tensor_tensor(out=t1[0:64, 0:NT], in0=rr_ap, scalar=mlaw_t[0:64, wr:wr + 1],
                                                                   in1=cos_m[:, tsl], op0=ALU.mult, op1=ALU.mult),
                     r=rdeps + [mlaw_t, cos_m], w=[t1])
                t2 = t1rot.get()
                s.op("dve", lambda: nc.vector.scalar_tensor_tensor(out=t2[0:64, 0:NT], in0=rp_ap, scalar=mlaw_t[0:64, wp:wp + 1],
                                                                   in1=sin_m[:, tsl], op0=ALU.mult, op1=ALU.mult),
                     r=rdeps + [mlaw_t, sin_m], w=[t2])
                s.op("pool", lambda: nc.gpsimd.tensor_tensor(out=t1[0:64, 0:NT], in0=t1[0:64, 0:NT], in1=t2[0:64, 0:NT], op=ALU.add),
                     r=[t1, t2], w=[t1])
                ob2 = bstg.get()
                s.op("dve", lambda: nc.vector.tensor_tensor(out=ob2[0:64, 0:NT], in0=t1[0:64, 0:NT], in1=rs2[0:64, 0:NT], op=ALU.mult),
                     r=[t1, rs2], w=[ob2])
                dstr = o_mqr if is_q else o_mkr
                s.dma("sp", dstr[h, :, tsl], ob2[0:64, 0:NT], r=[ob2], w=[(dstr, (h, p))], is_output=True)

            for h in range(4):
                head_qk(True, h)
                head_qk(False, h)
            for tt in range(NT // 128):
                psv = pp.get()
                for c in range(2):
                    s.op("pe", lambda: nc.tensor.matmul(psv[:, 0:512], ckvn[:, c, tt * 128:(tt + 1) * 128], wkvb_t[:, c, 512:1024],
                                                        start=(c == 0), stop=(c == 1)), r=[(ckvn, c), wkvb_t], w=[psv])
                vs = vstg.get()
                s.op("act", lambda: nc.scalar.copy(out=vs[:, :], in_=psv[:, 0:512]), r=[psv], w=[vs])
                s.dma("sp", o_mv[t0 + tt * 128:t0 + (tt + 1) * 128, :], vs[:, :], r=[vs], w=[(o_mv, (p, tt))], is_output=True)
        s.finish()
        print("build_A instrs", s.n_instr)
    return nc


def fm(a):
    K, N = a.shape
    return np.ascontiguousarray(a.reshape(K // 128, 128, N).transpose(1, 0, 2))


def colvec(v, n=None):
    v = np.asarray(v, dtype=np.float32)
    return np.ascontiguousarray(v.reshape(-1, 128).T)


def pad_rows(a, rows=128):
    out = np.zeros((rows,) + a.shape[1:], dtype=a.dtype)
    out[: a.shape[0]] = a
    return out


def rope_consts():
    inv64 = (1.0 / (np.float32(10000.0) ** (np.arange(0, 128, 2, dtype=np.float32) / np.float32(128)))).astype(np.float32)
    inv32 = (1.0 / (np.float32(10000.0) ** (np.arange(0, 64, 2, dtype=np.float32) / np.float32(64)))).astype(np.float32)
    rc = np.zeros((128, 4), np.float32)
    d = np.arange(128)
    rc[:, 0] = inv64[d % 64]
    rc[:, 1] = np.where(d < 64, -1.0, 1.0)
    rc[:64, 2] = inv32[np.arange(64) % 32]
    rc[:64, 3] = np.where(np.arange(64) < 32, -1.0, 1.0)
    return rc


def host_A_weights(inp, l):
    p32 = (np.arange(64) + 32) % 64
    w = {}
    w["wA"] = fm(inp["w_in"][l][:, colsA()])
    w["n1w"] = colvec(inp["norm1_w"][l])
    w["ropec"] = rope_consts()
    qb = inp["mla_w_qb"][l]
    cols = []
    for h in range(4):
        b = h * 192
        cols += list(b + np.arange(128)) + list(b + 128 + np.arange(64)) + list(b + 128 + p32)
    w["wqb"] = fm(qb[:, np.array(cols)])
    kvb = inp["mla_w_kvb"][l]
    cols = []
    for h in range(4):
        cols += list(h * 256 + np.arange(128))
    for h in range(4):
        cols += list(h * 256 + 128 + np.arange(128))
    w["wkvb"] = fm(kvb[:, np.array(cols)])
    m = np.zeros((128, 16), np.float32)
    m[:, 0:4] = colvec(inp["mla_q_a_norm_w"][l])
    m[:, 4:6] = colvec(inp["mla_kv_a_norm_w"][l])
    qn = inp["mla_q_norm_w"][l]
    kn = inp["mla_k_norm_w"][l]
    m[:, 6] = qn[0:128]
    m[:64, 7] = qn[128:192]
    m[:64, 8] = qn[128 + p32]
    m[:, 9] = kn[0:128]
    m[:64, 10] = kn[128:192]
    m[:64, 11] = kn[128 + p32]
    w["mlaw"] = m
    return w


def to_fm_tokens(x2d):
    T, Dd = x2d.shape
    return np.ascontiguousarray(x2d.T.reshape(Dd // 128, 128, T).transpose(1, 0, 2))


def run_spmd(nc, in_maps):
    res = run_bass_kernel_spmd(nc, in_maps, core_ids=list(range(NCORES)))
    return res.results


def ret_consts(h):
    lg = np.log1p(-np.exp2(np.float32(-5.0 - h))).astype(np.float64)
    m = np.arange(128)[:, None]
    l = np.arange(128)[None, :]
    cm, cl = m // 64, l // 64
    mask = np.where(cm == cl, np.exp(lg * np.abs(l - m)), np.where(cm < cl, np.exp(lg * (l - m)), 0.0))
    c = {}
    c["ret_mask"] = mask.astype(np.float32)
    col = np.zeros((128, 2), np.float32)
    col[:, 0] = np.exp(lg * (127 - np.arange(128)))
    col[:, 1] = np.exp(lg * 128)
    c["ret_col"] = col
    c["ret_qdec"] = np.tile(np.exp(lg * (np.arange(128) + 1.0))[None, :], (128, 1)).astype(np.float32)
    return c


def mixer_ret(s, nc, cn, pp, S, TB, D):
    NB = S // TB
    mask = s.sb("r_mask", [128, 128], F32)
    s.dma("sp", mask[:], D["ret_mask"][:], r=[D["ret_mask"]], w=[mask])
    qdec = s.sb("r_qdec", [128, 128], F32)
    s.dma("sp", qdec[:], D["ret_qdec"][:], r=[D["ret_qdec"]], w=[qdec])
    rcol = s.sb("r_col", [128, 2], F32)
    s.dma("sp", rcol[:], D["ret_col"][:], r=[D["ret_col"]], w=[rcol])
    qb = Rot([s.sb("r_q%d" % i, [128, TB], F32) for i in range(2)])
    kb = Rot([s.sb("r_k%d" % i, [128, TB], F32) for i in range(2)])
    vb = Rot([s.sb("r_v%d" % i, [128, TB], F32) for i in range(2)])
    ob = Rot([s.sb("r_o%d" % i, [128, TB], F32) for i in range(2)])
    St = [s.sb("r_S%d" % i, [128, 128], F32) for i in range(2)]
    s.op("pool", lambda: nc.gpsimd.memset(St[0][:], 0.0), w=[St[0]])
    ktm = Rot([s.sb("r_ktm%d" % i, [128, 128], F32) for i in range(2)])
    vtm = Rot([s.sb("r_vtm%d" % i, [128, 128], F32) for i in range(2)])
    pT = Rot([s.sb("r_pT%d" % i, [128, 128], F32) for i in range(2)])
    qd = Rot([s.sb("r_qd%d" % i, [128, 128], F32) for i in range(2)])
    ident = cn["ident"]
    sc_i = 0
    for b in range(NB):
        bs = slice(b * TB, (b + 1) * TB)
        q, k, v, o = qb.get(), kb.get(), vb.get(), ob.get()
        s.dma("sp", q[:], D["ret_q"][:, bs], r=[D["ret_q"]], w=[q])
        s.dma("sp", k[:], D["ret_k"][:, bs], r=[D["ret_k"]], w=[k])
        s.dma("sp", v[:], D["ret_v"][:, bs], r=[D["ret_v"]], w=[v])
        for j in range(TB // 128):
            cs_ = slice(j * 128, (j + 1) * 128)
            Sold, Snew = St[sc_i % 2], St[(sc_i + 1) % 2]
            sc_i += 1
            p1 = pp.get()
            s.op("pe", lambda: nc.tensor.transpose(p1[:, 0:128], k[:, cs_], ident[:, :]), r=[k, ident], w=[p1])
            kt = ktm.get()
            s.op("act", lambda: nc.scalar.activation(out=kt[:, :], in_=p1[:, 0:128], func=AF.Copy, scale=rcol[:, 0:1]),
                 r=[p1, rcol], w=[kt])
            p2 = pp.get()
            s.op("pe", lambda: nc.tensor.transpose(p2[:, 0:128], v[:, cs_], ident[:, :]), r=[v, ident], w=[p2])
            vt = vtm.get()
            s.op("dve", lambda: nc.vector.tensor_copy(out=vt[:, :], in_=p2[:, 0:128]), r=[p2], w=[vt])
            p3 = pp.get()
            s.op("pe", lambda: nc.tensor.matmul(p3[:, 0:128], k[:, cs_], q[:, cs_], start=True, stop=True), r=[k, q], w=[p3])
            pt = pT.get()
            s.op("dve", lambda: nc.vector.tensor_tensor(out=pt[:, :], in0=p3[:, 0:128], in1=mask[:, :], op=ALU.mult),
                 r=[p3, mask], w=[pt])
            qdt = qd.get()
            s.op("pool", lambda: nc.gpsimd.tensor_tensor(out=qdt[:, :], in0=q[:, cs_], in1=qdec[:, :], op=ALU.mult),
                 r=[q, qdec], w=[qdt])
            p4 = pp.get()
            s.op("pe", lambda: nc.tensor.matmul(p4[:, 0:128], vt[:, :], pt[:, :], start=True, stop=False), r=[vt, pt], w=[p4])
            s.op("pe", lambda: nc.tensor.matmul(p4[:, 0:128], Sold[:, :], qdt[:, :], start=False, stop=True), r=[Sold, qdt], w=[p4])
            s.op("act", lambda: nc.scalar.copy(out=o[:, cs_], in_=p4[:, 0:128]), r=[p4], w=[o])
            p5 = pp.get()
            s.op("pe", lambda: nc.tensor.matmul(p5[:, 0:128], kt[:, :], vt[:, :], start=True, stop=True), r=[kt, vt], w=[p5])
            s.op("dve", lambda: nc.vector.scalar_tensor_tensor(out=Snew[:, :], in0=Sold[:, :], scalar=rcol[:, 1:2], in1=p5[:, 0:128],
                                                               op0=ALU.mult, op1=ALU.add), r=[Sold, rcol, p5], w=[Snew])
        s.dma("sp", D["o_ret"][:, bs], o[:], r=[o], w=[(D["o_ret"], b)], is_output=True)


def ssd_consts():
    m = np.arange(128)[:, None]
    l = np.arange(128)[None, :]
    return {"ssd_negmask": np.where(l >= m, 0.0, -30000.0).astype(np.float32)}


def mixer_ssd(s, nc, cn, pp, S, TB, D):
    NB = S // TB
    ident = cn["ident"]
    negmask = s.sb("s_negmask", [128, 128], F32)
    s.dma("sp", negmask[:], D["ssd_negmask"][:], r=[D["ssd_negmask"]], w=[negmask])
    cwx = s.sb("s_cwx", [64, 5], F32)
    cwB = s.sb("s_cwB", [128, 5], F32)
    cwC = s.sb("s_cwC", [128, 5], F32)
    scal = s.sb("s_scal", [128, 4], F32)
    for t_, n_ in ((cwx, "ssd_cwx"), (cwB, "ssd_cwB"), (cwC, "ssd_cwC"), (scal, "ssd_scal")):
        s.dma("sp", t_[:], D[n_][:], r=[D[n_]], w=[t_])
    Acol = s.sb("s_A", [128, 1], F32)
    s.op("act", lambda: nc.scalar.activation(out=Acol[:, :], in_=scal[:, 1:2], func=AF.Exp), r=[scal], w=[Acol])
    s.op("dve", lambda: nc.vector.tensor_scalar(out=Acol[:, :], in0=Acol[:, :], scalar1=-1.0, scalar2=None, op0=ALU.mult),
         r=[Acol], w=[Acol])
    onesrow = s.sb("s_onesrow", [1, 128], F32)
    s.op("pool", lambda: nc.gpsimd.memset(onesrow[:], 1.0), w=[onesrow])
    negrow = s.sb("s_negrow", [1, 128], F32)
    s.op("pool", lambda: nc.gpsimd.memset(negrow[:], -1.0), w=[negrow])
    HW = TB + 3
    xin = Rot([s.sb("s_xin%d" % i, [64, HW], F32) for i in range(2)])
    Bin = Rot([s.sb("s_Bin%d" % i, [128, HW], F32) for i in range(2)])
    Cin = Rot([s.sb("s_Cin%d" % i, [128, HW], F32) for i in range(2)])
    dtin = Rot([s.sb("s_dtin%d" % i, [1, TB], F32) for i in range(2)])
    xc = s.sb("s_xc", [64, TB], F32)
    Bc = s.sb("s_Bc", [128, TB], F32)
    Cc = s.sb("s_Cc", [128, TB], F32)
    acc = Rot([s.sb("s_acc%d" % i, [128, TB], F32) for i in range(2)])
    r1 = s.sb("s_r1", [1, TB], F32)
    r2 = s.sb("s_r2", [1, TB], F32)
    dtr = s.sb("s_dtr", [1, TB], F32)
    adt = s.sb("s_adt", [1, TB], F32)
    csr = s.sb("s_csr", [1, TB], F32)
    ecs = s.sb("s_ecs", [1, TB], F32)
    ob = Rot([s.sb("s_o%d" % i, [64, TB], F32) for i in range(2)])
    Sst = [s.sb("s_S%d" % i, [128, 64], F32) for i in range(2)]
    s.op("pool", lambda: nc.gpsimd.memset(Sst[0][:], 0.0), w=[Sst[0]])
    cols = Rot([s.sb("s_cols%d" % i, [128, 6], F32) for i in range(2)])
    tmpm = Rot([s.sb("s_tm%d" % i, [128, 128], F32) for i in range(2)])
    decT = Rot([s.sb("s_dec%d" % i, [128, 128], F32) for i in range(2)])
    pT = Rot([s.sb("s_pT%d" % i, [128, 128], F32) for i in range(2)])
    xdt = Rot([s.sb("s_xdt%d" % i, [128, 64], F32) for i in range(2)])
    xck = Rot([s.sb("s_xck%d" % i, [128, 64], F32) for i in range(2)])
    Btm = Rot([s.sb("s_Btm%d" % i, [128, 128], F32) for i in range(2)])
    Cdec = Rot([s.sb("s_Cdec%d" % i, [128, 128], F32) for i in range(2)])
    sc_i = 0
    for b in range(NB):
        t0 = b * TB
        xi, Bi, Ci, dti = xin.get(), Bin.get(), Cin.get(), dtin.get()
        for (buf, name, P) in ((xi, "ssd_x", 64), (Bi, "ssd_B", 128), (Ci, "ssd_C", 128)):
            if b == 0:
                s.op("pool", lambda: nc.gpsimd.memset(buf[0:P, 0:3], 0.0), w=[buf])
                s.dma("sp", buf[0:P, 3:HW], D[name][:, 0:TB], r=[D[name]], w=[buf])
            else:
                s.dma("sp", buf[0:P, :], D[name][:, t0 - 3:t0 + TB], r=[D[name]], w=[buf])
        s.dma("sp", dti[:], D["ssd_dt"][:, t0:t0 + TB], r=[D["ssd_dt"]], w=[dti])
        for (buf, cw, outt, P) in ((xi, cwx, xc, 64), (Bi, cwB, Bc, 128), (Ci, cwC, Cc, 128)):
            a = acc.get()
            s.op("dve", lambda: nc.vector.tensor_scalar(out=a[0:P, :], in0=buf[0:P, 0:TB], scalar1=cw[0:P, 0:1], scalar2=cw[0:P, 4:5],
                                                        op0=ALU.mult, op1=ALU.add), r=[buf, cw], w=[a])
            for kk in range(1, 4):
                s.op("dve", lambda: nc.vector.scalar_tensor_tensor(out=a[0:P, :], in0=buf[0:P, kk:kk + TB], scalar=cw[0:P, kk:kk + 1],
                                                                   in1=a[0:P, :], op0=ALU.mult, op1=ALU.add), r=[buf, cw, a], w=[a])
            s.op("act", lambda: nc.scalar.activation(out=outt[0:P, :], in_=a[0:P, :], func=AF.Silu), r=[a], w=[outt])
        s.op("dve", lambda: nc.vector.tensor_scalar(out=r1[:, :], in0=dti[:, :], scalar1=scal[0:1, 0:1], scalar2=None, op0=ALU.add),
             r=[dti, scal], w=[r1])
        s.op("dve", lambda: nc.vector.scalar_tensor_tensor(out=r2[:, :], in0=r1[:, :], scalar=-1.0, in1=r1[:, :], op0=ALU.mult, op1=ALU.max),
             r=[r1], w=[r2])
        s.op("act", lambda: nc.scalar.activation(out=r2[:, :], in_=r2[:, :], func=AF.Exp, scale=-1.0), r=[r2], w=[r2])
        s.op("act", lambda: nc.scalar.activation(out=r2[:, :], in_=r2[:, :], func=AF.Ln, bias=1.0), r=[r2], w=[r2])
        s.op("dve", lambda: nc.vector.scalar_tensor_tensor(out=dtr[:, :], in0=r1[:, :], scalar=0.0, in1=r2[:, :], op0=ALU.max, op1=ALU.add),
             r=[r1, r2], w=[dtr])
        s.op("dve", lambda: nc.vector.tensor_scalar(out=adt[:, :], in0=dtr[:, :], scalar1=Acol[0:1, 0:1], scalar2=None, op0=ALU.mult),
             r=[dtr, Acol], w=[adt])
        for j in range(TB // 128):
            cs_ = slice(j * 128, (j + 1) * 128)
            s.op("dve", lambda: nc.vector.tensor_tensor_scan(out=csr[:, cs_], data0=onesrow[:, :], data1=adt[:, cs_], initial=0.0,
                                                             op0=ALU.mult, op1=ALU.add), r=[onesrow, adt], w=[csr])
        s.op("act", lambda: nc.scalar.activation(out=ecs[:, :], in_=csr[:, :], func=AF.Exp), r=[csr], w=[ecs])
        o = ob.get()
        for j in range(TB // 128):
            cs_ = slice(j * 128, (j + 1) * 128)
            Sold, Snew = Sst[sc_i % 2], Sst[(sc_i + 1) % 2]
            sc_i += 1
            pc = pp.get()
            s.op("pe", lambda: nc.tensor.matmul(pc[:, 0:1], dtr[:, cs_], onesrow[:, 0:1], start=True, stop=True), r=[dtr, onesrow], w=[pc])
            s.op("pe", lambda: nc.tensor.matmul(pc[:, 1:2], csr[:, cs_], onesrow[:, 0:1], start=True, stop=True), r=[csr, onesrow], w=[pc])
            e_ = j * 128 + 127
            s.op("pe", lambda: nc.tensor.matmul(pc[:, 2:3], onesrow[:, :], csr[:, e_:e_ + 1], start=True, stop=True), r=[csr, onesrow], w=[pc])
            cl = cols.get()
            s.op("dve", lambda: nc.vector.tensor_copy(out=cl[:, 0:3], in_=pc[:, 0:3]), r=[pc], w=[cl])
            s.op("act", lambda: nc.scalar.activation(out=cl[:, 3:4], in_=cl[:, 1:2], func=AF.Exp, scale=-1.0, bias=cl[:, 2:3]), r=[cl], w=[cl])
            s.op("act", lambda: nc.scalar.activation(out=cl[:, 4:5], in_=cl[:, 2:3], func=AF.Exp), r=[cl], w=[cl])
            pg = pp.get()
            s.op("pe", lambda: nc.tensor.matmul(pg[:, 0:128], onesrow[:, :], csr[:, cs_], start=True, stop=False), r=[csr, onesrow], w=[pg])
            s.op("pe", lambda: nc.tensor.matmul(pg[:, 0:128], csr[:, cs_], negrow[:, :], start=False, stop=True), r=[csr, negrow], w=[pg])
            tm = tmpm.get()
            s.op("dve", lambda: nc.vector.scalar_tensor_tensor(out=tm[:, :], in0=pg[:, 0:128], scalar=0.0, in1=negmask[:, :],
                                                               op0=ALU.min, op1=ALU.add), r=[pg, negmask], w=[tm])
            dc = decT.get()
            s.op("act", lambda: nc.scalar.activation(out=dc[:, :], in_=tm[:, :], func=AF.Exp), r=[tm], w=[dc])
            pb = pp.get()
            s.op("pe", lambda: nc.tensor.matmul(pb[:, 0:128], Bc[:, cs_], Cc[:, cs_], start=True, stop=True), r=[Bc, Cc], w=[pb])
            pt = pT.get()
            s.op("dve", lambda: nc.vector.tensor_tensor(out=pt[:, :], in0=pb[:, 0:128], in1=dc[:, :], op=ALU.mult), r=[pb, dc], w=[pt])
            px = pp.get()
            s.op("pe", lambda: nc.tensor.transpose(px[:, 0:64], xc[:, cs_], ident[0:64, 0:64]), r=[xc, ident], w=[px])
            xd = xdt.get()
            s.op("act", lambda: nc.scalar.activation(out=xd[:, :], in_=px[:, 0:64], func=AF.Copy, scale=cl[:, 0:1]), r=[px, cl], w=[xd])
            xk = xck.get()
            s.op("dve", lambda: nc.vector.tensor_scalar(out=xk[:, :], in0=xd[:, :], scalar1=cl[:, 3:4], scalar2=None, op0=ALU.mult),
                 r=[xd, cl], w=[xk])
            pe_ = pp.get()
            s.op("pe", lambda: nc.tensor.matmul(pe_[:, 0:128], onesrow[:, :], ecs[:, cs_], start=True, stop=True), r=[ecs, onesrow], w=[pe_])
            cd = Cdec.get()
            s.op("dve", lambda: nc.vector.tensor_tensor(out=cd[:, :], in0=pe_[:, 0:128], in1=Cc[:, cs_], op=ALU.mult), r=[pe_, Cc], w=[cd])
            py = pp.get()
            s.op("pe", lambda: nc.tensor.matmul(py[0:64, 0:128], xd[:, :], pt[:, :], start=True, stop=False), r=[xd, pt], w=[py])
            s.op("pe", lambda: nc.tensor.matmul(py[0:64, 0:128], Sold[:, :], cd[:, :], start=False, stop=True), r=[Sold, cd], w=[py])
            s.op("dve", lambda: nc.vector.scalar_tensor_tensor(out=o[:, cs_], in0=xc[:, cs_], scalar=scal[0:64, 2:3], in1=py[0:64, 0:128],
                                                               op0=ALU.mult, op1=ALU.add), r=[xc, scal, py], w=[o])
            pB = pp.get()
            s.op("pe", lambda: nc.tensor.transpose(pB[:, 0:128], Bc[:, cs_], ident[:, :]), r=[Bc, ident], w=[pB])
            bt = Btm.get()
            s.op("act", lambda: nc.scalar.copy(out=bt[:, :], in_=pB[:, 0:128]), r=[pB], w=[bt])
            pS = pp.get()
            s.op("pe", lambda: nc.tensor.matmul(pS[:, 0:64], bt[:, :], xk[:, :], start=True, stop=True), r=[bt, xk], w=[pS])
            s.op("dve", lambda: nc.vector.scalar_tensor_tensor(out=Snew[:, :], in0=Sold[:, :], scalar=cl[:, 4:5], in1=pS[:, 0:64],
                                                               op0=ALU.mult, op1=ALU.add), r=[Sold, cl, pS], w=[Snew])
        s.dma("sp", D["o_ssd"][:, t0:t0 + TB], o[:], r=[o], w=[(D["o_ssd"], b)], is_output=True)


def mla_masks(par):
    k = np.arange(128)[:, None]
    q = np.arange(128)[None, :]
    diag = np.where((k >= 64) & (q < 64), 0.0, 1.0).astype(np.float32)
    if par == 0:
        mA, mB = diag, np.zeros((128, 128), np.float32)
    else:
        mA, mB = np.ones((128, 128), np.float32), diag
    return np.concatenate([mA, mB], axis=1)


def mixer_mla(s, nc, cn, S, D, psA, psO):
    NKB = S // 128
    NQB = NKB // 2
    Sq = S // 2
    kn = s.sb("m_kn", [128, S], BF16)
    kr = s.sb("m_kr", [64, S], BF16)
    qn = s.sb("m_qn", [128, Sq], BF16)
    qr = s.sb("m_qr", [64, Sq], BF16)
    v = s.sb("m_v", [128, NKB, 130], BF16)
    msk = s.sb("m_msk", [128, 256], BF16)
    mskf = s.sb("m_mskf", [128, 256], F32)
    s.dma("sp", kn[:], D["mla_kn"][:], r=[D["mla_kn"]], w=[kn])
    s.dma("sp", kr[:], D["mla_kr"][:], r=[D["mla_kr"]], w=[kr])
    s.dma("sp", qn[:], D["mla_qn"][:], r=[D["mla_qn"]], w=[qn])
    s.dma("sp", qr[:], D["mla_qr"][:], r=[D["mla_qr"]], w=[qr])
    s.op("pool", lambda: nc.gpsimd.memset(v[:, :, 128:130], 1.0), w=[(v, "ones")])
    s.dma("sp", v[:, :, 0:128], D["mla_v"][:].rearrange("(b p) e -> p b e", p=128), r=[D["mla_v"]], w=[(v, "data")])
    s.dma("sp", mskf[:], D["mla_mask"][:], r=[D["mla_mask"]], w=[mskf])
    s.op("dve", lambda: nc.vector.tensor_copy(out=msk[:, :], in_=mskf[:, :]), r=[mskf], w=[msk])
    pT = Rot([s.sb("m_pT%d" % i, [128, 512], BF16) for i in range(3)])
    rc = Rot([s.sb("m_rc%d" % i, [128, 1], F32) for i in range(2)])
    ost = Rot([s.sb("m_o%d" % i, [128, 128], F32) for i in range(2)])
    gi = 0
    for i in range(NQB):
        nkb = 2 * i + 2
        O = psO[i % 2]
        qs = slice(i * 128, (i + 1) * 128)
        for g0 in range(0, nkb, 4):
            gn = min(4, nkb - g0)
            ps = psA[gi % 2]
            gi += 1
            for j in range(gn):
                kb = g0 + j
                ks = slice(kb * 128, (kb + 1) * 128)
                s.op("pe", lambda: nc.tensor.matmul(ps[:, j * 128:(j + 1) * 128], kn[:, ks], qn[:, qs], start=True, stop=False),
                     r=[kn, qn], w=[ps])
                s.op("pe", lambda: nc.tensor.matmul(ps[:, j * 128:(j + 1) * 128], kr[:, ks], qr[:, qs], start=False, stop=True),
                     r=[kr, qr], w=[ps])
            pt = pT.get()
            s.op("act", lambda: nc.scalar.activation(out=pt[:, 0:gn * 128], in_=ps[:, 0:gn * 128], func=AF.Exp), r=[ps], w=[pt])
            for j in range(gn):
                kb = g0 + j
                if kb >= 2 * i:
                    mo = (kb - 2 * i) * 128
                    s.op("dve", lambda: nc.vector.tensor_tensor(out=pt[:, j * 128:(j + 1) * 128], in0=pt[:, j * 128:(j + 1) * 128],
                                                                in1=msk[:, mo:mo + 128], op=ALU.mult), r=[pt, msk], w=[pt])
            for j in range(gn):
                kb = g0 + j
                s.op("pe", lambda: nc.tensor.matmul(O[:, 0:129], pt[:, j * 128:(j + 1) * 128], v[:, kb, 0:129],
                                                    start=(kb == 0), stop=(kb == nkb - 1)), r=[pt, v], w=[O])
        r_ = rc.get()
        s.op("dve", lambda: nc.vector.reciprocal(out=r_[:, :], in_=O[:, 128:129]), r=[O], w=[r_])
        o = ost.get()
        s.op("act", lambda: nc.scalar.activation(out=o[:, :], in_=O[:, 0:128], func=AF.Copy, scale=r_[:, 0:1]), r=[O, r_], w=[o])
        s.dma("sp", D["o_mla"][i * 128:(i + 1) * 128, :], o[:, :], r=[o], w=[(D["o_mla"], i)], is_output=True)


def rwkv_masks():
    s_ = np.arange(128)[:, None]
    t_ = np.arange(128)[None, :]
    m = np.zeros((128, 4, 128), np.float32)
    m[:, 0, :] = (s_ < t_)
    m[:, 1, :] = (s_ > t_)
    m[:, 2, :] = (s_ <= t_)
    m[:, 3, :] = (s_ == t_)
    return m


def mixer_rwkv(s, nc, cn, pp, S, TB, D, has_vres):
    NB = S // TB
    NCH = TB // 128
    EH = math.exp(-0.5)
    ones = cn["ones_f"]
    ident = cn["ident"]
    msk = s.sb("w_msk", [128, 4, 128], F32)
    s.dma("sp", msk[:], D["rw_masks"][:], r=[D["rw_masks"]], w=[msk])
    c64 = s.sb("w_c64", [64, 16], F32)
    s.dma("sp", c64[:], D["rw_c64"][:], r=[D["rw_c64"]], w=[c64])
    c128 = s.sb("w_c128", [128, 5], F32)
    s.dma("sp", c128[:], D["rw_c128"][:], r=[D["rw_c128"]], w=[c128])
    w2 = s.sb("w_w2", [64, 64], F32)
    a2 = s.sb("w_a2", [64, 64], F32)
    g2 = s.sb("w_g2", [128, 64], F32)
    for t_, n_ in ((w2, "rw_w2"), (a2, "rw_a2"), (g2, "rw_g2")):
        s.dma("sp", t_[:], D[n_][:], r=[D[n_]], w=[t_])
    if has_vres:
        v1 = s.sb("w_v1", [128, 4, 32], F32)
        v2 = s.sb("w_v2", [32, 64], F32)
        s.dma("sp", v1[:], D["rw_v1"][:], r=[D["rw_v1"]], w=[v1])
        s.dma("sp", v2[:], D["rw_v2"][:], r=[D["rw_v2"]], w=[v2])
    s.op("dve", lambda: nc.vector.tensor_scalar(out=c64[:, 13:14], in0=c64[:, 8:9], scalar1=-1.0, scalar2=1.0, op0=ALU.mult, op1=ALU.add),
         r=[c64], w=[c64])
    HW = TB + 1
    Ain = Rot([s.sb("w_Ain%d" % i, [64, 5, HW], F32) for i in range(2)])
    Gin = Rot([s.sb("w_Gin%d" % i, [128, HW], F32) for i in range(2)])
    pm = s.sb("w_pm", [64, 5, TB], F32)
    pg = s.sb("w_pg", [128, TB], F32)
    dtmp = Rot([s.sb("w_dt%d" % i, [128, TB], F32) for i in range(2)])
    if has_vres:
        Vin = Rot([s.sb("w_Vin%d" % i, [128, 4, HW], F32) for i in range(2)])
        pv = s.sb("w_pv", [128, 4, TB], F32)
        vfin = Rot([s.sb("w_vf%d" % i, [64, TB], F32) for i in range(2)])
        t1s = s.sb("w_t1s", [32, TB], F32)
    names = ["sg", "asig", "gT", "kk", "kmod", "bvec", "lw", "cum", "epos", "eprev", "eneg", "Rt", "At", "Bt", "Kt", "bonus", "yT", "t64a", "t64b"]
    T = {n: s.sb("w_" + n, [64, TB], F32) for n in names}
    ob = Rot([s.sb("w_o%d" % i, [64, TB], F32) for i in range(2)])
    M = [s.sb("w_M%d" % i, [64, 64], F32) for i in range(2)]
    s.op("pool", lambda: nc.gpsimd.memset(M[0][:], 0.0), w=[M[0]])
    Mp = s.sb("w_Mp", [64, 64], F32)
    sq = {n: Rot([s.sb("w_%s%d" % (n, i), [128, 128], F32) for i in range(3 if n in ("NakT", "NrbT", "NrkT", "Zf") else 2)])
          for n in ("NakT", "NrbT", "NrkT", "P", "Q", "Z", "Zf")}
    tm = {n: Rot([s.sb("w_%s%d" % (n, i), [128, 64], F32) for i in range(3)]) for n in ("Btm", "Ktm", "Vtm", "W2", "RHS", "U")}
    KVp = Rot([s.sb("w_KVp%d" % i, [64, 64], F32) for i in range(3)])
    mi = 0
    for b in range(NB):
        t0 = b * TB
        A_, G_ = Ain.get(), Gin.get()
        loads = [(A_, "rw_A", True), (G_, "rw_G", False)]
        if has_vres:
            V_ = Vin.get()
            loads.append((V_, "rw_V", True))
        for (buf, name, three) in loads:
            if b == 0:
                if three:
                    s.op("pool", lambda: nc.gpsimd.memset(buf[:, :, 0:1], 0.0), w=[buf])
                    s.dma("sp", buf[:, :, 1:HW], D[name][:, :, 0:TB], r=[D[name]], w=[buf])
                else:
                    s.op("pool", lambda: nc.gpsimd.memset(buf[:, 0:1], 0.0), w=[buf])
                    s.dma("sp", buf[:, 1:HW], D[name][:, 0:TB], r=[D[name]], w=[buf])
            else:
                if three:
                    s.dma("sp", buf[:, :, :], D[name][:, :, t0 - 1:t0 + TB], r=[D[name]], w=[buf])
                else:
                    s.dma("sp", buf[:, :], D[name][:, t0 - 1:t0 + TB], r=[D[name]], w=[buf])
        for j in range(5):
            dd = dtmp.get()
            s.op("pool", lambda: nc.gpsimd.tensor_tensor(out=dd[0:64, :], in0=A_[:, j, 0:TB], in1=A_[:, j, 1:HW], op=ALU.subtract), r=[A_], w=[dd])
            s.op("dve", lambda: nc.vector.scalar_tensor_tensor(out=pm[:, j, :], in0=dd[0:64, :], scalar=c64[:, j:j + 1], in1=A_[:, j, 1:HW],
                                                               op0=ALU.mult, op1=ALU.add), r=[dd, c64, A_], w=[(pm, j)])
        dd = dtmp.get()
        s.op("pool", lambda: nc.gpsimd.tensor_tensor(out=dd[:, :], in0=G_[:, 0:TB], in1=G_[:, 1:HW], op=ALU.subtract), r=[G_], w=[dd])
        s.op("dve", lambda: nc.vector.scalar_tensor_tensor(out=pg[:, :], in0=dd[:, :], scalar=c128[:, 0:1], in1=G_[:, 1:HW],
                                                           op0=ALU.mult, op1=ALU.add), r=[dd, c128, G_], w=[pg])
        if has_vres:
            for j in range(4):
                dd = dtmp.get()
                s.op("pool", lambda: nc.gpsimd.tensor_tensor(out=dd[:, :], in0=V_[:, j, 0:TB], in1=V_[:, j, 1:HW], op=ALU.subtract), r=[V_], w=[dd])
                s.op("dve", lambda: nc.vector.scalar_tensor_tensor(out=pv[:, j, :], in0=dd[:, :], scalar=c128[:, 1 + j:2 + j], in1=V_[:, j, 1:HW],
                                                                   op0=ALU.mult, op1=ALU.add), r=[dd, c128, V_], w=[(pv, j)])
        pr, pk, pvh, pwl, pal = (pm[:, j, :] for j in range(5))
        R5 = [(pm, j) for j in range(5)]
        s.op("act", lambda: nc.scalar.activation(out=T["t64a"][:, :], in_=pwl, func=AF.Tanh), r=[R5[3]], w=[T["t64a"]])
        p_ = pp.get()
        s.op("pe", lambda: nc.tensor.matmul(p_[0:64, 0:TB], w2[:, :], T["t64a"][:, :], start=True, stop=True), r=[w2, T["t64a"]], w=[p_])
        s.op("act", lambda: nc.scalar.activation(out=T["sg"][:, :], in_=p_[0:64, 0:TB], func=AF.Sigmoid, bias=c64[:, 5:6]), r=[p_, c64], w=[T["sg"]])
        s.op("dve", lambda: nc.vector.tensor_scalar(out=T["lw"][:, :], in0=T["sg"][:, :], scalar1=-EH, scalar2=None, op0=ALU.mult), r=[T["sg"]], w=[T["lw"]])
        p_ = pp.get()
        s.op("pe", lambda: nc.tensor.matmul(p_[0:64, 0:TB], a2[:, :], pal, start=True, stop=True), r=[a2, R5[4]], w=[p_])
        s.op("act", lambda: nc.scalar.activation(out=T["asig"][:, :], in_=p_[0:64, 0:TB], func=AF.Sigmoid, bias=c64[:, 6:7]), r=[p_, c64], w=[T["asig"]])
        dd = dtmp.get()
        s.op("act", lambda: nc.scalar.activation(out=dd[:, :], in_=pg[:, :], func=AF.Sigmoid), r=[pg], w=[dd])
        p_ = pp.get()
        s.op("pe", lambda: nc.tensor.matmul(p_[0:64, 0:TB], g2[:, :], dd[:, :], start=True, stop=True), r=[g2, dd], w=[p_])
        s.op("act", lambda: nc.scalar.copy(out=T["gT"][:, :], in_=p_[0:64, 0:TB]), r=[p_], w=[T["gT"]])
        if has_vres:
            p_ = pp.get()
            for j in range(4):
                s.op("pe", lambda: nc.tensor.matmul(p_[0:32, 0:TB], v1[:, j, :], pv[:, j, :], start=(j == 0), stop=(j == 3)), r=[v1, (pv, j)], w=[p_])
            s.op("act", lambda: nc.scalar.copy(out=t1s[:, :], in_=p_[0:32, 0:TB]), r=[p_], w=[t1s])
            p_ = pp.get()
            s.op("pe", lambda: nc.tensor.matmul(p_[0:64, 0:TB], v2[:, :], t1s[:, :], start=True, stop=True), r=[v2, t1s], w=[p_])
            s.op("act", lambda: nc.scalar.activation(out=T["t64a"][:, :], in_=p_[0:64, 0:TB], func=AF.Sigmoid, bias=c64[:, 12:13]), r=[p_, c64], w=[T["t64a"]])
            vf = vfin.get()
            s.dma("sp", vf[:], D["rw_vf"][:, t0:t0 + TB], r=[D["rw_vf"]], w=[vf])
            s.op("dve", lambda: nc.vector.tensor_tensor(out=T["t64b"][:, :], in0=vf[:, :], in1=pvh, op=ALU.subtract), r=[vf, R5[2]], w=[T["t64b"]])
            s.op("dve", lambda: nc.vector.tensor_tensor(out=T["t64b"][:, :], in0=T["t64b"][:, :], in1=T["t64a"][:, :], op=ALU.mult), r=[T["t64b"], T["t64a"]], w=[T["t64b"]])
            s.op("dve", lambda: nc.vector.tensor_tensor(out=pvh, in0=pvh, in1=T["t64b"][:, :], op=ALU.add), r=[R5[2], T["t64b"]], w=[R5[2]])
        else:
            s.dma("sp", D["o_vf"][:, t0:t0 + TB], pvh, r=[R5[2]], w=[(D["o_vf"], b)], is_output=True)
        s.op("dve", lambda: nc.vector.tensor_scalar(out=T["kk"][:, :], in0=pk, scalar1=c64[:, 7:8], scalar2=None, op0=ALU.mult), r=[R5[1], c64], w=[T["kk"]])
        s.op("act", lambda: nc.scalar.activation(out=T["t64a"][:, :], in_=T["kk"][:, :], func=AF.Square), r=[T["kk"]], w=[T["t64a"]])
        p_ = pp.get()
        s.op("pe", lambda: nc.tensor.matmul(p_[0:64, 0:TB], ones[0:64, 0:64], T["t64a"][:, :], start=True, stop=True), r=[ones, T["t64a"]], w=[p_])
        s.op("act", lambda: nc.scalar.activation(out=T["t64b"][:, :], in_=p_[0:64, 0:TB], func=AF.Sqrt), r=[p_], w=[T["t64b"]])
        s.op("dve", lambda: nc.vector.tensor_scalar(out=T["t64b"][:, :], in0=T["t64b"][:, :], scalar1=1e-12, scalar2=None, op0=ALU.max), r=[T["t64b"]], w=[T["t64b"]])
        s.op("dve", lambda: nc.vector.reciprocal(out=T["t64b"][:, :], in_=T["t64b"][:, :]), r=[T["t64b"]], w=[T["t64b"]])
        s.op("dve", lambda: nc.vector.tensor_tensor(out=T["kk"][:, :], in0=T["kk"][:, :], in1=T["t64b"][:, :], op=ALU.mult), r=[T["kk"], T["t64b"]], w=[T["kk"]])
        s.op("dve", lambda: nc.vector.tensor_scalar(out=T["t64a"][:, :], in0=T["asig"][:, :], scalar1=c64[:, 8:9], scalar2=c64[:, 13:14], op0=ALU.mult, op1=ALU.add),
             r=[T["asig"], c64], w=[T["t64a"]])
        s.op("dve", lambda: nc.vector.tensor_tensor(out=T["kmod"][:, :], in0=pk, in1=T["t64a"][:, :], op=ALU.mult), r=[R5[1], T["t64a"]], w=[T["kmod"]])
        s.op("pool", lambda: nc.gpsimd.tensor_tensor(out=T["bvec"][:, :], in0=T["kk"][:, :], in1=T["asig"][:, :], op=ALU.mult), r=[T["kk"], T["asig"]], w=[T["bvec"]])
        for j in range(NCH):
            cs_ = slice(j * 128, (j + 1) * 128)
            s.op("dve", lambda: nc.vector.tensor_tensor_scan(out=T["cum"][:, cs_], data0=ones[0:64, 0:128], data1=T["lw"][:, cs_], initial=0.0,
                                                             op0=ALU.mult, op1=ALU.add), r=[ones, T["lw"]], w=[T["cum"]])
        s.op("act", lambda: nc.scalar.activation(out=T["epos"][:, :], in_=T["cum"][:, :], func=AF.Exp), r=[T["cum"]], w=[T["epos"]])
        s.op("act", lambda: nc.scalar.activation(out=T["eneg"][:, :], in_=T["cum"][:, :], func=AF.Exp, scale=-1.0), r=[T["cum"]], w=[T["eneg"]])
        s.op("pool", lambda: nc.gpsimd.tensor_tensor(out=T["t64a"][:, :], in0=T["cum"][:, :], in1=T["lw"][:, :], op=ALU.subtract), r=[T["cum"], T["lw"]], w=[T["t64a"]])
        s.op("act", lambda: nc.scalar.activation(out=T["eprev"][:, :], in_=T["t64a"][:, :], func=AF.Exp), r=[T["t64a"]], w=[T["eprev"]])
        s.op("dve", lambda: nc.vector.tensor_tensor(out=T["Rt"][:, :], in0=pr, in1=T["epos"][:, :], op=ALU.mult), r=[R5[0], T["epos"]], w=[T["Rt"]])
        s.op("dve", lambda: nc.vector.scalar_tensor_tensor(out=T["At"][:, :], in0=T["kk"][:, :], scalar=-1.0, in1=T["eprev"][:, :], op0=ALU.mult, op1=ALU.mult),
             r=[T["kk"], T["eprev"]], w=[T["At"]])
        s.op("pool", lambda: nc.gpsimd.tensor_tensor(out=T["Bt"][:, :], in0=T["bvec"][:, :], in1=T["eneg"][:, :], op=ALU.mult), r=[T["bvec"], T["eneg"]], w=[T["Bt"]])
        s.op("dve", lambda: nc.vector.tensor_tensor(out=T["Kt"][:, :], in0=T["kmod"][:, :], in1=T["eneg"][:, :], op=ALU.mult), r=[T["kmod"], T["eneg"]], w=[T["Kt"]])
        s.op("dve", lambda: nc.vector.scalar_tensor_tensor(out=T["t64b"][:, :], in0=pr, scalar=c64[:, 9:10], in1=T["kmod"][:, :], op0=ALU.mult, op1=ALU.mult),
             r=[R5[0], c64, T["kmod"]], w=[T["t64b"]])
        p_ = pp.get()
        s.op("pe", lambda: nc.tensor.matmul(p_[0:64, 0:TB], ones[0:64, 0:64], T["t64b"][:, :], start=True, stop=True), r=[ones, T["t64b"]], w=[p_])
        s.op("dve", lambda: nc.vector.tensor_tensor(out=T["bonus"][:, :], in0=p_[0:64, 0:TB], in1=pvh, op=ALU.mult), r=[p_, R5[2]], w=[T["bonus"]])

        for j in range(NCH):
            cs_ = slice(j * 128, (j + 1) * 128)
            At, Bt, Kt, Rt = T["At"][:, cs_], T["Bt"][:, cs_], T["Kt"][:, cs_], T["Rt"][:, cs_]
            def gram(lhs, lhs_t, rhs, rhs_t, mk, dst):
                pq = pp.get()
                s.op("pe", lambda: nc.tensor.matmul(pq[:, 0:128], lhs, rhs, start=True, stop=True), r=[lhs_t, rhs_t], w=[pq])
                s.op("dve", lambda: nc.vector.tensor_tensor(out=dst[:, :], in0=pq[:, 0:128], in1=msk[:, mk, :], op=ALU.mult), r=[pq, msk], w=[dst])
            P = sq["P"].get(); Q = sq["Q"].get(); NakT = sq["NakT"].get(); NrbT = sq["NrbT"].get(); NrkT = sq["NrkT"].get(); Z = sq["Z"].get()
            gram(Bt, T["Bt"], At, T["At"], 0, P)
            gram(At, T["At"], Bt, T["Bt"], 1, Q)
            gram(Kt, T["Kt"], At, T["At"], 0, NakT)
            gram(Bt, T["Bt"], Rt, T["Rt"], 2, NrbT)
            gram(Kt, T["Kt"], Rt, T["Rt"], 2, NrkT)
            s.op("pool", lambda: nc.gpsimd.tensor_tensor(out=Z[:, :], in0=P[:, :], in1=msk[:, 3, :], op=ALU.add), r=[P, msk], w=[Z])
            for lv in range(6):
                Qn = sq["Q"].get()
                pq = pp.get()
                s.op("pe", lambda: nc.tensor.matmul(pq[:, 0:128], P[:, :], Q[:, :], start=True, stop=True), r=[P, Q], w=[pq])
                s.op("act", lambda: nc.scalar.copy(out=Qn[:, :], in_=pq[:, 0:128]), r=[pq], w=[Qn])
                if lv < 5:
                    Pn = sq["P"].get()
                    pq2 = pp.get()
                    s.op("pe", lambda: nc.tensor.matmul(pq2[:, 0:128], Q[:, :], P[:, :], start=True, stop=True), r=[P, Q], w=[pq2])
                    s.op("dve", lambda: nc.vector.tensor_copy(out=Pn[:, :], in_=pq2[:, 0:128]), r=[pq2], w=[Pn])
                pz = pp.get()
                s.op("pe", lambda: nc.tensor.matmul(pz[:, 0:128], Qn[:, :], Z[:, :], start=True, stop=True), r=[Qn, Z], w=[pz])
                Zn = sq["Z"].get() if lv < 5 else sq["Zf"].get()
                s.op("dve", lambda: nc.vector.tensor_tensor(out=Zn[:, :], in0=pz[:, 0:128], in1=Z[:, :], op=ALU.add), r=[pz, Z], w=[Zn])
                Z = Zn
                Q = Qn
                if lv < 5:
                    P = Pn
            def tpose(src, src_t, dst, scale=None):
                pq = pp.get()
                s.op("pe", lambda: nc.tensor.transpose(pq[:, 0:64], src, ident[0:64, 0:64]), r=[src_t, ident], w=[pq])
                s.op("act", lambda: nc.scalar.copy(out=dst[:, :], in_=pq[:, 0:64]), r=[pq], w=[dst])
            Btm, Ktm, Vtm = tm["Btm"].get(), tm["Ktm"].get(), tm["Vtm"].get()
            tpose(Bt, T["Bt"], Btm)
            tpose(Kt, T["Kt"], Ktm)
            tpose(pm[:, 2, cs_], R5[2], Vtm)
            W2 = tm["W2"].get()
            pq = pp.get()
            s.op("pe", lambda: nc.tensor.matmul(pq[:, 0:64], NakT[:, :], Vtm[:, :], start=True, stop=True), r=[NakT, Vtm], w=[pq])
            s.op("act", lambda: nc.scalar.copy(out=W2[:, :], in_=pq[:, 0:64]), r=[pq], w=[W2])
            e_ = j * 128 + 127
            pL = T["epos"][:, e_:e_ + 1]
            kvp = KVp.get()
            pq = pp.get()
            s.op("pe", lambda: nc.tensor.matmul(pq[0:64, 0:64], Ktm[:, :], Vtm[:, :], start=True, stop=True), r=[Ktm, Vtm], w=[pq])
            s.op("act", lambda: nc.scalar.activation(out=kvp[:, :], in_=pq[0:64, 0:64], func=AF.Copy, scale=pL), r=[pq, T["epos"]], w=[kvp])
            Mo, Mn = M[mi % 2], M[(mi + 1) % 2]
            mi += 1
            s.op("dve", lambda: nc.vector.scalar_tensor_tensor(out=Mp[:, :], in0=Mo[:, :], scalar=pL, in1=kvp[:, :], op0=ALU.mult, op1=ALU.add),
                 r=[Mo, T["epos"], kvp], w=[Mp])
            p1 = pp.get()
            s.op("pe", lambda: nc.tensor.matmul(p1[:, 0:64], At, Mo[:, :], start=True, stop=True), r=[T["At"], Mo], w=[p1])
            RHS = tm["RHS"].get()
            s.op("dve", lambda: nc.vector.tensor_tensor(out=RHS[:, :], in0=p1[:, 0:64], in1=W2[:, :], op=ALU.add), r=[p1, W2], w=[RHS])
            p2 = pp.get()
            s.op("pe", lambda: nc.tensor.matmul(p2[:, 0:64], Z[:, :], RHS[:, :], start=True, stop=True), r=[Z, RHS], w=[p2])
            U = tm["U"].get()
            s.op("act", lambda: nc.scalar.copy(out=U[:, :], in_=p2[:, 0:64]), r=[p2], w=[U])
            p3 = pp.get()
            s.op("pe", lambda: nc.tensor.matmul(p3[0:64, 0:64], Btm[:, :], U[:, :], start=True, stop=True), r=[Btm, U], w=[p3])
            s.op("dve", lambda: nc.vector.scalar_tensor_tensor(out=Mn[:, :], in0=p3[0:64, 0:64], scalar=pL, in1=Mp[:, :], op0=ALU.mult, op1=ALU.add),
                 r=[p3, T["epos"], Mp], w=[Mn])
            p4 = pp.get()
            s.op("pe", lambda: nc.tensor.matmul(p4[0:64, 0:128], Mo[:, :], Rt, start=True, stop=False), r=[Mo, T["Rt"]], w=[p4])
            s.op("pe", lambda: nc.tensor.matmul(p4[0:64, 0:128], U[:, :], NrbT[:, :], start=False, stop=False), r=[U, NrbT], w=[p4])
            s.op("pe", lambda: nc.tensor.matmul(p4[0:64, 0:128], Vtm[:, :], NrkT[:, :], start=False, stop=True), r=[Vtm, NrkT], w=[p4])
            s.op("act", lambda: nc.scalar.copy(out=T["yT"][:, cs_], in_=p4[0:64, 0:128]), r=[p4], w=[T["yT"]])

        y = T["yT"]
        s.op("act", lambda: nc.scalar.activation(out=T["t64a"][:, :], in_=y[:, :], func=AF.Square), r=[y], w=[T["t64a"]])
        pm_ = pp.get()
        s.op("pe", lambda: nc.tensor.matmul(pm_[0:64, 0:TB], ones[0:64, 0:64], y[:, :], start=True, stop=True), r=[ones, y], w=[pm_])
        pe_ = pp.get()
        s.op("pe", lambda: nc.tensor.matmul(pe_[0:64, 0:TB], ones[0:64, 0:64], T["t64a"][:, :], start=True, stop=True), r=[ones, T["t64a"]], w=[pe_])
        mean = T["t64b"]
        s.op("act", lambda: nc.scalar.activation(out=mean[:, :], in_=pm_[0:64, 0:TB], func=AF.Copy, scale=1.0 / 64), r=[pm_], w=[mean])
        s.op("dve", lambda: nc.vector.tensor_tensor(out=T["t64a"][:, :], in0=mean[:, :], in1=mean[:, :], op=ALU.mult), r=[mean], w=[T["t64a"]])
        s.op("dve", lambda: nc.vector.scalar_tensor_tensor(out=T["t64a"][:, :], in0=pe_[0:64, 0:TB], scalar=1.0 / 64, in1=T["t64a"][:, :], op0=ALU.mult, op1=ALU.subtract),
             r=[pe_, T["t64a"]], w=[T["t64a"]])
        s.op("act", lambda: nc.scalar.activation(out=T["t64a"][:, :], in_=T["t64a"][:, :], func=AF.Sqrt, bias=RWKV_GN_EPS), r=[T["t64a"]], w=[T["t64a"]])
        s.op("dve", lambda: nc.vector.reciprocal(out=T["t64a"][:, :], in_=T["t64a"][:, :]), r=[T["t64a"]], w=[T["t64a"]])
        s.op("dve", lambda: nc.vector.tensor_tensor(out=mean[:, :], in0=y[:, :], in1=mean[:, :], op=ALU.subtract), r=[y, mean], w=[mean])
        s.op("dve", lambda: nc.vector.tensor_tensor(out=mean[:, :], in0=mean[:, :], in1=T["t64a"][:, :], op=ALU.mult), r=[mean, T["t64a"]], w=[mean])
        s.op("dve", lambda: nc.vector.tensor_scalar(out=mean[:, :], in0=mean[:, :], scalar1=c64[:, 10:11], scalar2=c64[:, 11:12], op0=ALU.mult, op1=ALU.add),
             r=[mean, c64], w=[mean])
        s.op("pool", lambda: nc.gpsimd.tensor_tensor(out=mean[:, :], in0=mean[:, :], in1=T["bonus"][:, :], op=ALU.add), r=[mean, T["bonus"]], w=[mean])
        o = ob.get()
        s.op("dve", lambda: nc.vector.tensor_tensor(out=o[:, :], in0=mean[:, :], in1=T["gT"][:, :], op=ALU.mult), r=[mean, T["gT"]], w=[o])
        s.dma("sp", D["o_rwkv"][:, t0:t0 + TB], o[:, :], r=[o], w=[(D["o_rwkv"], b)], is_output=True)


def build_B(cfg, has_vres, parts=("ret", "ssd", "rwkv", "mla")):
    S = cfg.S
    TB = min(512, S)
    Sq = S // 2
    nc = bass.Bass("TRN2", target_bir_lowering=False)
    es = contextlib.ExitStack()
    with es:
        s = Sched(nc, es)
        IN, OUT = "ExternalInput", "ExternalOutput"
        D = {}

        def di(name, shape, dt=F32):
            D[name] = s.dram(name, shape, dt, IN)

        def do(name, shape, dt=F32):
            D[name] = s.dram(name, shape, dt, OUT)
        di("ident", [128, 128])
        for n in ("ret_q", "ret_k", "ret_v"):
            di(n, [128, S])
        di("ret_mask", [128, 128]); di("ret_qdec", [128, 128]); di("ret_col", [128, 2])
        do("o_ret", [128, S])
        di("ssd_x", [64, S]); di("ssd_B", [128, S]); di("ssd_C", [128, S]); di("ssd_dt", [1, S])
        di("ssd_cwx", [64, 5]); di("ssd_cwB", [128, 5]); di("ssd_cwC", [128, 5]); di("ssd_scal", [128, 4]); di("ssd_negmask", [128, 128])
        do("o_ssd", [64, S])
        di("mla_kn", [128, S], BF16); di("mla_kr", [64, S], BF16); di("mla_qn", [128, Sq], BF16); di("mla_qr", [64, Sq], BF16)
        di("mla_v", [S, 128], BF16); di("mla_mask", [128, 256])
        do("o_mla", [Sq, 128])
        di("rw_A", [64, 5, S]); di("rw_G", [128, S]); di("rw_c64", [64, 16]); di("rw_c128", [128, 5])
        di("rw_w2", [64, 64]); di("rw_a2", [64, 64]); di("rw_g2", [128, 64]); di("rw_masks", [128, 4, 128])
        if has_vres:
            di("rw_V", [128, 4, S]); di("rw_vf", [64, S]); di("rw_v1", [128, 4, 32]); di("rw_v2", [32, 64])
        else:
            do("o_vf", [64, S])
        do("o_rwkv", [64, S])
        cn = mk_consts(s, nc)
        cn["ident"] = s.sb("ident_t", [128, 128], F32)
        s.dma("sp", cn["ident"][:], D["ident"][:], r=[D["ident"]], w=[cn["ident"]])
        pp = PsumPool(s, 8)
        if "ret" in parts:
            s.push_scope()
            mixer_ret(s, nc, cn, pp, S, TB, D)
            s.pop_scope()
        if "ssd" in parts:
            s.push_scope()
            mixer_ssd(s, nc, cn, pp, S, TB, D)
            s.pop_scope()
        if "rwkv" in parts:
            s.push_scope()
            mixer_rwkv(s, nc, cn, pp, S, TB, D, has_vres)
            s.pop_scope()
        if "mla" in parts:
            s.push_scope()
            mixer_mla(s, nc, cn, S, D, pp.tiles[0:2], pp.tiles[2:4])
            s.pop_scope()
        s.finish()
        print("build_B instrs", s.n_instr)
    return nc


def host_B_inputs(inp, l, A, d, cfg, v_first):
    S = cfg.S
    m = {}
    m["ident"] = np.eye(128, dtype=np.float32)
    hr = d // 2
    m["ret_q"], m["ret_k"], m["ret_v"] = A["rq"][hr], A["rk"][hr], A["rv"][hr]
    m.update(ret_consts(hr))
    sx = A["sx"].reshape(1024, S)
    g = d // 4
    m["ssd_x"] = sx[d * 64:(d + 1) * 64]
    m["ssd_B"] = sx[512 + g * 128:512 + (g + 1) * 128]
    m["ssd_C"] = sx[768 + g * 128:768 + (g + 1) * 128]
    m["ssd_dt"] = A["sdt"][d:d + 1]
    cw, cb = inp["ssm_conv_w"][l], inp["ssm_conv_b"][l]

    def cwp(ch):
        return np.concatenate([cw[:, ch].T, cb[ch][:, None]], axis=1).astype(np.float32)
    m["ssd_cwx"] = cwp(np.arange(d * 64, (d + 1) * 64))
    m["ssd_cwB"] = cwp(512 + g * 128 + np.arange(128))
    m["ssd_cwC"] = cwp(768 + g * 128 + np.arange(128))
    sc = np.zeros((128, 4), np.float32)
    sc[:, 0] = inp["ssm_dt_bias"][l][d]
    sc[:, 1] = inp["ssm_a_log"][l][d]
    sc[:, 2] = inp["ssm_d"][l][d]
    m["ssd_scal"] = sc
    m.update(ssd_consts())
    hm, par = d // 2, d % 2
    NKB = S // 128
    qsel = np.concatenate([np.arange(b * 128, (b + 1) * 128) for b in range(par, NKB, 2)])
    m["mla_kn"], m["mla_kr"] = A["mkn"][hm], A["mkr"][hm]
    m["mla_qn"], m["mla_qr"] = A["mqn"][hm][:, qsel], A["mqr"][hm][:, qsel]
    m["mla_v"] = A["mv"][:, hm * 128:(hm + 1) * 128]
    m["mla_mask"] = mla_masks(par)
    rwf = A["rw"].reshape(1792, S)
    hd = np.arange(d * 64, (d + 1) * 64)
    m["rw_A"] = np.stack([rwf[hd], rwf[512 + hd], rwf[1024 + hd], rwf[1536:1600], rwf[1600:1664]], axis=1)
    m["rw_G"] = rwf[1664:1792]
    mu = inp["rwkv_mu"][l]
    c64 = np.zeros((64, 16), np.float32)
    c64[:, 0], c64[:, 1], c64[:, 2], c64[:, 3], c64[:, 4] = mu[hd], mu[512 + hd], mu[1024 + hd], mu[1536:1600], mu[1600:1664]
    c64[:, 5], c64[:, 6] = inp["rwkv_w0"][l][hd], inp["rwkv_a0"][l][hd]
    c64[:, 7], c64[:, 8] = inp["rwkv_k_k"][l][hd], inp["rwkv_k_a"][l][hd]
    c64[:, 9] = inp["rwkv_r_k"][l][d]
    c64[:, 10], c64[:, 11] = inp["rwkv_ln_w"][l][hd], inp["rwkv_ln_b"][l][hd]
    if l > 0:
        c64[:, 12] = inp["rwkv_v0"][l - 1][hd]
    m["rw_c64"] = c64
    c128 = np.zeros((128, 5), np.float32)
    c128[:, 0] = mu[1664:1792]
    c128[:, 1:5] = mu[1024:1536].reshape(4, 128).T
    m["rw_c128"] = c128
    m["rw_w2"], m["rw_a2"], m["rw_g2"] = inp["rwkv_w2"][l][:, hd], inp["rwkv_a2"][l][:, hd], inp["rwkv_g2"][l][:, hd]
    m["rw_masks"] = rwkv_masks()
    if l > 0:
        m["rw_V"] = rwf[1024:1536].reshape(4, 128, S).transpose(1, 0, 2)
        m["rw_vf"] = v_first[d]
        m["rw_v1"] = fm(inp["rwkv_v1"][l - 1])
        m["rw_v2"] = inp["rwkv_v2"][l - 1][:, hd]
    return {k: np.ascontiguousarray(v) for k, v in m.items()}


WB_COLS = 11264


def build_C(cfg, debug=False):
    TPC, NT, NP = cfg.TPC, cfg.NT, cfg.NP
    nc = bass.Bass("TRN2", target_bir_lowering=False)
    es = contextlib.ExitStack()
    with es:
        s = Sched(nc, es)
        IN, OUT = "ExternalInput", "ExternalOutput"
        xT = s.dram("xT", [128, KC, TPC], F32, IN)
        oa = s.dram("oa", [4, 128, TPC], F32, IN)
        ob = s.dram("ob", [4, 128, TPC], F32, IN)
        oc = s.dram("oc", [4, 128, TPC], F32, IN)
        od = s.dram("od", [4, 128, TPC], F32, IN)
        cw = s.dram("cw", [128, 40], F32, IN)
        wC1 = s.dram("wC1", [128, KC, 1024], F32, IN)
        wG = s.dram("wG", [128, KC, 8192], F32, IN)
        wBr = s.dram("wBr", [128, 16, 2048], F32, IN)
        wOut = s.dram("wOut", [128, KC, 2048], F32, IN)
        wGU = s.dram("wGU", [128, KC, 2 * D_FF], F32, IN)
        wDn = s.dram("wDn", [128, FFT, 2048], F32, IN)
        o_xT = s.dram("o_xT", [128, KC, TPC], F32, OUT)

        cn = mk_consts(s, nc)
        pp = PsumPool(s, 8)
        cw_t = s.sb("cw_t", [128, 40], F32)
        s.dma("sp", cw_t[:], cw[:], r=[cw], w=[cw_t])
        x_res = s.sb("x_res", [128, KC, TPC], F32)
        for c in range(KC):
            s.dma("sp", x_res[:, c, :], xT[:, c, :], r=[xT], w=[(x_res, c)])
        hT = s.sb("hT", [128, KC, NT], BF16)
        wbs = Rot([s.sb("wbuf%d" % i, [128, WB_COLS], BF16) for i in range(2)])
        sqrot = Rot([s.sb("sq%d" % i, [128, NT], F32) for i in range(2)])
        rstd_t = s.sb("rstd_t", [128, NT], F32)
        rtmp = s.sb("rtmp", [128, NT], F32)

        def load_w(dram_t, c0, ncols, nk):
            wb = wbs.get()
            view = wb[:, 0:nk * ncols].rearrange("p (c n) -> p c n", c=nk)
            step = max(1, 2048 // ncols)
            for k0 in range(0, nk, 8):
                k1 = min(nk, k0 + 8)
                s.dma("pool", view[:, k0:k1, :], dram_t[:, k0:k1, c0:c0 + ncols], r=[dram_t], w=[wb])
            return wb, view

        for p in range(NP):
            t0 = p * NT
            tsl = slice(t0, t0 + NT)

            def x_src(c):
                return x_res[:, c, tsl], [(x_res, c)]
            compute_hT(s, nc, cn, pp, x_src, Tn("cw_t", cw_t.t[:, 0:16]), hT, NT, sqrot, rstd_t, rtmp, True)

            s.push_scope()
            on_bf = s.sb("on_bf%d" % p, [128, 16, NT], BF16)
            merged = s.sb("merged%d" % p, [128, KC, NT], BF16)
            wbr = Rot([s.sb("wbr%d_%d" % (i, p), [128, 2048], BF16) for i in range(2)])
            ft = {n: s.sb("c_%s%d" % (n, p), [128, NT], F32) for n in ("x0", "x1", "x2", "x3", "sg", "mean", "var", "t1", "t2", "gate", "macc")}
            xin = [ft["x0"], ft["x1"], ft["x2"], ft["x3"]]
            wcur = {}

            def proj(col0, ps):
                wb1, w1v = wcur["w"]
                for c in range(KC):
                    s.op("pe", lambda: nc.tensor.matmul(ps[:, 0:NT], w1v[:, c, col0:col0 + 128], hT[:, c, 0:NT], start=(c == 0), stop=(c == KC - 1)),
                         r=[wb1, (hT, c)], w=[ps])
            wcur["w"] = load_w(wC1, 0, 512, KC)
            for h in range(4):
                x = xin[h % 2]
                s.dma("sp", x[:, :], oa[h, :, tsl], r=[oa], w=[x])
                psg = pp.get()
                proj(h * 128, psg)
                s.op("act", lambda: nc.scalar.activation(out=ft["sg"][:, :], in_=psg[:, 0:NT], func=AF.Silu), r=[psg], w=[ft["sg"]])
                sq = sqrot.get()
                s.op("act", lambda: nc.scalar.activation(out=sq[:, 0:NT], in_=x[:, :], func=AF.Square), r=[x], w=[sq])
                pm_ = pp.get()
                s.op("pe", lambda: nc.tensor.matmul(pm_[:, 0:NT], cn["ones_f"][:, :], x[:, :], start=True, stop=True), r=[cn["ones_f"], x], w=[pm_])
                pe_ = pp.get()
                s.op("pe", lambda: nc.tensor.matmul(pe_[:, 0:NT], cn["ones_f"][:, :], sq[:, 0:NT], start=True, stop=True), r=[cn["ones_f"], sq], w=[pe_])
                s.op("act", lambda: nc.scalar.activation(out=ft["mean"][:, :], in_=pm_[:, 0:NT], func=AF.Copy, scale=1.0 / 128), r=[pm_], w=[ft["mean"]])
                s.op("dve", lambda: nc.vector.tensor_tensor(out=ft["var"][:, :], in0=ft["mean"][:, :], in1=ft["mean"][:, :], op=ALU.mult), r=[ft["mean"]], w=[ft["var"]])
                s.op("dve", lambda: nc.vector.scalar_tensor_tensor(out=ft["var"][:, :], in0=pe_[:, 0:NT], scalar=1.0 / 128, in1=ft["var"][:, :],
                                                                   op0=ALU.mult, op1=ALU.subtract), r=[pe_, ft["var"]], w=[ft["var"]])
                s.op("act", lambda: nc.scalar.activation(out=ft["var"][:, :], in_=ft["var"][:, :], func=AF.Sqrt, bias=GN_EPS), r=[ft["var"]], w=[ft["var"]])
                s.op("dve", lambda: nc.vector.reciprocal(out=ft["var"][:, :], in_=ft["var"][:, :]), r=[ft["var"]], w=[ft["var"]])
                s.op("dve", lambda: nc.vector.tensor_tensor(out=ft["t1"][:, :], in0=x[:, :], in1=ft["mean"][:, :], op=ALU.subtract), r=[x, ft["mean"]], w=[ft["t1"]])
                s.op("dve", lambda: nc.vector.scalar_tensor_tensor(out=ft["t1"][:, :], in0=ft["t1"][:, :], scalar=cw_t[:, 32 + h:33 + h], in1=ft["var"][:, :],
                                                                   op0=ALU.mult, op1=ALU.mult), r=[ft["t1"], cw_t, ft["var"]], w=[ft["t1"]])
                s.op("dve", lambda: nc.vector.tensor_tensor(out=on_bf[:, h, :], in0=ft["t1"][:, :], in1=ft["sg"][:, :], op=ALU.mult),
                     r=[ft["t1"], ft["sg"]], w=[(on_bf, h)])
            wcur["w"] = load_w(wC1, 512, 512, KC)
            for g in range(2):
                for j in range(2):
                    i = g * 2 + j
                    x = xin[i]
                    s.dma("sp", x[:, :], ob[i, :, tsl], r=[ob], w=[x])
                    psz = pp.get()
                    proj(i * 128, psz)
                    s.op("act", lambda: nc.scalar.activation(out=ft["sg"][:, :], in_=psz[:, 0:NT], func=AF.Silu), r=[psz], w=[ft["sg"]])
                    s.op("dve", lambda: nc.vector.tensor_tensor(out=x[:, :], in0=x[:, :], in1=ft["sg"][:, :], op=ALU.mult), r=[x, ft["sg"]], w=[x])
                pss = pp.get()
                for j in range(2):
                    x = xin[g * 2 + j]
                    sq = sqrot.get()
                    s.op("act", lambda: nc.scalar.activation(out=sq[:, 0:NT], in_=x[:, :], func=AF.Square), r=[x], w=[sq])
                    s.op("pe", lambda: nc.tensor.matmul(pss[:, 0:NT], cn["ones_f"][:, :], sq[:, 0:NT], start=(j == 0), stop=(j == 1)),
                         r=[cn["ones_f"], sq], w=[pss])
                rstd_from_psum(s, nc, pss, 128, NT, 1.0 / 256, RMS_EPS, ft["var"], ft["mean"])
                for j in range(2):
                    i = g * 2 + j
                    x = xin[i]
                    s.op("dve", lambda: nc.vector.scalar_tensor_tensor(out=on_bf[:, 4 + i, :], in0=x[:, :], scalar=cw_t[:, 36 + i:37 + i], in1=ft["var"][:, :],
                                                                       op0=ALU.mult, op1=ALU.mult), r=[x, cw_t, ft["var"]], w=[(on_bf, 4 + i)])
            for i in range(4):
                s.dma("pool", on_bf[:, 8 + i, :], oc[i, :, tsl], r=[oc], w=[(on_bf, 8 + i)])
                s.dma("pool", on_bf[:, 12 + i, :], od[i, :, tsl], r=[od], w=[(on_bf, 12 + i)])
            for dt in range(16):
                wbg, wgv = load_w(wG, dt * 512, 512, KC)
                wb_ = wbr.get()
                s.dma("pool", wb_[:, :], wBr[:, dt, :], r=[wBr], w=[wb_])
                for n in range(4):
                    psg = pp.get()
                    for c in range(KC):
                        s.op("pe", lambda: nc.tensor.matmul(psg[:, 0:NT], wgv[:, c, n * 128:(n + 1) * 128], hT[:, c, 0:NT], start=(c == 0), stop=(c == KC - 1)),
                             r=[wbg, (hT, c)], w=[psg])
                    psb = pp.get()
                    for kc in range(4):
                        s.op("pe", lambda: nc.tensor.matmul(psb[:, 0:NT], wb_[:, (n * 4 + kc) * 128:(n * 4 + kc + 1) * 128], on_bf[:, n * 4 + kc, :],
                                                            start=(kc == 0), stop=(kc == 3)), r=[wb_, (on_bf, n * 4 + kc)], w=[psb])
                    s.op("act", lambda: nc.scalar.activation(out=ft["gate"][:, :], in_=psg[:, 0:NT], func=AF.Sigmoid), r=[psg], w=[ft["gate"]])
                    if n == 0:
                        s.op("dve", lambda: nc.vector.tensor_tensor(out=ft["macc"][:, :], in0=psb[:, 0:NT], in1=ft["gate"][:, :], op=ALU.mult),
                             r=[psb, ft["gate"]], w=[ft["macc"]])
                    else:
                        s.op("dve", lambda: nc.vector.tensor_tensor(out=ft["t2"][:, :], in0=psb[:, 0:NT], in1=ft["gate"][:, :], op=ALU.mult),
                             r=[psb, ft["gate"]], w=[ft["t2"]])
                        if n < 3:
                            s.op("pool", lambda: nc.gpsimd.tensor_tensor(out=ft["macc"][:, :], in0=ft["macc"][:, :], in1=ft["t2"][:, :], op=ALU.add),
                                 r=[ft["macc"], ft["t2"]], w=[ft["macc"]])
                        else:
                            s.op("pool", lambda: nc.gpsimd.tensor_tensor(out=merged[:, dt, :], in0=ft["macc"][:, :], in1=ft["t2"][:, :], op=ALU.add),
                                 r=[ft["macc"], ft["t2"]], w=[(merged, dt)])
            for g4 in range(4):
                wbo, wov = load_w(wOut, g4 * 512, 512, KC)
                for j in range(4):
                    d2 = g4 * 4 + j
                    ps = pp.get()
                    for c in range(KC):
                        s.op("pe", lambda: nc.tensor.matmul(ps[:, 0:NT], wov[:, c, j * 128:(j + 1) * 128], merged[:, c, :], start=(c == 0), stop=(c == KC - 1)),
                             r=[wbo, (merged, c)], w=[ps])
                    s.op("dve", lambda: nc.vector.tensor_tensor(out=x_res[:, d2, tsl], in0=x_res[:, d2, tsl], in1=ps[:, 0:NT], op=ALU.add),
                         r=[(x_res, d2), ps], w=[(x_res, d2)])
            if debug and p == 0:
                dbg1 = s.dram("dbg_on", [128, 16, NT], BF16, OUT)
                s.dma("sp", dbg1[:], on_bf[:], r=[on_bf], w=[dbg1], is_output=True)
                dbg2 = s.dram("dbg_mg", [128, 16, NT], BF16, OUT)
                s.dma("sp", dbg2[:], merged[:], r=[merged], w=[dbg2], is_output=True)
                dbg3 = s.dram("dbg_xm", [128, 16, NT], F32, OUT)
                for c in range(KC):
                    s.dma("sp", dbg3[:, c, :], x_res[:, c, tsl], r=[(x_res, c)], w=[(dbg3, c)], is_output=True)
            s.pop_scope()

            s.push_scope()
            compute_hT(s, nc, cn, pp, x_src, Tn("cw_t", cw_t.t[:, 16:32]), hT, NT, sqrot, rstd_t, rtmp, True)
            aT = s.sb("aT%d" % p, [128, FFT, NT], BF16)
            sgr = Rot([s.sb("f_sg%d_%d" % (i, p), [128, NT], F32) for i in range(2)])
            for gq in range(FFT // 2):
                wbf, wfv = load_w(wGU, gq * 512, 512, KC)
                for jj in range(2):
                    j = gq * 2 + jj
                    psg = pp.get()
                    psu = pp.get()
                    for (ps_, co) in ((psg, jj * 256), (psu, jj * 256 + 128)):
                        for c in range(KC):
                            s.op("pe", lambda: nc.tensor.matmul(ps_[:, 0:NT], wfv[:, c, co:co + 128], hT[:, c, 0:NT], start=(c == 0), stop=(c == KC - 1)),
                                 r=[wbf, (hT, c)], w=[ps_])
                    sg = sgr.get()
                    s.op("act", lambda: nc.scalar.activation(out=sg[:, :], in_=psg[:, 0:NT], func=AF.Silu), r=[psg], w=[sg])
                    s.op("dve", lambda: nc.vector.tensor_tensor(out=aT[:, j, :], in0=psu[:, 0:NT], in1=sg[:, :], op=ALU.mult), r=[psu, sg], w=[(aT, j)])
            for g8 in range(8):
                wbd, wdv = load_w(wDn, g8 * 256, 256, FFT)
                for j in range(2):
                    d2 = g8 * 2 + j
                    ps = pp.get()
                    for c in range(FFT):
                        s.op("pe", lambda: nc.tensor.matmul(ps[:, 0:NT], wdv[:, c, j * 128:(j + 1) * 128], aT[:, c, :], start=(c == 0), stop=(c == FFT - 1)),
                             r=[wbd, (aT, c)], w=[ps])
                    s.op("dve", lambda: nc.vector.tensor_tensor(out=x_res[:, d2, tsl], in0=x_res[:, d2, tsl], in1=ps[:, 0:NT], op=ALU.add),
                         r=[(x_res, d2), ps], w=[(x_res, d2)])
            s.pop_scope()
        for c in range(KC):
            s.dma("sp", o_xT[:, c, :], x_res[:, c, :], r=[(x_res, c)], w=[(o_xT, c)], is_output=True)
        s.finish()
        print("build_C instrs", s.n_instr)
    return nc


def host_C_weights(inp, l):
    w = {}
    cwm = np.zeros((128, 40), np.float32)
    cwm[:, 0:16] = colvec(inp["norm1_w"][l])
    cwm[:, 16:32] = colvec(inp["norm2_w"][l])
    cwm[:, 32:36] = colvec(inp["ret_gn_w"][l])
    cwm[:, 36:40] = colvec(inp["ssm_norm_w"][l])
    w["cw"] = cwm
    win = inp["w_in"][l]
    w["wC1"] = fm(win[:, 1536:2560])
    gcols = np.concatenate([6216 + n * 2048 + dt * 128 + np.arange(128) for dt in range(16) for n in range(4)])
    w["wG"] = fm(win[:, gcols])
    wb = inp["w_branch"][l]
    t = wb.reshape(4, 4, 128, 16, 128)
    w["wBr"] = np.ascontiguousarray(t.transpose(2, 3, 0, 1, 4).reshape(128, 16, 2048))
    w["wOut"] = fm(inp["w_out"][l])
    gu = inp["ffn_w_gu"][l]
    fcols = np.concatenate([np.concatenate([j * 128 + np.arange(128), D_FF + j * 128 + np.arange(128)]) for j in range(FFT)])
    w["wGU"] = fm(gu[:, fcols])
    w["wDn"] = fm(inp["ffn_w_down"][l])
    return w


_NC_CACHE = {}


def _get_nc(kind, cfg, *args):
    key = (kind, cfg.S, cfg.depth) + tuple(args)
    if key not in _NC_CACHE:
        if kind == "A":
            _NC_CACHE[key] = build_A(cfg)
        elif kind == "B":
            _NC_CACHE[key] = build_B(cfg, *args)
        else:
            _NC_CACHE[key] = build_C(cfg)
    return _NC_CACHE[key]


def kernel(**inputs):
    inp = {k: np.asarray(v) for k, v in inputs.items()}
    x = inp["x"][0]
    S = x.shape[0]
    depth = inp["norm1_w"].shape[0]
    cfg = Cfg(S, depth)
    TPC = cfg.TPC
    sls = [slice(c * TPC, (c + 1) * TPC) for c in range(NCORES)]
    xT = [to_fm_tokens(x[sl].astype(np.float32)) for sl in sls]
    pos = [np.ascontiguousarray(inp["positions"][:, sl]).astype(np.int32) for sl in sls]
    v_first = None
    for l in range(depth):
        wA = host_A_weights(inp, l)
        resA = run_spmd(_get_nc("A", cfg), [dict(wA, xT=xT[c], pos=pos[c]) for c in range(NCORES)])
        A = {}
        for k_, n_ in (("rq", "o_rq"), ("rk", "o_rk"), ("rv", "o_rv"), ("sx", "o_sx"), ("sdt", "o_sdt"), ("rw", "o_rw"),
                       ("mqn", "o_mqn"), ("mqr", "o_mqr"), ("mkn", "o_mkn"), ("mkr", "o_mkr")):
            A[k_] = np.concatenate([np.asarray(r[n_]) for r in resA], axis=-1)
        A["mv"] = np.concatenate([np.asarray(r["o_mv"]) for r in resA], axis=0)
        del resA
        resB = run_spmd(_get_nc("B", cfg, l > 0), [host_B_inputs(inp, l, A, d, cfg, v_first) for d in range(NCORES)])
        del A
        if l == 0:
            v_first = [np.asarray(resB[d]["o_vf"]) for d in range(NCORES)]
        oa = np.stack([np.asarray(resB[2 * h]["o_ret"]) for h in range(4)])
        ob = np.concatenate([np.asarray(resB[d]["o_ssd"]) for d in range(NCORES)], axis=0).reshape(4, 128, S)
        od = np.concatenate([np.asarray(resB[d]["o_rwkv"]) for d in range(NCORES)], axis=0).reshape(4, 128, S)
        oc = np.zeros((4, 128, S), np.float32)
        for h in range(4):
            t = np.zeros((S // 128, 128, 128), np.float32)
            t[0::2] = np.asarray(resB[2 * h]["o_mla"]).reshape(-1, 128, 128)
            t[1::2] = np.asarray(resB[2 * h + 1]["o_mla"]).reshape(-1, 128, 128)
            oc[h] = t.reshape(S, 128).T
        del resB
        wC = host_C_weights(inp, l)
        resC = run_spmd(_get_nc("C", cfg), [dict(wC, xT=xT[c], oa=np.ascontiguousarray(oa[:, :, sls[c]]), ob=np.ascontiguousarray(ob[:, :, sls[c]]),
                                                oc=np.ascontiguousarray(oc[:, :, sls[c]]), od=np.ascontiguousarray(od[:, :, sls[c]]))
                                           for c in range(NCORES)])
        xT = [np.asarray(r["o_xT"]) for r in resC]
        del resC, wC, wA
    out = np.concatenate([t.transpose(1, 0, 2).reshape(D_MODEL, TPC).T for t in xT], axis=0)
    return np.ascontiguousarray(out[None]).astype(np.float32)
```

```python
import contextlib
import math
import numpy as np
import ml_dtypes
import concourse.bass as bass
import concourse.mybir as mybir
from concourse.bass_utils import run_bass_kernel_spmd

F32 = mybir.dt.float32
BF16 = mybir.dt.bfloat16
I32 = mybir.dt.int32
AF = mybir.ActivationFunctionType
ALU = mybir.AluOpType

NCORES = 8
D_MODEL = 2048
KC = D_MODEL // 128
D_FF = 5632
FFT = D_FF // 128
RMS_EPS = 1e-6
GN_EPS = 1e-5
RWKV_GN_EPS = 64e-5
N_IN = 14408


class Cfg:
    def __init__(self, S=8192, depth=4):
        self.S = S
        self.depth = depth
        self.TPC = S // NCORES
        self.NT = min(512, self.TPC)
        self.NP = self.TPC // self.NT


class Tn:
    def __init__(self, name, t):
        self.name = name
        self.t = t

    def __getitem__(self, k):
        return self.t[k]


class Sched:
    ND = 6
    SAME = {"pe": False, "act": True, "dve": True, "pool": True}

    def __init__(self, nc, es):
        self.nc = nc
        self.es = es
        self.eng = dict(pe=nc.tensor, act=nc.scalar, dve=nc.vector, pool=nc.gpsimd, sp=nc.sync)
        self.csem = {e: es.enter_context(nc.semaphore("c_" + e)) for e in ("pe", "act", "dve", "pool")}
        self.ccnt = {e: 0 for e in self.csem}
        self.waited = {e: {} for e in self.eng}
        self.dsem = {q: [es.enter_context(nc.semaphore("d_%s%d" % (q, i))) for i in range(self.ND)]
                     for q in ("sp", "pool", "act")}
        self.dcnt = {q: [0] * self.ND for q in self.dsem}
        self.drr = {q: 0 for q in self.dsem}
        self.state = {}
        self.scopes = []
        self.out_tokens = []
        self.n_instr = 0

    def sb(self, name, shape, dtype):
        es = self.scopes[-1] if self.scopes else self.es
        return Tn(name, es.enter_context(self.nc.sbuf_tensor(name, list(shape), dtype)))

    def push_scope(self):
        self.scopes.append(contextlib.ExitStack())

    def pop_scope(self):
        self.barrier()
        self.scopes.pop().close()

    def barrier(self):
        toks = [("c_" + e, self.csem[e], self.ccnt[e], "x") for e in self.csem if self.ccnt[e] > 0]
        for q in self.dsem:
            for i in range(self.ND):
                if self.dcnt[q][i] > 0:
                    toks.append(("d_%s%d" % (q, i), self.dsem[q][i], self.dcnt[q][i] * 16, "dma"))
        for e in self.eng:
            self._emit_waits(e, toks)

    def ps(self, name, shape, dtype=F32):
        return Tn(name, self.es.enter_context(self.nc.psum_tensor(name, list(shape), dtype)))

    def dram(self, name, shape, dtype, kind):
        return Tn(name, self.nc.dram_tensor(name, list(shape), dtype, kind=kind).ap())

    @staticmethod
    def _rk(r):
        if isinstance(r, tuple):
            return (r[0].name, r[1])
        return (r.name, "*")

    def _conf(self, key):
        name, k = key
        if k == "*":
            return [kk for kk in self.state if kk[0] == name]
        out = []
        if (name, k) in self.state:
            out.append((name, k))
        if (name, "*") in self.state:
            out.append((name, "*"))
        return out

    def _collect(self, reads, writes):
        toks = []
        for r in reads:
            for kk in self._conf(self._rk(r)):
                w = self.state[kk][0]
                if w is not None:
                    toks.append(w)
        for r in writes:
            for kk in self._conf(self._rk(r)):
                w, rds = self.state[kk]
                if w is not None:
                    toks.append(w)
                toks.extend(rds)
        return toks

    def _emit_waits(self, e, toks):
        best = {}
        for (sn, sh, val, pe) in toks:
            if pe == e and e in self.SAME and not self.SAME[e]:
                continue
            if self.waited[e].get(sn, 0) >= val:
                continue
            if sn not in best or best[sn][1] < val:
                best[sn] = (sh, val)
        for sn, (sh, val) in best.items():
            self.eng[e].wait_ge(sh, val)
            self.waited[e][sn] = val
            self.n_instr += 1

    def _update(self, tok, reads, writes):
        for r in writes:
            key = self._rk(r)
            if key[1] == "*":
                for kk in [kk for kk in self.state if kk[0] == key[0]]:
                    del self.state[kk]
            self.state[key] = [tok, []]
        for r in reads:
            key = self._rk(r)
            if key not in self.state:
                self.state[key] = [None, []]
            rds = self.state[key][1]
            rds[:] = [t for t in rds if t[0] != tok[0]]
            rds.append(tok)

    def op(self, e, fn, r=(), w=()):
        toks = self._collect(r, w)
        self._emit_waits(e, toks)
        ins = fn()
        self.ccnt[e] += 1
        ins.then_inc(self.csem[e], 1)
        tok = ("c_" + e, self.csem[e], self.ccnt[e], e)
        self._update(tok, r, w)
        self.n_instr += 1
        return tok

    def dma(self, q, out, in_, r=(), w=(), is_output=False):
        toks = self._collect(r, w)
        i = self.drr[q]
        self.drr[q] = (i + 1) % self.ND
        sn = "d_%s%d" % (q, i)
        sh = self.dsem[q][i]
        if self.dcnt[q][i] > 0:
            toks.append((sn, sh, self.dcnt[q][i] * 16, "dma"))
        self._emit_waits(q, toks)
        self.eng[q].dma_start(out=out, in_=in_).then_inc(sh, 16)
        self.dcnt[q][i] += 1
        tok = (sn, sh, self.dcnt[q][i] * 16, "dma")
        self._update(tok, r, w)
        if is_output:
            self.out_tokens.append(tok)
        self.n_instr += 1
        return tok

    def finish(self):
        self._emit_waits("sp", self.out_tokens)
        toks = [("c_" + e, self.csem[e], self.ccnt[e], e) for e in self.csem if self.ccnt[e] > 0]
        self._emit_waits("sp", toks)


class PsumPool:
    def __init__(self, s, n, width=512, tiles=None):
        self.tiles = tiles if tiles is not None else [s.ps("ps%d" % i, [128, width]) for i in range(n)]
        self.i = 0

    def get(self):
        t = self.tiles[self.i]
        self.i = (self.i + 1) % len(self.tiles)
        return t


class Rot:
    def __init__(self, tiles):
        self.tiles = tiles
        self.i = 0

    def get(self):
        t = self.tiles[self.i]
        self.i = (self.i + 1) % len(self.tiles)
        return t


def mk_consts(s, nc):
    c = {}
    c["ones_f"] = s.sb("ones_f", [128, 128], F32)
    s.op("pool", lambda: nc.gpsimd.memset(c["ones_f"][:], 1.0), w=[c["ones_f"]])
    c["ones_b"] = s.sb("ones_b", [128, 128], BF16)
    s.op("pool", lambda: nc.gpsimd.memset(c["ones_b"][:], 1.0), w=[c["ones_b"]])
    return c


def rstd_from_psum(s, nc, ps, P, N, scale, bias, out_t, tmp_t):
    s.op("act", lambda: nc.scalar.activation(out=tmp_t[0:P, 0:N], in_=ps[0:P, 0:N], func=AF.Sqrt,
                                             scale=scale, bias=bias), r=[ps], w=[tmp_t])
    s.op("dve", lambda: nc.vector.reciprocal(out=out_t[0:P, 0:N], in_=tmp_t[0:P, 0:N]), r=[tmp_t], w=[out_t])


def rope_tables(s, nc, pos_bc_i, TPC, inv_col, sgn_col, P, cos_t, sins_t, tmp):
    posf, ang, t, n, r = tmp
    INV2PI = 1.0 / (2.0 * math.pi)
    MAGIC = 12582912.0
    C1 = 6.28125
    C2 = 2.0 * math.pi - 6.28125
    PI_LO = 3.1415925
    s.op("dve", lambda: nc.vector.tensor_copy(out=posf[0:P, :], in_=pos_bc_i[0:P, :]), r=[pos_bc_i], w=[posf])
    s.op("dve", lambda: nc.vector.tensor_scalar(out=ang[0:P, :], in0=posf[0:P, :], scalar1=inv_col[0:P, 0:1],
                                                scalar2=None, op0=ALU.mult), r=[posf, inv_col], w=[ang])
    for which in ("sin", "cos"):
        off = 0.0 if which == "sin" else 0.25
        s.op("dve", lambda: nc.vector.tensor_scalar(out=t[0:P, :], in0=ang[0:P, :], scalar1=INV2PI, scalar2=off,
                                                    op0=ALU.mult, op1=ALU.add), r=[ang], w=[t])
        s.op("dve", lambda: nc.vector.tensor_scalar(out=n[0:P, :], in0=t[0:P, :], scalar1=MAGIC, scalar2=None,
                                                    op0=ALU.add), r=[t], w=[n])
        s.op("dve", lambda: nc.vector.tensor_scalar(out=n[0:P, :], in0=n[0:P, :], scalar1=MAGIC, scalar2=None,
                                                    op0=ALU.subtract), r=[n], w=[n])
        s.op("dve", lambda: nc.vector.scalar_tensor_tensor(out=r[0:P, :], in0=n[0:P, :], scalar=-C1, in1=ang[0:P, :],
                                                           op0=ALU.mult, op1=ALU.add), r=[n, ang], w=[r])
        s.op("dve", lambda: nc.vector.scalar_tensor_tensor(out=t[0:P, :], in0=n[0:P, :], scalar=-C2, in1=r[0:P, :],
                                                           op0=ALU.mult, op1=ALU.add), r=[n, r], w=[t])
        if which == "cos":
            s.op("dve", lambda: nc.vector.tensor_scalar(out=t[0:P, :], in0=t[0:P, :], scalar1=math.pi / 2, scalar2=None,
                                                        op0=ALU.add), r=[t], w=[t])
        s.op("dve", lambda: nc.vector.tensor_scalar(out=r[0:P, :], in0=t[0:P, :], scalar1=-PI_LO, scalar2=PI_LO,
                                                    op0=ALU.max, op1=ALU.min), r=[t], w=[r])
        if which == "sin":
            s.op("act", lambda: nc.scalar.activation(out=t[0:P, :], in_=r[0:P, :], func=AF.Sin), r=[r], w=[t])
            s.op("dve", lambda: nc.vector.tensor_scalar(out=sins_t[0:P, :], in0=t[0:P, :], scalar1=sgn_col[0:P, 0:1],
                                                        scalar2=None, op0=ALU.mult), r=[t, sgn_col], w=[sins_t])
        else:
            s.op("act", lambda: nc.scalar.activation(out=cos_t[0:P, :], in_=r[0:P, :], func=AF.Sin), r=[r], w=[cos_t])


def compute_hT(s, nc, cn, pp, x_src, nw, hT, NT, sq_rot, rstd_t, tmp_t, x_resident):
    ps = pp.get()
    for c in range(KC):
        xa, xd = x_src(c)
        sq = sq_rot.get()
        s.op("act", lambda: nc.scalar.activation(out=sq[:, 0:NT], in_=xa, func=AF.Square), r=xd, w=[sq])
        s.op("pe", lambda: nc.tensor.matmul(ps[:, 0:NT], cn["ones_f"][:, :], sq[:, 0:NT], start=(c == 0), stop=(c == KC - 1)),
             r=[sq, cn["ones_f"]], w=[ps])
    rstd_from_psum(s, nc, ps, 128, NT, 1.0 / D_MODEL, RMS_EPS, rstd_t, tmp_t)
    for c in range(KC):
        xa, xd = x_src(c)
        s.op("dve", lambda: nc.vector.scalar_tensor_tensor(out=hT[:, c, 0:NT], in0=xa, scalar=nw[:, c:c + 1],
                                                           in1=rstd_t[:, 0:NT], op0=ALU.mult, op1=ALU.mult),
             r=xd + [nw, rstd_t], w=[(hT, c)])


def planA():
    tiles = []
    for h in range(4):
        for kind in ("rq", "rqp", "rk", "rkp", "rv"):
            tiles.append((kind, h, 128))
    for i in range(8):
        tiles.append(("sx", i, 128))
    tiles.append(("sdt", 0, 8))
    for i in range(4):
        tiles.append(("cq", i, 128))
    for i in range(2):
        tiles.append(("ckv", i, 128))
    tiles.append(("kr", 0, 64))
    tiles.append(("krp", 0, 64))
    for i in range(14):
        tiles.append(("rw", i, 128))
    groups = []
    cur = []
    curw = 0
    off = 0
    for t in tiles:
        if curw + t[2] > 512:
            groups.append((off - curw, curw, cur))
            cur = []
            curw = 0
        cur.append((t[0], t[1], t[2], curw))
        curw += t[2]
        off += t[2]
    groups.append((off - curw, curw, cur))
    return tiles, groups, off


def colsA():
    idx = []
    p64 = (np.arange(128) + 64) % 128
    p32 = (np.arange(64) + 32) % 64
    for h in range(4):
        q0 = 0 + h * 128
        k0 = 512 + h * 128
        v0 = 1024 + h * 128
        idx += list(q0 + np.arange(128)) + list(q0 + p64) + list(k0 + np.arange(128)) + list(k0 + p64) + list(v0 + np.arange(128))
    idx += list(2560 + np.arange(1024))
    idx += list(3584 + np.arange(8))
    idx += list(3592 + np.arange(512))
    idx += list(4104 + np.arange(256))
    idx += list(4360 + np.arange(64)) + list(4360 + p32)
    idx += list(4424 + np.arange(1792))
    return np.array(idx, dtype=np.int64)


def build_A(cfg):
    TPC, NT, NP = cfg.TPC, cfg.NT, cfg.NP
    tiles, groups, NCA = planA()
    nc = bass.Bass("TRN2", target_bir_lowering=False)
    es = contextlib.ExitStack()
    with es:
        s = Sched(nc, es)
        IN, OUT = "ExternalInput", "ExternalOutput"
        xT = s.dram("xT", [128, KC, TPC], F32, IN)
        pos = s.dram("pos", [1, TPC], I32, IN)
        n1w = s.dram("n1w", [128, KC], F32, IN)
        wA = s.dram("wA", [128, KC, NCA], F32, IN)
        ropec = s.dram("ropec", [128, 4], F32, IN)
        wqb = s.dram("wqb", [128, 4, 1024], F32, IN)
        wkvb = s.dram("wkvb", [128, 2, 1024], F32, IN)
        mlaw = s.dram("mlaw", [128, 16], F32, IN)
        o_rq = s.dram("o_rq", [4, 128, TPC], F32, OUT)
        o_rk = s.dram("o_rk", [4, 128, TPC], F32, OUT)
        o_rv = s.dram("o_rv", [4, 128, TPC], F32, OUT)
        o_sx = s.dram("o_sx", [8, 128, TPC], F32, OUT)
        o_sdt = s.dram("o_sdt", [8, TPC], F32, OUT)
        o_rw = s.dram("o_rw", [14, 128, TPC], F32, OUT)
        o_mqn = s.dram("o_mqn", [4, 128, TPC], BF16, OUT)
        o_mqr = s.dram("o_mqr", [4, 64, TPC], BF16, OUT)
        o_mkn = s.dram("o_mkn", [4, 128, TPC], BF16, OUT)
        o_mkr = s.dram("o_mkr", [4, 64, TPC], BF16, OUT)
        o_mv = s.dram("o_mv", [TPC, 512], BF16, OUT)

        cn = mk_consts(s, nc)
        pp = PsumPool(s, 8)
        n1w_t = s.sb("n1w_t", [128, KC], F32)
        s.dma("sp", n1w_t[:], n1w[:], r=[n1w], w=[n1w_t])
        ropec_t = s.sb("ropec_t", [128, 4], F32)
        s.dma("sp", ropec_t[:], ropec[:], r=[ropec], w=[ropec_t])
        mlaw_t = s.sb("mlaw_t", [128, 16], F32)
        s.dma("sp", mlaw_t[:], mlaw[:], r=[mlaw], w=[mlaw_t])
        wqb_t = s.sb("wqb_t", [128, 4, 1024], BF16)
        for c in range(4):
            s.dma("pool", wqb_t[:, c, :], wqb[:, c, :], r=[wqb], w=[wqb_t])
        wkvb_t = s.sb("wkvb_t", [128, 2, 1024], BF16)
        for c in range(2):
            s.dma("pool", wkvb_t[:, c, :], wkvb[:, c, :], r=[wkvb], w=[wkvb_t])

        pos_i = s.sb("pos_i", [128, TPC], I32)
        s.dma("sp", pos_i[:], pos[0:1, :].partition_broadcast(128), r=[pos], w=[pos_i])
        tmp5 = [s.sb("rt%d" % i, [128, TPC], F32) for i in range(5)]
        cos_r = s.sb("cos_r", [128, TPC], F32)
        sin_r = s.sb("sin_r", [128, TPC], F32)
        cos_m = s.sb("cos_m", [64, TPC], F32)
        sin_m = s.sb("sin_m", [64, TPC], F32)
        inv_r = Tn("ropec_t", ropec_t.t[:, 0:1])
        sgn_r = Tn("ropec_t", ropec_t.t[:, 1:2])
        inv_m = Tn("ropec_t", ropec_t.t[:, 2:3])
        sgn_m = Tn("ropec_t", ropec_t.t[:, 3:4])
        rope_tables(s, nc, pos_i, TPC, inv_r, sgn_r, 128, cos_r, sin_r, tmp5)
        rope_tables(s, nc, pos_i, TPC, inv_m, sgn_m, 64, cos_m, sin_m, tmp5)

        hT = s.sb("hT", [128, KC, NT], BF16)
        xrot = Rot([s.sb("xb%d" % i, [128, NT], F32) for i in range(4)])
        sqrot = Rot([s.sb("sq%d" % i, [128, NT], F32) for i in range(2)])
        rstd_t = s.sb("rstd_t", [128, NT], F32)
        rtmp = s.sb("rtmp", [128, NT], F32)
        wrot = Rot([s.sb("wb%d" % i, [128, KC, 512], BF16) for i in range(3)])
        stg = Rot([s.sb("stg%d" % i, [128, NT], F32) for i in range(4)])
        t1rot = Rot([s.sb("t1_%d" % i, [128, NT], F32) for i in range(2)])
        cqT = s.sb("cqT", [128, 4, NT], F32)
        ckvT = s.sb("ckvT", [128, 2, NT], F32)
        krT = s.sb("krT", [64, 2, NT], F32)
        cqn = s.sb("cqn", [128, 4, NT], BF16)
        ckvn = s.sb("ckvn", [128, 2, NT], BF16)
        bstg = Rot([s.sb("bstg%d" % i, [128, NT], BF16) for i in range(3)])
        vstg = Rot([s.sb("vstg%d" % i, [128, 512], BF16) for i in range(2)])
        rs2 = s.sb("rs2", [128, NT], F32)

        for p in range(NP):
            t0 = p * NT
            tsl = slice(t0, t0 + NT)

            def x_src(c):
                xb = xrot.get()
                s.dma("sp", xb[:, 0:NT], xT[:, c, tsl], r=[xT], w=[xb])
                return xb[:, 0:NT], [xb]
            compute_hT(s, nc, cn, pp, x_src, n1w_t, hT, NT, sqrot, rstd_t, rtmp, False)

            pend = None
            for (c0, gw, gt) in groups:
                wb = wrot.get()
                s.dma("pool", wb[:, :, 0:gw], wA[:, :, c0:c0 + gw], r=[wA], w=[wb])
                for (kind, idx, ncol, lo) in gt:
                    ps = pp.get()
                    for c in range(KC):
                        s.op("pe", lambda: nc.tensor.matmul(ps[0:ncol, 0:NT], wb[:, c, lo:lo + ncol], hT[:, c, 0:NT],
                                                            start=(c == 0), stop=(c == KC - 1)),
                             r=[wb, (hT, c)], w=[ps])
                    if kind in ("rq", "rk"):
                        sc = 1.0 if kind == "rq" else 128.0 ** -0.5
                        t1 = t1rot.get()
                        s.op("dve", lambda: nc.vector.scalar_tensor_tensor(out=t1[:, 0:NT], in0=ps[:, 0:NT], scalar=sc,
                                                                           in1=cos_r[:, tsl], op0=ALU.mult, op1=ALU.mult),
                             r=[ps, cos_r], w=[t1])
                        pend = t1
                    elif kind in ("rqp", "rkp"):
                        sc = 1.0 if kind == "rqp" else 128.0 ** -0.5
                        t2 = t1rot.get()
                        s.op("dve", lambda: nc.vector.scalar_tensor_tensor(out=t2[:, 0:NT], in0=ps[:, 0:NT], scalar=sc,
                                                                           in1=sin_r[:, tsl], op0=ALU.mult, op1=ALU.mult),
                             r=[ps, sin_r], w=[t2])
                        st = stg.get()
                        s.op("dve", lambda: nc.vector.tensor_tensor(out=st[:, 0:NT], in0=pend[:, 0:NT], in1=t2[:, 0:NT],
                                                                    op=ALU.add), r=[pend, t2], w=[st])
                        dst = o_rq if kind == "rqp" else o_rk
                        s.dma("sp", dst[idx, :, tsl], st[:, 0:NT], r=[st], w=[(dst, (idx, p))], is_output=True)
                    elif kind in ("rv", "sx", "rw", "sdt"):
                        st = stg.get()
                        s.op("act", lambda: nc.scalar.copy(out=st[0:ncol, 0:NT], in_=ps[0:ncol, 0:NT]), r=[ps], w=[st])
                        if kind == "sdt":
                            s.dma("sp", o_sdt[:, tsl], st[0:8, 0:NT], r=[st], w=[(o_sdt, p)], is_output=True)
                        else:
                            dst = {"rv": o_rv, "sx": o_sx, "rw": o_rw}[kind]
                            s.dma("sp", dst[idx, :, tsl], st[:, 0:NT], r=[st], w=[(dst, (idx, p))], is_output=True)
                    elif kind == "cq":
                        s.op("act", lambda: nc.scalar.copy(out=cqT[:, idx, 0:NT], in_=ps[:, 0:NT]), r=[ps], w=[(cqT, idx)])
                    elif kind == "ckv":
                        s.op("act", lambda: nc.scalar.copy(out=ckvT[:, idx, 0:NT], in_=ps[:, 0:NT]), r=[ps], w=[(ckvT, idx)])
                    elif kind in ("kr", "krp"):
                        j = 0 if kind == "kr" else 1
                        s.op("act", lambda: nc.scalar.copy(out=krT[:, j, 0:NT], in_=ps[0:64, 0:NT]), r=[ps], w=[(krT, j)])

            def rms_feat(src, nch, P, wcol0, dst, nfeat):
                psn = pp.get()
                for c in range(nch):
                    sq = sqrot.get()
                    s.op("act", lambda: nc.scalar.activation(out=sq[0:P, 0:NT], in_=src[0:P, c, 0:NT], func=AF.Square),
                         r=[(src, c)], w=[sq])
                    s.op("pe", lambda: nc.tensor.matmul(psn[:, 0:NT], cn["ones_f"][0:P, :], sq[0:P, 0:NT],
                                                        start=(c == 0), stop=(c == nch - 1)), r=[sq, cn["ones_f"]], w=[psn])
                rstd_from_psum(s, nc, psn, 128, NT, 1.0 / nfeat, RMS_EPS, rs2, rtmp)
                for c in range(nch):
                    s.op("dve", lambda: nc.vector.scalar_tensor_tensor(out=dst[:, c, 0:NT], in0=src[:, c, 0:NT],
                                                                       scalar=mlaw_t[:, wcol0 + c:wcol0 + c + 1],
                                                                       in1=rs2[:, 0:NT], op0=ALU.mult, op1=ALU.mult),
                         r=[(src, c), mlaw_t, rs2], w=[(dst, c)])
            rms_feat(cqT, 4, 128, 0, cqn, 512)
            rms_feat(ckvT, 2, 128, 4, ckvn, 256)

            def head_qk(is_q, h):
                if is_q:
                    wt, nkc, src = wqb_t, 4, cqn
                    cn0, cr0, cp0 = h * 256, h * 256 + 128, h * 256 + 192
                    wn, wr, wp = 6, 7, 8
                else:
                    wt, nkc, src = wkvb_t, 2, ckvn
                    cn0 = h * 128
                    wn, wr, wp = 9, 10, 11
                psn = pp.get()
                for c in range(nkc):
                    s.op("pe", lambda: nc.tensor.matmul(psn[:, 0:NT], wt[:, c, cn0:cn0 + 128], src[:, c, 0:NT],
                                                        start=(c == 0), stop=(c == nkc - 1)), r=[wt, (src, c)], w=[psn])
                nope = stg.get()
                s.op("act", lambda: nc.scalar.copy(out=nope[:, 0:NT], in_=psn[:, 0:NT]), r=[psn], w=[nope])
                if is_q:
                    psr = pp.get()
                    psp = pp.get()
                    for (pst, c00) in ((psr, cr0), (psp, cp0)):
                        for c in range(nkc):
                            s.op("pe", lambda: nc.tensor.matmul(pst[0:64, 0:NT], wt[:, c, c00:c00 + 64], src[:, c, 0:NT],
                                                                start=(c == 0), stop=(c == nkc - 1)), r=[wt, (src, c)], w=[pst])
                    rr = stg.get()
                    s.op("act", lambda: nc.scalar.copy(out=rr[0:64, 0:NT], in_=psr[0:64, 0:NT]), r=[psr], w=[rr])
                    rp = stg.get()
                    s.op("act", lambda: nc.scalar.copy(out=rp[0:64, 0:NT], in_=psp[0:64, 0:NT]), r=[psp], w=[rp])
                    rr_ap, rp_ap, rdeps = rr[0:64, 0:NT], rp[0:64, 0:NT], [rr, rp]
                else:
                    rr_ap, rp_ap, rdeps = krT[:, 0, 0:NT], krT[:, 1, 0:NT], [(krT, 0), (krT, 1)]
                pss = pp.get()
                sq = sqrot.get()
                s.op("act", lambda: nc.scalar.activation(out=sq[:, 0:NT], in_=nope[:, 0:NT], func=AF.Square), r=[nope], w=[sq])
                s.op("pe", lambda: nc.tensor.matmul(pss[:, 0:NT], cn["ones_f"][:, :], sq[:, 0:NT], start=True, stop=False),
                     r=[sq, cn["ones_f"]], w=[pss])
                sq2 = sqrot.get()
                s.op("act", lambda: nc.scalar.activation(out=sq2[0:64, 0:NT], in_=rr_ap, func=AF.Square), r=rdeps, w=[sq2])
                s.op("pe", lambda: nc.tensor.matmul(pss[:, 0:NT], cn["ones_f"][0:64, :], sq2[0:64, 0:NT], start=False, stop=True),
                     r=[sq2, cn["ones_f"]], w=[pss])
                if is_q:
                    rstd_from_psum(s, nc, pss, 128, NT, 1.0, 192.0 * RMS_EPS, rs2, rtmp)
                else:
                    rstd_from_psum(s, nc, pss, 128, NT, 1.0 / 192.0, RMS_EPS, rs2, rtmp)
                ob = bstg.get()
                s.op("dve", lambda: nc.vector.scalar_tensor_tensor(out=ob[:, 0:NT], in0=nope[:, 0:NT], scalar=mlaw_t[:, wn:wn + 1],
                                                                   in1=rs2[:, 0:NT], op0=ALU.mult, op1=ALU.mult),
                     r=[nope, mlaw_t, rs2], w=[ob])
                dstn = o_mqn if is_q else o_mkn
                s.dma("sp", dstn[h, :, tsl], ob[:, 0:NT], r=[ob], w=[(dstn, (h, p))], is_output=True)
                t1 = t1rot.get()
                s.op("dve", lambda: nc.vector.scalar_tensor_tensor(out=t1[0:64, 0:NT], in0=rr_ap, scalar=mlaw_t[0:64, wr:wr + 1],
                                                                   in1=cos_m[:, tsl], op0=ALU.mult, op1=ALU.mult),
                     r=rdeps + [mlaw_t, cos_m], w=[t1])
                t2 = t1rot.get()
                s.op("dve", lambda: nc.vector.scalar_tensor_tensor(out=t2[0:64, 0:NT], in0=rp_ap, scalar=mlaw_t[0:64, wp:wp + 1],
                                                                   in1=sin_m[:, tsl], op0=ALU.mult, op1=ALU.mult),
                     r=rdeps + [mlaw_t, sin_m], w=[t2])
                s.op("dve", lambda: nc.vector.tensor_tensor(out=t1[0:64, 0:NT], in0=t1[0:64, 0:NT], in1=t2[0:64, 0:NT], op=ALU.add),
                     r=[t1, t2], w=[t1])
                ob2 = bstg.get()
                s.op("dve", lambda: nc.vector.tensor_tensor(out=ob2[0:64, 0:NT], in0=t1[0:64, 0:NT], in1=rs2[0:64, 0:NT], op=ALU.mult),
                     r=[t1, rs2], w=[ob2])
                dstr = o_mqr if is_q else o_mkr
                s.dma("sp", dstr[h, :, tsl], ob2[0:64, 0:NT], r=[ob2], w=[(dstr, (h, p))], is_output=True)

            for h in range(4):
                head_qk(True, h)
                head_qk(False, h)
            for tt in range(NT // 128):
                psv = pp.get()
                for c in range(2):
                    s.op("pe", lambda: nc.tensor.matmul(psv[:, 0:512], ckvn[:, c, tt * 128:(tt + 1) * 128], wkvb_t[:, c, 512:1024],
                                                        start=(c == 0), stop=(c == 1)), r=[(ckvn, c), wkvb_t], w=[psv])
                vs = vstg.get()
                s.op("act", lambda: nc.scalar.copy(out=vs[:, :], in_=psv[:, 0:512]), r=[psv], w=[vs])
                s.dma("sp", o_mv[t0 + tt * 128:t0 + (tt + 1) * 128, :], vs[:, :], r=[vs], w=[(o_mv, (p, tt))], is_output=True)
        s.finish()
        print("build_A instrs", s.n_instr)
    return nc


def fm(a):
    K, N = a.shape
    return np.ascontiguousarray(a.reshape(K // 128, 128, N).transpose(1, 0, 2))


def colvec(v, n=None):
    v = np.asarray(v, dtype=np.float32)
    return np.ascontiguousarray(v.reshape(-1, 128).T)


def pad_rows(a, rows=128):
    out = np.zeros((rows,) + a.shape[1:], dtype=a.dtype)
    out[: a.shape[0]] = a
    return out


def rope_consts():
    inv64 = (1.0 / (np.float32(10000.0) ** (np.arange(0, 128, 2, dtype=np.float32) / np.float32(128)))).astype(np.float32)
    inv32 = (1.0 / (np.float32(10000.0) ** (np.arange(0, 64, 2, dtype=np.float32) / np.float32(64)))).astype(np.float32)
    rc = np.zeros((128, 4), np.float32)
    d = np.arange(128)
    rc[:, 0] = inv64[d % 64]
    rc[:, 1] = np.where(d < 64, -1.0, 1.0)
    rc[:64, 2] = inv32[np.arange(64) % 32]
    rc[:64, 3] = np.where(np.arange(64) < 32, -1.0, 1.0)
    return rc


def host_A_weights(inp, l):
    p32 = (np.arange(64) + 32) % 64
    w = {}
    w["wA"] = fm(inp["w_in"][l][:, colsA()])
    w["n1w"] = colvec(inp["norm1_w"][l])
    w["ropec"] = rope_consts()
    qb = inp["mla_w_qb"][l]
    cols = []
    for h in range(4):
        b = h * 192
        cols += list(b + np.arange(128)) + list(b + 128 + np.arange(64)) + list(b + 128 + p32)
    w["wqb"] = fm(qb[:, np.array(cols)])
    kvb = inp["mla_w_kvb"][l]
    cols = []
    for h in range(4):
        cols += list(h * 256 + np.arange(128))
    for h in range(4):
        cols += list(h * 256 + 128 + np.arange(128))
    w["wkvb"] = fm(kvb[:, np.array(cols)])
    m = np.zeros((128, 16), np.float32)
    m[:, 0:4] = colvec(inp["mla_q_a_norm_w"][l])
    m[:, 4:6] = colvec(inp["mla_kv_a_norm_w"][l])
    qn = inp["mla_q_norm_w"][l]
    kn = inp["mla_k_norm_w"][l]
    m[:, 6] = qn[0:128]
    m[:64, 7] = qn[128:192]
    m[:64, 8] = qn[128 + p32]
    m[:, 9] = kn[0:128]
    m[:64, 10] = kn[128:192]
    m[:64, 11] = kn[128 + p32]
    w["mlaw"] = m
    return w


def to_fm_tokens(x2d):
    T, Dd = x2d.shape
    return np.ascontiguousarray(x2d.T.reshape(Dd // 128, 128, T).transpose(1, 0, 2))


def run_spmd(nc, in_maps):
    res = run_bass_kernel_spmd(nc, in_maps, core_ids=list(range(NCORES)))
    return res.results


def ret_consts(h):
    lg = np.log1p(-np.exp2(np.float32(-5.0 - h))).astype(np.float64)
    m = np.arange(128)[:, None]
    l = np.arange(128)[None, :]
    cm, cl = m // 64, l // 64
    mask = np.where(cm == cl, np.exp(lg * np.abs(l - m)), np.where(cm < cl, np.exp(lg * (l - m)), 0.0))
    c = {}
    c["ret_mask"] = mask.astype(np.float32)
    col = np.zeros((128, 2), np.float32)
    col[:, 0] = np.exp(lg * (127 - np.arange(128)))
    col[:, 1] = np.exp(lg * 128)
    c["ret_col"] = col
    c["ret_qdec"] = np.tile(np.exp(lg * (np.arange(128) + 1.0))[None, :], (128, 1)).astype(np.float32)
    return c


def mixer_ret(s, nc, cn, pp, S, TB, D):
    NB = S // TB
    mask = s.sb("r_mask", [128, 128], F32)
    s.dma("sp", mask[:], D["ret_mask"][:], r=[D["ret_mask"]], w=[mask])
    qdec = s.sb("r_qdec", [128, 128], F32)
    s.dma("sp", qdec[:], D["ret_qdec"][:], r=[D["ret_qdec"]], w=[qdec])
    rcol = s.sb("r_col", [128, 2], F32)
    s.dma("sp", rcol[:], D["ret_col"][:], r=[D["ret_col"]], w=[rcol])
    qb = Rot([s.sb("r_q%d" % i, [128, TB], F32) for i in range(2)])
    kb = Rot([s.sb("r_k%d" % i, [128, TB], F32) for i in range(2)])
    vb = Rot([s.sb("r_v%d" % i, [128, TB], F32) for i in range(2)])
    ob = Rot([s.sb("r_o%d" % i, [128, TB], F32) for i in range(2)])
    St = [s.sb("r_S%d" % i, [128, 128], F32) for i in range(2)]
    s.op("pool", lambda: nc.gpsimd.memset(St[0][:], 0.0), w=[St[0]])
    ktm = Rot([s.sb("r_ktm%d" % i, [128, 128], F32) for i in range(2)])
    vtm = Rot([s.sb("r_vtm%d" % i, [128, 128], F32) for i in range(2)])
    pT = Rot([s.sb("r_pT%d" % i, [128, 128], F32) for i in range(2)])
    qd = Rot([s.sb("r_qd%d" % i, [128, 128], F32) for i in range(2)])
    ident = cn["ident"]
    sc_i = 0
    for b in range(NB):
        bs = slice(b * TB, (b + 1) * TB)
        q, k, v, o = qb.get(), kb.get(), vb.get(), ob.get()
        s.dma("sp", q[:], D["ret_q"][:, bs], r=[D["ret_q"]], w=[q])
        s.dma("sp", k[:], D["ret_k"][:, bs], r=[D["ret_k"]], w=[k])
        s.dma("sp", v[:], D["ret_v"][:, bs], r=[D["ret_v"]], w=[v])
        for j in range(TB // 128):
            cs_ = slice(j * 128, (j + 1) * 128)
            Sold, Snew = St[sc_i % 2], St[(sc_i + 1) % 2]
            sc_i += 1
            p1 = pp.get()
            s.op("pe", lambda: nc.tensor.transpose(p1[:, 0:128], k[:, cs_], ident[:, :]), r=[k, ident], w=[p1])
            kt = ktm.get()
            s.op("act", lambda: nc.scalar.activation(out=kt[:, :], in_=p1[:, 0:128], func=AF.Copy, scale=rcol[:, 0:1]),
                 r=[p1, rcol], w=[kt])
            p2 = pp.get()
            s.op("pe", lambda: nc.tensor.transpose(p2[:, 0:128], v[:, cs_], ident[:, :]), r=[v, ident], w=[p2])
            vt = vtm.get()
            s.op("dve", lambda: nc.vector.tensor_copy(out=vt[:, :], in_=p2[:, 0:128]), r=[p2], w=[vt])
            p3 = pp.get()
            s.op("pe", lambda: nc.tensor.matmul(p3[:, 0:128], k[:, cs_], q[:, cs_], start=True, stop=True), r=[k, q], w=[p3])
            pt = pT.get()
            s.op("dve", lambda: nc.vector.tensor_tensor(out=pt[:, :], in0=p3[:, 0:128], in1=mask[:, :], op=ALU.mult),
                 r=[p3, mask], w=[pt])
            qdt = qd.get()
            s.op("pool", lambda: nc.gpsimd.tensor_tensor(out=qdt[:, :], in0=q[:, cs_], in1=qdec[:, :], op=ALU.mult),
                 r=[q, qdec], w=[qdt])
            p4 = pp.get()
            s.op("pe", lambda: nc.tensor.matmul(p4[:, 0:128], vt[:, :], pt[:, :], start=True, stop=False), r=[vt, pt], w=[p4])
            s.op("pe", lambda: nc.tensor.matmul(p4[:, 0:128], Sold[:, :], qdt[:, :], start=False, stop=True), r=[Sold, qdt], w=[p4])
            s.op("act", lambda: nc.scalar.copy(out=o[:, cs_], in_=p4[:, 0:128]), r=[p4], w=[o])
            p5 = pp.get()
            s.op("pe", lambda: nc.tensor.matmul(p5[:, 0:128], kt[:, :], vt[:, :], start=True, stop=True), r=[kt, vt], w=[p5])
            s.op("dve", lambda: nc.vector.scalar_tensor_tensor(out=Snew[:, :], in0=Sold[:, :], scalar=rcol[:, 1:2], in1=p5[:, 0:128],
                                                               op0=ALU.mult, op1=ALU.add), r=[Sold, rcol, p5], w=[Snew])
            yield 4.5
        s.dma("sp", D["o_ret"][:, bs], o[:], r=[o], w=[(D["o_ret"], b)], is_output=True)


def ssd_consts():
    m = np.arange(128)[:, None]
    l = np.arange(128)[None, :]
    return {"ssd_negmask": np.where(l >= m, 0.0, -30000.0).astype(np.float32)}


def mixer_ssd(s, nc, cn, pp, S, TB, D):
    NB = S // TB
    ident = cn["ident"]
    negmask = s.sb("s_negmask", [128, 128], F32)
    s.dma("sp", negmask[:], D["ssd_negmask"][:], r=[D["ssd_negmask"]], w=[negmask])
    cwx = s.sb("s_cwx", [64, 5], F32)
    cwB = s.sb("s_cwB", [128, 5], F32)
    cwC = s.sb("s_cwC", [128, 5], F32)
    scal = s.sb("s_scal", [128, 4], F32)
    for t_, n_ in ((cwx, "ssd_cwx"), (cwB, "ssd_cwB"), (cwC, "ssd_cwC"), (scal, "ssd_scal")):
        s.dma("sp", t_[:], D[n_][:], r=[D[n_]], w=[t_])
    Acol = s.sb("s_A", [128, 1], F32)
    s.op("act", lambda: nc.scalar.activation(out=Acol[:, :], in_=scal[:, 1:2], func=AF.Exp), r=[scal], w=[Acol])
    s.op("dve", lambda: nc.vector.tensor_scalar(out=Acol[:, :], in0=Acol[:, :], scalar1=-1.0, scalar2=None, op0=ALU.mult),
         r=[Acol], w=[Acol])
    onesrow = s.sb("s_onesrow", [1, 128], F32)
    s.op("pool", lambda: nc.gpsimd.memset(onesrow[:], 1.0), w=[onesrow])
    negrow = s.sb("s_negrow", [1, 128], F32)
    s.op("pool", lambda: nc.gpsimd.memset(negrow[:], -1.0), w=[negrow])
    HW = TB + 3
    xin = Rot([s.sb("s_xin%d" % i, [64, HW], F32) for i in range(2)])
    Bin = Rot([s.sb("s_Bin%d" % i, [128, HW], F32) for i in range(2)])
    Cin = Rot([s.sb("s_Cin%d" % i, [128, HW], F32) for i in range(2)])
    dtin = Rot([s.sb("s_dtin%d" % i, [1, TB], F32) for i in range(2)])
    xc = s.sb("s_xc", [64, TB], F32)
    Bc = s.sb("s_Bc", [128, TB], F32)
    Cc = s.sb("s_Cc", [128, TB], F32)
    acc = Rot([s.sb("s_acc%d" % i, [128, TB], F32) for i in range(2)])
    r1 = s.sb("s_r1", [1, TB], F32)
    r2 = s.sb("s_r2", [1, TB], F32)
    dtr = s.sb("s_dtr", [1, TB], F32)
    adt = s.sb("s_adt", [1, TB], F32)
    csr = s.sb("s_csr", [1, TB], F32)
    ecs = s.sb("s_ecs", [1, TB], F32)
    ob = Rot([s.sb("s_o%d" % i, [64, TB], F32) for i in range(2)])
    Sst = [s.sb("s_S%d" % i, [128, 64], F32) for i in range(2)]
    s.op("pool", lambda: nc.gpsimd.memset(Sst[0][:], 0.0), w=[Sst[0]])
    cols = Rot([s.sb("s_cols%d" % i, [128, 6], F32) for i in range(2)])
    tmpm = Rot([s.sb("s_tm%d" % i, [128, 128], F32) for i in range(2)])
    decT = Rot([s.sb("s_dec%d" % i, [128, 128], F32) for i in range(2)])
    pT = Rot([s.sb("s_pT%d" % i, [128, 128], F32) for i in range(2)])
    xdt = Rot([s.sb("s_xdt%d" % i, [128, 64], F32) for i in range(2)])
    xck = Rot([s.sb("s_xck%d" % i, [128, 64], F32) for i in range(2)])
    Btm = Rot([s.sb("s_Btm%d" % i, [128, 128], F32) for i in range(2)])
    Cdec = Rot([s.sb("s_Cdec%d" % i, [128, 128], F32) for i in range(2)])
    sc_i = 0
    for b in range(NB):
        t0 = b * TB
        xi, Bi, Ci, dti = xin.get(), Bin.get(), Cin.get(), dtin.get()
        for (buf, name, P) in ((xi, "ssd_x", 64), (Bi, "ssd_B", 128), (Ci, "ssd_C", 128)):
            if b == 0:
                s.op("pool", lambda: nc.gpsimd.memset(buf[0:P, 0:3], 0.0), w=[buf])
                s.dma("sp", buf[0:P, 3:HW], D[name][:, 0:TB], r=[D[name]], w=[buf])
            else:
                s.dma("sp", buf[0:P, :], D[name][:, t0 - 3:t0 + TB], r=[D[name]], w=[buf])
        s.dma("sp", dti[:], D["ssd_dt"][:, t0:t0 + TB], r=[D["ssd_dt"]], w=[dti])
        for (buf, cw, outt, P) in ((xi, cwx, xc, 64), (Bi, cwB, Bc, 128), (Ci, cwC, Cc, 128)):
            a = acc.get()
            s.op("dve", lambda: nc.vector.tensor_scalar(out=a[0:P, :], in0=buf[0:P, 0:TB], scalar1=cw[0:P, 0:1], scalar2=cw[0:P, 4:5],
                                                        op0=ALU.mult, op1=ALU.add), r=[buf, cw], w=[a])
            for kk in range(1, 4):
                s.op("dve", lambda: nc.vector.scalar_tensor_tensor(out=a[0:P, :], in0=buf[0:P, kk:kk + TB], scalar=cw[0:P, kk:kk + 1],
                                                                   in1=a[0:P, :], op0=ALU.mult, op1=ALU.add), r=[buf, cw, a], w=[a])
            s.op("act", lambda: nc.scalar.activation(out=outt[0:P, :], in_=a[0:P, :], func=AF.Silu), r=[a], w=[outt])
        s.op("dve", lambda: nc.vector.tensor_scalar(out=r1[:, :], in0=dti[:, :], scalar1=scal[0:1, 0:1], scalar2=None, op0=ALU.add),
             r=[dti, scal], w=[r1])
        s.op("dve", lambda: nc.vector.scalar_tensor_tensor(out=r2[:, :], in0=r1[:, :], scalar=-1.0, in1=r1[:, :], op0=ALU.mult, op1=ALU.max),
             r=[r1], w=[r2])
        s.op("act", lambda: nc.scalar.activation(out=r2[:, :], in_=r2[:, :], func=AF.Exp, scale=-1.0), r=[r2], w=[r2])
        s.op("act", lambda: nc.scalar.activation(out=r2[:, :], in_=r2[:, :], func=AF.Ln, bias=1.0), r=[r2], w=[r2])
        s.op("dve", lambda: nc.vector.scalar_tensor_tensor(out=dtr[:, :], in0=r1[:, :], scalar=0.0, in1=r2[:, :], op0=ALU.max, op1=ALU.add),
             r=[r1, r2], w=[dtr])
        s.op("dve", lambda: nc.vector.tensor_scalar(out=adt[:, :], in0=dtr[:, :], scalar1=Acol[0:1, 0:1], scalar2=None, op0=ALU.mult),
             r=[dtr, Acol], w=[adt])
        for j in range(TB // 128):
            cs_ = slice(j * 128, (j + 1) * 128)
            s.op("dve", lambda: nc.vector.tensor_tensor_scan(out=csr[:, cs_], data0=onesrow[:, :], data1=adt[:, cs_], initial=0.0,
                                                             op0=ALU.mult, op1=ALU.add), r=[onesrow, adt], w=[csr])
        s.op("act", lambda: nc.scalar.activation(out=ecs[:, :], in_=csr[:, :], func=AF.Exp), r=[csr], w=[ecs])
        yield 8.0
        o = ob.get()
        for j in range(TB // 128):
            cs_ = slice(j * 128, (j + 1) * 128)
            Sold, Snew = Sst[sc_i % 2], Sst[(sc_i + 1) % 2]
            sc_i += 1
            pc = pp.get()
            s.op("pe", lambda: nc.tensor.matmul(pc[:, 0:1], dtr[:, cs_], onesrow[:, 0:1], start=True, stop=True), r=[dtr, onesrow], w=[pc])
            s.op("pe", lambda: nc.tensor.matmul(pc[:, 1:2], csr[:, cs_], onesrow[:, 0:1], start=True, stop=True), r=[csr, onesrow], w=[pc])
            e_ = j * 128 + 127
            s.op("pe", lambda: nc.tensor.matmul(pc[:, 2:3], onesrow[:, :], csr[:, e_:e_ + 1], start=True, stop=True), r=[csr, onesrow], w=[pc])
            cl = cols.get()
            s.op("dve", lambda: nc.vector.tensor_copy(out=cl[:, 0:3], in_=pc[:, 0:3]), r=[pc], w=[cl])
            s.op("act", lambda: nc.scalar.activation(out=cl[:, 3:4], in_=cl[:, 1:2], func=AF.Exp, scale=-1.0, bias=cl[:, 2:3]), r=[cl], w=[cl])
            s.op("act", lambda: nc.scalar.activation(out=cl[:, 4:5], in_=cl[:, 2:3], func=AF.Exp), r=[cl], w=[cl])
            pg = pp.get()
            s.op("pe", lambda: nc.tensor.matmul(pg[:, 0:128], onesrow[:, :], csr[:, cs_], start=True, stop=False), r=[csr, onesrow], w=[pg])
            s.op("pe", lambda: nc.tensor.matmul(pg[:, 0:128], csr[:, cs_], negrow[:, :], start=False, stop=True), r=[csr, negrow], w=[pg])
            tm = tmpm.get()
            s.op("dve", lambda: nc.vector.scalar_tensor_tensor(out=tm[:, :], in0=pg[:, 0:128], scalar=0.0, in1=negmask[:, :],
                                                               op0=ALU.min, op1=ALU.add), r=[pg, negmask], w=[tm])
            dc = decT.get()
            s.op("act", lambda: nc.scalar.activation(out=dc[:, :], in_=tm[:, :], func=AF.Exp), r=[tm], w=[dc])
            pb = pp.get()
            s.op("pe", lambda: nc.tensor.matmul(pb[:, 0:128], Bc[:, cs_], Cc[:, cs_], start=True, stop=True), r=[Bc, Cc], w=[pb])
            pt = pT.get()
            s.op("dve", lambda: nc.vector.tensor_tensor(out=pt[:, :], in0=pb[:, 0:128], in1=dc[:, :], op=ALU.mult), r=[pb, dc], w=[pt])
            px = pp.get()
            s.op("pe", lambda: nc.tensor.transpose(px[:, 0:64], xc[:, cs_], ident[0:64, 0:64]), r=[xc, ident], w=[px])
            xd = xdt.get()
            s.op("act", lambda: nc.scalar.activation(out=xd[:, :], in_=px[:, 0:64], func=AF.Copy, scale=cl[:, 0:1]), r=[px, cl], w=[xd])
            xk = xck.get()
            s.op("dve", lambda: nc.vector.tensor_scalar(out=xk[:, :], in0=xd[:, :], scalar1=cl[:, 3:4], scalar2=None, op0=ALU.mult),
                 r=[xd, cl], w=[xk])
            pe_ = pp.get()
            s.op("pe", lambda: nc.tensor.matmul(pe_[:, 0:128], onesrow[:, :], ecs[:, cs_], start=True, stop=True), r=[ecs, onesrow], w=[pe_])
            cd = Cdec.get()
            s.op("dve", lambda: nc.vector.tensor_tensor(out=cd[:, :], in0=pe_[:, 0:128], in1=Cc[:, cs_], op=ALU.mult), r=[pe_, Cc], w=[cd])
            py = pp.get()
            s.op("pe", lambda: nc.tensor.matmul(py[0:64, 0:128], xd[:, :], pt[:, :], start=True, stop=False), r=[xd, pt], w=[py])
            s.op("pe", lambda: nc.tensor.matmul(py[0:64, 0:128], Sold[:, :], cd[:, :], start=False, stop=True), r=[Sold, cd], w=[py])
            s.op("dve", lambda: nc.vector.scalar_tensor_tensor(out=o[:, cs_], in0=xc[:, cs_], scalar=scal[0:64, 2:3], in1=py[0:64, 0:128],
                                                               op0=ALU.mult, op1=ALU.add), r=[xc, scal, py], w=[o])
            pB = pp.get()
            s.op("pe", lambda: nc.tensor.transpose(pB[:, 0:128], Bc[:, cs_], ident[:, :]), r=[Bc, ident], w=[pB])
            bt = Btm.get()
            s.op("act", lambda: nc.scalar.copy(out=bt[:, :], in_=pB[:, 0:128]), r=[pB], w=[bt])
            pS = pp.get()
            s.op("pe", lambda: nc.tensor.matmul(pS[:, 0:64], bt[:, :], xk[:, :], start=True, stop=True), r=[bt, xk], w=[pS])
            s.op("dve", lambda: nc.vector.scalar_tensor_tensor(out=Snew[:, :], in0=Sold[:, :], scalar=cl[:, 4:5], in1=pS[:, 0:64],
                                                               op0=ALU.mult, op1=ALU.add), r=[Sold, cl, pS], w=[Snew])
            yield 10.0
        s.dma("sp", D["o_ssd"][:, t0:t0 + TB], o[:], r=[o], w=[(D["o_ssd"], b)], is_output=True)


def mla_masks(par):
    k = np.arange(128)[:, None]
    q = np.arange(128)[None, :]
    diag = np.where((k >= 64) & (q < 64), 0.0, 1.0).astype(np.float32)
    if par == 0:
        mA, mB = diag, np.zeros((128, 128), np.float32)
    else:
        mA, mB = np.ones((128, 128), np.float32), diag
    return np.concatenate([mA, mB], axis=1)


def mixer_mla(s, nc, cn, S, D, psA, psOt):
    NKB = S // 128
    NQB = NKB // 2
    Sq = S // 2
    kn = s.sb("m_kn", [128, S], BF16)
    kr = s.sb("m_kr", [64, S], BF16)
    qn = s.sb("m_qn", [128, Sq], BF16)
    qr = s.sb("m_qr", [64, Sq], BF16)
    v = s.sb("m_v", [128, NKB, 130], BF16)
    msk = s.sb("m_msk", [128, 256], BF16)
    mskf = s.sb("m_mskf", [128, 256], F32)
    s.dma("sp", kn[:], D["mla_kn"][:], r=[D["mla_kn"]], w=[kn])
    s.dma("sp", kr[:], D["mla_kr"][:], r=[D["mla_kr"]], w=[kr])
    s.dma("sp", qn[:], D["mla_qn"][:], r=[D["mla_qn"]], w=[qn])
    s.dma("sp", qr[:], D["mla_qr"][:], r=[D["mla_qr"]], w=[qr])
    s.op("pool", lambda: nc.gpsimd.memset(v[:, :, 128:130], 1.0), w=[(v, "ones")])
    s.dma("sp", v[:, :, 0:128], D["mla_v"][:].rearrange("(b p) e -> p b e", p=128), r=[D["mla_v"]], w=[(v, "data")])
    s.dma("sp", mskf[:], D["mla_mask"][:], r=[D["mla_mask"]], w=[mskf])
    s.op("dve", lambda: nc.vector.tensor_copy(out=msk[:, :], in_=mskf[:, :]), r=[mskf], w=[msk])
    pT = Rot([s.sb("m_pT%d" % i, [128, 512], BF16) for i in range(3)])
    rc = Rot([s.sb("m_rc%d" % i, [128, 1], F32) for i in range(2)])
    ost = Rot([s.sb("m_o%d" % i, [128, 128], F32) for i in range(2)])
    gi = 0
    yield 0.0
    for i in range(NQB):
        nkb = 2 * i + 2
        oo = (i % 2) * 256
        Ok = (psOt, i % 2)
        qs = slice(i * 128, (i + 1) * 128)
        for g0 in range(0, nkb, 4):
            gn = min(4, nkb - g0)
            ps = psA[gi % 2]
            gi += 1
            for j in range(gn):
                kb = g0 + j
                ks = slice(kb * 128, (kb + 1) * 128)
                s.op("pe", lambda: nc.tensor.matmul(ps[:, j * 128:(j + 1) * 128], kn[:, ks], qn[:, qs], start=True, stop=False),
                     r=[kn, qn], w=[ps])
                s.op("pe", lambda: nc.tensor.matmul(ps[:, j * 128:(j + 1) * 128], kr[:, ks], qr[:, qs], start=False, stop=True),
                     r=[kr, qr], w=[ps])
            pt = pT.get()
            s.op("act", lambda: nc.scalar.activation(out=pt[:, 0:gn * 128], in_=ps[:, 0:gn * 128], func=AF.Exp), r=[ps], w=[pt])
            for j in range(gn):
                kb = g0 + j
                if kb >= 2 * i:
                    mo = (kb - 2 * i) * 128
                    s.op("dve", lambda: nc.vector.tensor_tensor(out=pt[:, j * 128:(j + 1) * 128], in0=pt[:, j * 128:(j + 1) * 128],
                                                                in1=msk[:, mo:mo + 128], op=ALU.mult), r=[pt, msk], w=[pt])
            for j in range(gn):
                kb = g0 + j
                s.op("pe", lambda: nc.tensor.matmul(psOt[:, oo:oo + 129], pt[:, j * 128:(j + 1) * 128], v[:, kb, 0:129],
                                                    start=(kb == 0), stop=(kb == nkb - 1)), r=[pt, v], w=[Ok])
            yield 3.2
        r_ = rc.get()
        s.op("dve", lambda: nc.vector.reciprocal(out=r_[:, :], in_=psOt[:, oo + 128:oo + 129]), r=[Ok], w=[r_])
        o = ost.get()
        s.op("act", lambda: nc.scalar.activation(out=o[:, :], in_=psOt[:, oo:oo + 128], func=AF.Copy, scale=r_[:, 0:1]), r=[Ok, r_], w=[o])
        s.dma("sp", D["o_mla"][i * 128:(i + 1) * 128, :], o[:, :], r=[o], w=[(D["o_mla"], i)], is_output=True)


def rwkv_masks():
    s_ = np.arange(128)[:, None]
    t_ = np.arange(128)[None, :]
    m = np.zeros((128, 4, 128), np.float32)
    m[:, 0, :] = (s_ < t_)
    m[:, 1, :] = (s_ > t_)
    m[:, 2, :] = (s_ <= t_)
    m[:, 3, :] = (s_ == t_)
    return m


def mixer_rwkv(s, nc, cn, pp, S, TB, D, has_vres):
    NB = S // TB
    NCH = TB // 128
    EH = math.exp(-0.5)
    ones = cn["ones_f"]
    ident = cn["ident"]
    msk = s.sb("w_msk", [128, 4, 128], F32)
    s.dma("sp", msk[:], D["rw_masks"][:], r=[D["rw_masks"]], w=[msk])
    c64 = s.sb("w_c64", [64, 16], F32)
    s.dma("sp", c64[:], D["rw_c64"][:], r=[D["rw_c64"]], w=[c64])
    c128 = s.sb("w_c128", [128, 5], F32)
    s.dma("sp", c128[:], D["rw_c128"][:], r=[D["rw_c128"]], w=[c128])
    w2 = s.sb("w_w2", [64, 64], F32)
    a2 = s.sb("w_a2", [64, 64], F32)
    g2 = s.sb("w_g2", [128, 64], F32)
    for t_, n_ in ((w2, "rw_w2"), (a2, "rw_a2"), (g2, "rw_g2")):
        s.dma("sp", t_[:], D[n_][:], r=[D[n_]], w=[t_])
    if has_vres:
        v1 = s.sb("w_v1", [128, 4, 32], F32)
        v2 = s.sb("w_v2", [32, 64], F32)
        s.dma("sp", v1[:], D["rw_v1"][:], r=[D["rw_v1"]], w=[v1])
        s.dma("sp", v2[:], D["rw_v2"][:], r=[D["rw_v2"]], w=[v2])
    s.op("dve", lambda: nc.vector.tensor_scalar(out=c64[:, 13:14], in0=c64[:, 8:9], scalar1=-1.0, scalar2=1.0, op0=ALU.mult, op1=ALU.add),
         r=[c64], w=[c64])
    HW = TB + 1
    Ain = Rot([s.sb("w_Ain%d" % i, [64, 5, HW], F32) for i in range(1)])
    Gin = Rot([s.sb("w_Gin%d" % i, [128, HW], F32) for i in range(2)])
    pm = s.sb("w_pm", [64, 5, TB], F32)
    pg = s.sb("w_pg", [128, TB], F32)
    dtmp = Rot([s.sb("w_dt%d" % i, [128, TB], F32) for i in range(2)])
    if has_vres:
        Vin = Rot([s.sb("w_Vin%d" % i, [128, 4, HW], F32) for i in range(1)])
        pv = s.sb("w_pv", [128, 4, TB], F32)
        vfin = Rot([s.sb("w_vf%d" % i, [64, TB], F32) for i in range(2)])
        t1s = s.sb("w_t1s", [32, TB], F32)
    names = ["sg", "asig", "gT", "kk", "kmod", "bvec", "lw", "cum", "epos", "eprev", "eneg", "Rt", "At", "Bt", "Kt", "bonus", "yT", "t64a", "t64b"]
    T = {n: s.sb("w_" + n, [64, TB], F32) for n in names}
    ob = Rot([s.sb("w_o%d" % i, [64, TB], F32) for i in range(2)])
    M = [s.sb("w_M%d" % i, [64, 64], F32) for i in range(2)]
    s.op("pool", lambda: nc.gpsimd.memset(M[0][:], 0.0), w=[M[0]])
    Mp = s.sb("w_Mp", [64, 64], F32)
    sqs = {}
    for n in ("P", "Q", "Z"):
        sqs[n] = [[s.sb("w_%s%d_%d" % (n, j, i), [128, 128], F32) for i in range(2)] for j in range(NCH)]
    for n in ("Zf", "NakT", "NrbT", "NrkT"):
        sqs[n] = [s.sb("w_%s%d" % (n, j), [128, 128], F32) for j in range(NCH)]
    tms = {n: [s.sb("w_%s%d" % (n, j), [128, 64], F32) for j in range(NCH)] for n in ("Btm", "Ktm", "Vtm", "W2")}
    tm = {n: Rot([s.sb("w_%s%d" % (n, i), [128, 64], F32) for i in range(3)]) for n in ("RHS", "U")}
    KVp = [s.sb("w_KVp%d" % j, [64, 64], F32) for j in range(NCH)]
    mi = 0
    for b in range(NB):
        t0 = b * TB
        A_, G_ = Ain.get(), Gin.get()
        loads = [(A_, "rw_A", True), (G_, "rw_G", False)]
        if has_vres:
            V_ = Vin.get()
            loads.append((V_, "rw_V", True))
        for (buf, name, three) in loads:
            if b == 0:
                if three:
                    s.op("pool", lambda: nc.gpsimd.memset(buf[:, :, 0:1], 0.0), w=[buf])
                    s.dma("sp", buf[:, :, 1:HW], D[name][:, :, 0:TB], r=[D[name]], w=[buf])
                else:
                    s.op("pool", lambda: nc.gpsimd.memset(buf[:, 0:1], 0.0), w=[buf])
                    s.dma("sp", buf[:, 1:HW], D[name][:, 0:TB], r=[D[name]], w=[buf])
            else:
                if three:
                    s.dma("sp", buf[:, :, :], D[name][:, :, t0 - 1:t0 + TB], r=[D[name]], w=[buf])
                else:
                    s.dma("sp", buf[:, :], D[name][:, t0 - 1:t0 + TB], r=[D[name]], w=[buf])
        for j in range(5):
            dd = dtmp.get()
            s.op("pool", lambda: nc.gpsimd.tensor_tensor(out=dd[0:64, :], in0=A_[:, j, 0:TB], in1=A_[:, j, 1:HW], op=ALU.subtract), r=[A_], w=[dd])
            s.op("dve", lambda: nc.vector.scalar_tensor_tensor(out=pm[:, j, :], in0=dd[0:64, :], scalar=c64[:, j:j + 1], in1=A_[:, j, 1:HW],
                                                               op0=ALU.mult, op1=ALU.add), r=[dd, c64, A_], w=[(pm, j)])
        dd = dtmp.get()
        s.op("pool", lambda: nc.gpsimd.tensor_tensor(out=dd[:, :], in0=G_[:, 0:TB], in1=G_[:, 1:HW], op=ALU.subtract), r=[G_], w=[dd])
        s.op("dve", lambda: nc.vector.scalar_tensor_tensor(out=pg[:, :], in0=dd[:, :], scalar=c128[:, 0:1], in1=G_[:, 1:HW],
                                                           op0=ALU.mult, op1=ALU.add), r=[dd, c128, G_], w=[pg])
        if has_vres:
            for j in range(4):
                dd = dtmp.get()
                s.op("pool", lambda: nc.gpsimd.tensor_tensor(out=dd[:, :], in0=V_[:, j, 0:TB], in1=V_[:, j, 1:HW], op=ALU.subtract), r=[V_], w=[dd])
                s.op("dve", lambda: nc.vector.scalar_tensor_tensor(out=pv[:, j, :], in0=dd[:, :], scalar=c128[:, 1 + j:2 + j], in1=V_[:, j, 1:HW],
                                                                   op0=ALU.mult, op1=ALU.add), r=[dd, c128, V_], w=[(pv, j)])
        pr, pk, pvh, pwl, pal = (pm[:, j, :] for j in range(5))
        R5 = [(pm, j) for j in range(5)]
        yield 8.0
        s.op("act", lambda: nc.scalar.activation(out=T["t64a"][:, :], in_=pwl, func=AF.Tanh), r=[R5[3]], w=[T["t64a"]])
        p_ = pp.get()
        s.op("pe", lambda: nc.tensor.matmul(p_[0:64, 0:TB], w2[:, :], T["t64a"][:, :], start=True, stop=True), r=[w2, T["t64a"]], w=[p_])
        s.op("act", lambda: nc.scalar.activation(out=T["sg"][:, :], in_=p_[0:64, 0:TB], func=AF.Sigmoid, bias=c64[:, 5:6]), r=[p_, c64], w=[T["sg"]])
        s.op("dve", lambda: nc.vector.tensor_scalar(out=T["lw"][:, :], in0=T["sg"][:, :], scalar1=-EH, scalar2=None, op0=ALU.mult), r=[T["sg"]], w=[T["lw"]])
        p_ = pp.get()
        s.op("pe", lambda: nc.tensor.matmul(p_[0:64, 0:TB], a2[:, :], pal, start=True, stop=True), r=[a2, R5[4]], w=[p_])
        s.op("act", lambda: nc.scalar.activation(out=T["asig"][:, :], in_=p_[0:64, 0:TB], func=AF.Sigmoid, bias=c64[:, 6:7]), r=[p_, c64], w=[T["asig"]])
        dd = dtmp.get()
        s.op("act", lambda: nc.scalar.activation(out=dd[:, :], in_=pg[:, :], func=AF.Sigmoid), r=[pg], w=[dd])
        p_ = pp.get()
        s.op("pe", lambda: nc.tensor.matmul(p_[0:64, 0:TB], g2[:, :], dd[:, :], start=True, stop=True), r=[g2, dd], w=[p_])
        s.op("act", lambda: nc.scalar.copy(out=T["gT"][:, :], in_=p_[0:64, 0:TB]), r=[p_], w=[T["gT"]])
        if has_vres:
            p_ = pp.get()
            for j in range(4):
                s.op("pe", lambda: nc.tensor.matmul(p_[0:32, 0:TB], v1[:, j, :], pv[:, j, :], start=(j == 0), stop=(j == 3)), r=[v1, (pv, j)], w=[p_])
            s.op("act", lambda: nc.scalar.copy(out=t1s[:, :], in_=p_[0:32, 0:TB]), r=[p_], w=[t1s])
            p_ = pp.get()
            s.op("pe", lambda: nc.tensor.matmul(p_[0:64, 0:TB], v2[:, :], t1s[:, :], start=True, stop=True), r=[v2, t1s], w=[p_])
            s.op("act", lambda: nc.scalar.activation(out=T["t64a"][:, :], in_=p_[0:64, 0:TB], func=AF.Sigmoid, bias=c64[:, 12:13]), r=[p_, c64], w=[T["t64a"]])
            vf = vfin.get()
            s.dma("sp", vf[:], D["rw_vf"][:, t0:t0 + TB], r=[D["rw_vf"]], w=[vf])
            s.op("dve", lambda: nc.vector.tensor_tensor(out=T["t64b"][:, :], in0=vf[:, :], in1=pvh, op=ALU.subtract), r=[vf, R5[2]], w=[T["t64b"]])
            s.op("dve", lambda: nc.vector.tensor_tensor(out=T["t64b"][:, :], in0=T["t64b"][:, :], in1=T["t64a"][:, :], op=ALU.mult), r=[T["t64b"], T["t64a"]], w=[T["t64b"]])
            s.op("dve", lambda: nc.vector.tensor_tensor(out=pvh, in0=pvh, in1=T["t64b"][:, :], op=ALU.add), r=[R5[2], T["t64b"]], w=[R5[2]])
        else:
            s.dma("sp", D["o_vf"][:, t0:t0 + TB], pvh, r=[R5[2]], w=[(D["o_vf"], b)], is_output=True)
        yield 8.0
        s.op("dve", lambda: nc.vector.tensor_scalar(out=T["kk"][:, :], in0=pk, scalar1=c64[:, 7:8], scalar2=None, op0=ALU.mult), r=[R5[1], c64], w=[T["kk"]])
        s.op("act", lambda: nc.scalar.activation(out=T["t64a"][:, :], in_=T["kk"][:, :], func=AF.Square), r=[T["kk"]], w=[T["t64a"]])
        p_ = pp.get()
        s.op("pe", lambda: nc.tensor.matmul(p_[0:64, 0:TB], ones[0:64, 0:64], T["t64a"][:, :], start=True, stop=True), r=[ones, T["t64a"]], w=[p_])
        s.op("act", lambda: nc.scalar.activation(out=T["t64b"][:, :], in_=p_[0:64, 0:TB], func=AF.Sqrt), r=[p_], w=[T["t64b"]])
        s.op("dve", lambda: nc.vector.tensor_scalar(out=T["t64b"][:, :], in0=T["t64b"][:, :], scalar1=1e-12, scalar2=None, op0=ALU.max), r=[T["t64b"]], w=[T["t64b"]])
        s.op("dve", lambda: nc.vector.reciprocal(out=T["t64b"][:, :], in_=T["t64b"][:, :]), r=[T["t64b"]], w=[T["t64b"]])
        s.op("dve", lambda: nc.vector.tensor_tensor(out=T["kk"][:, :], in0=T["kk"][:, :], in1=T["t64b"][:, :], op=ALU.mult), r=[T["kk"], T["t64b"]], w=[T["kk"]])
        s.op("dve", lambda: nc.vector.tensor_scalar(out=T["t64a"][:, :], in0=T["asig"][:, :], scalar1=c64[:, 8:9], scalar2=c64[:, 13:14], op0=ALU.mult, op1=ALU.add),
             r=[T["asig"], c64], w=[T["t64a"]])
        s.op("dve", lambda: nc.vector.tensor_tensor(out=T["kmod"][:, :], in0=pk, in1=T["t64a"][:, :], op=ALU.mult), r=[R5[1], T["t64a"]], w=[T["kmod"]])
        s.op("pool", lambda: nc.gpsimd.tensor_tensor(out=T["bvec"][:, :], in0=T["kk"][:, :], in1=T["asig"][:, :], op=ALU.mult), r=[T["kk"], T["asig"]], w=[T["bvec"]])
        yield 6.0
        for j in range(NCH):
            cs_ = slice(j * 128, (j + 1) * 128)
            s.op("dve", lambda: nc.vector.tensor_tensor_scan(out=T["cum"][:, cs_], data0=ones[0:64, 0:128], data1=T["lw"][:, cs_], initial=0.0,
                                                             op0=ALU.mult, op1=ALU.add), r=[ones, T["lw"]], w=[T["cum"]])
        s.op("act", lambda: nc.scalar.activation(out=T["epos"][:, :], in_=T["cum"][:, :], func=AF.Exp), r=[T["cum"]], w=[T["epos"]])
        s.op("act", lambda: nc.scalar.activation(out=T["eneg"][:, :], in_=T["cum"][:, :], func=AF.Exp, scale=-1.0), r=[T["cum"]], w=[T["eneg"]])
        s.op("pool", lambda: nc.gpsimd.tensor_tensor(out=T["t64a"][:, :], in0=T["cum"][:, :], in1=T["lw"][:, :], op=ALU.subtract), r=[T["cum"], T["lw"]], w=[T["t64a"]])
        s.op("act", lambda: nc.scalar.activation(out=T["eprev"][:, :], in_=T["t64a"][:, :], func=AF.Exp), r=[T["t64a"]], w=[T["eprev"]])
        s.op("dve", lambda: nc.vector.tensor_tensor(out=T["Rt"][:, :], in0=pr, in1=T["epos"][:, :], op=ALU.mult), r=[R5[0], T["epos"]], w=[T["Rt"]])
        s.op("dve", lambda: nc.vector.scalar_tensor_tensor(out=T["At"][:, :], in0=T["kk"][:, :], scalar=-1.0, in1=T["eprev"][:, :], op0=ALU.mult, op1=ALU.mult),
             r=[T["kk"], T["eprev"]], w=[T["At"]])
        s.op("pool", lambda: nc.gpsimd.tensor_tensor(out=T["Bt"][:, :], in0=T["bvec"][:, :], in1=T["eneg"][:, :], op=ALU.mult), r=[T["bvec"], T["eneg"]], w=[T["Bt"]])
        s.op("dve", lambda: nc.vector.tensor_tensor(out=T["Kt"][:, :], in0=T["kmod"][:, :], in1=T["eneg"][:, :], op=ALU.mult), r=[T["kmod"], T["eneg"]], w=[T["Kt"]])
        s.op("dve", lambda: nc.vector.scalar_tensor_tensor(out=T["t64b"][:, :], in0=pr, scalar=c64[:, 9:10], in1=T["kmod"][:, :], op0=ALU.mult, op1=ALU.mult),
             r=[R5[0], c64, T["kmod"]], w=[T["t64b"]])
        p_ = pp.get()
        s.op("pe", lambda: nc.tensor.matmul(p_[0:64, 0:TB], ones[0:64, 0:64], T["t64b"][:, :], start=True, stop=True), r=[ones, T["t64b"]], w=[p_])
        s.op("dve", lambda: nc.vector.tensor_tensor(out=T["bonus"][:, :], in0=p_[0:64, 0:TB], in1=pvh, op=ALU.mult), r=[p_, R5[2]], w=[T["bonus"]])

        yield 8.0
        ch = {}

        def chunk_prep(j):
            cs_ = slice(j * 128, (j + 1) * 128)
            At, Bt, Kt, Rt = T["At"][:, cs_], T["Bt"][:, cs_], T["Kt"][:, cs_], T["Rt"][:, cs_]

            def gram(lhs, lhs_t, rhs, rhs_t, mk, dst):
                pq = pp.get()
                s.op("pe", lambda: nc.tensor.matmul(pq[:, 0:128], lhs, rhs, start=True, stop=True), r=[lhs_t, rhs_t], w=[pq])
                s.op("dve", lambda: nc.vector.tensor_tensor(out=dst[:, :], in0=pq[:, 0:128], in1=msk[:, mk, :], op=ALU.mult), r=[pq, msk], w=[dst])

            def tpose(src_, src_t, dst):
                pq = pp.get()
                s.op("pe", lambda: nc.tensor.transpose(pq[:, 0:64], src_, ident[0:64, 0:64]), r=[src_t, ident], w=[pq])
                s.op("act", lambda: nc.scalar.copy(out=dst[:, :], in_=pq[:, 0:64]), r=[pq], w=[dst])
            Pb, Qb, Zb = sqs["P"][j], sqs["Q"][j], sqs["Z"][j]
            P, Q, Z = Pb[0], Qb[0], Zb[0]
            NakT, NrbT, NrkT = sqs["NakT"][j], sqs["NrbT"][j], sqs["NrkT"][j]
            Btm, Ktm, Vtm, W2 = tms["Btm"][j], tms["Ktm"][j], tms["Vtm"][j], tms["W2"][j]
            kvp = KVp[j]
            gram(Bt, T["Bt"], At, T["At"], 0, P)
            yield
            gram(At, T["At"], Bt, T["Bt"], 1, Q)
            yield
            gram(Kt, T["Kt"], At, T["At"], 0, NakT)
            yield
            gram(Bt, T["Bt"], Rt, T["Rt"], 2, NrbT)
            yield
            gram(Kt, T["Kt"], Rt, T["Rt"], 2, NrkT)
            s.op("pool", lambda: nc.gpsimd.tensor_tensor(out=Z[:, :], in0=P[:, :], in1=msk[:, 3, :], op=ALU.add), r=[P, msk], w=[Z])
            yield
            tpose(Bt, T["Bt"], Btm)
            yield
            tpose(Kt, T["Kt"], Ktm)
            yield
            tpose(pm[:, 2, cs_], R5[2], Vtm)
            yield
            pq = pp.get()
            s.op("pe", lambda: nc.tensor.matmul(pq[:, 0:64], NakT[:, :], Vtm[:, :], start=True, stop=True), r=[NakT, Vtm], w=[pq])
            s.op("act", lambda: nc.scalar.copy(out=W2[:, :], in_=pq[:, 0:64]), r=[pq], w=[W2])
            yield
            e_ = j * 128 + 127
            pL = T["epos"][:, e_:e_ + 1]
            pq = pp.get()
            s.op("pe", lambda: nc.tensor.matmul(pq[0:64, 0:64], Ktm[:, :], Vtm[:, :], start=True, stop=True), r=[Ktm, Vtm], w=[pq])
            s.op("act", lambda: nc.scalar.activation(out=kvp[:, :], in_=pq[0:64, 0:64], func=AF.Copy, scale=pL), r=[pq, T["epos"]], w=[kvp])
            yield
            for lv in range(6):
                Qn = Qb[(lv + 1) % 2]
                pq = pp.get()
                s.op("pe", lambda: nc.tensor.matmul(pq[:, 0:128], P[:, :], Q[:, :], start=True, stop=True), r=[P, Q], w=[pq])
                s.op("act", lambda: nc.scalar.copy(out=Qn[:, :], in_=pq[:, 0:128]), r=[pq], w=[Qn])
                yield
                if lv < 5:
                    Pn = Pb[(lv + 1) % 2]
                    pq2 = pp.get()
                    s.op("pe", lambda: nc.tensor.matmul(pq2[:, 0:128], Q[:, :], P[:, :], start=True, stop=True), r=[P, Q], w=[pq2])
                    s.op("pool" if False else "dve", lambda: nc.vector.tensor_copy(out=Pn[:, :], in_=pq2[:, 0:128]), r=[pq2], w=[Pn])
                    yield
                pz = pp.get()
                s.op("pe", lambda: nc.tensor.matmul(pz[:, 0:128], Qn[:, :], Z[:, :], start=True, stop=True), r=[Qn, Z], w=[pz])
                Zn = Zb[(lv + 1) % 2] if lv < 5 else sqs["Zf"][j]
                s.op("dve", lambda: nc.vector.tensor_tensor(out=Zn[:, :], in0=pz[:, 0:128], in1=Z[:, :], op=ALU.add), r=[pz, Z], w=[Zn])
                yield
                Z = Zn
                Q = Qn
                if lv < 5:
                    P = Pn
            ch[j] = dict(Z=Z, NrbT=NrbT, NrkT=NrkT, Btm=Btm, Vtm=Vtm, W2=W2, kvp=kvp, pL=pL)

        gens = [chunk_prep(j) for j in range(NCH)]
        while gens:
            for g_ in list(gens):
                try:
                    next(g_)
                except StopIteration:
                    gens.remove(g_)
            yield 2.0

        for j in range(NCH):
            cs_ = slice(j * 128, (j + 1) * 128)
            At, Rt = T["At"][:, cs_], T["Rt"][:, cs_]
            c_ = ch[j]
            Z, NrbT, NrkT, Btm, Vtm, W2, kvp, pL = (c_[k] for k in ("Z", "NrbT", "NrkT", "Btm", "Vtm", "W2", "kvp", "pL"))
            Mo, Mn = M[mi % 2], M[(mi + 1) % 2]
            mi += 1
            s.op("dve", lambda: nc.vector.scalar_tensor_tensor(out=Mp[:, :], in0=Mo[:, :], scalar=pL, in1=kvp[:, :], op0=ALU.mult, op1=ALU.add),
                 r=[Mo, T["epos"], kvp], w=[Mp])
            p1 = pp.get()
            s.op("pe", lambda: nc.tensor.matmul(p1[:, 0:64], At, Mo[:, :], start=True, stop=True), r=[T["At"], Mo], w=[p1])
            RHS = tm["RHS"].get()
            s.op("dve", lambda: nc.vector.tensor_tensor(out=RHS[:, :], in0=p1[:, 0:64], in1=W2[:, :], op=ALU.add), r=[p1, W2], w=[RHS])
            p2 = pp.get()
            s.op("pe", lambda: nc.tensor.matmul(p2[:, 0:64], Z[:, :], RHS[:, :], start=True, stop=True), r=[Z, RHS], w=[p2])
            U = tm["U"].get()
            s.op("act", lambda: nc.scalar.copy(out=U[:, :], in_=p2[:, 0:64]), r=[p2], w=[U])
            p3 = pp.get()
            s.op("pe", lambda: nc.tensor.matmul(p3[0:64, 0:64], Btm[:, :], U[:, :], start=True, stop=True), r=[Btm, U], w=[p3])
            s.op("dve", lambda: nc.vector.scalar_tensor_tensor(out=Mn[:, :], in0=p3[0:64, 0:64], scalar=pL, in1=Mp[:, :], op0=ALU.mult, op1=ALU.add),
                 r=[p3, T["epos"], Mp], w=[Mn])
            p4 = pp.get()
            s.op("pe", lambda: nc.tensor.matmul(p4[0:64, 0:128], Mo[:, :], Rt, start=True, stop=False), r=[Mo, T["Rt"]], w=[p4])
            s.op("pe", lambda: nc.tensor.matmul(p4[0:64, 0:128], U[:, :], NrbT[:, :], start=False, stop=False), r=[U, NrbT], w=[p4])
            s.op("pe", lambda: nc.tensor.matmul(p4[0:64, 0:128], Vtm[:, :], NrkT[:, :], start=False, stop=True), r=[Vtm, NrkT], w=[p4])
            s.op("act", lambda: nc.scalar.copy(out=T["yT"][:, cs_], in_=p4[0:64, 0:128]), r=[p4], w=[T["yT"]])
            yield 5.0

        y = T["yT"]
        s.op("act", lambda: nc.scalar.activation(out=T["t64a"][:, :], in_=y[:, :], func=AF.Square), r=[y], w=[T["t64a"]])
        pm_ = pp.get()
        s.op("pe", lambda: nc.tensor.matmul(pm_[0:64, 0:TB], ones[0:64, 0:64], y[:, :], start=True, stop=True), r=[ones, y], w=[pm_])
        pe_ = pp.get()
        s.op("pe", lambda: nc.tensor.matmul(pe_[0:64, 0:TB], ones[0:64, 0:64], T["t64a"][:, :], start=True, stop=True), r=[ones, T["t64a"]], w=[pe_])
        mean = T["t64b"]
        s.op("act", lambda: nc.scalar.activation(out=mean[:, :], in_=pm_[0:64, 0:TB], func=AF.Copy, scale=1.0 / 64), r=[pm_], w=[mean])
        s.op("dve", lambda: nc.vector.tensor_tensor(out=T["t64a"][:, :], in0=mean[:, :], in1=mean[:, :], op=ALU.mult), r=[mean], w=[T["t64a"]])
        s.op("dve", lambda: nc.vector.scalar_tensor_tensor(out=T["t64a"][:, :], in0=pe_[0:64, 0:TB], scalar=1.0 / 64, in1=T["t64a"][:, :], op0=ALU.mult, op1=ALU.subtract),
             r=[pe_, T["t64a"]], w=[T["t64a"]])
        s.op("act", lambda: nc.scalar.activation(out=T["t64a"][:, :], in_=T["t64a"][:, :], func=AF.Sqrt, bias=RWKV_GN_EPS), r=[T["t64a"]], w=[T["t64a"]])
        s.op("dve", lambda: nc.vector.reciprocal(out=T["t64a"][:, :], in_=T["t64a"][:, :]), r=[T["t64a"]], w=[T["t64a"]])
        s.op("dve", lambda: nc.vector.tensor_tensor(out=mean[:, :], in0=y[:, :], in1=mean[:, :], op=ALU.subtract), r=[y, mean], w=[mean])
        s.op("dve", lambda: nc.vector.tensor_tensor(out=mean[:, :], in0=mean[:, :], in1=T["t64a"][:, :], op=ALU.mult), r=[mean, T["t64a"]], w=[mean])
        s.op("dve", lambda: nc.vector.tensor_scalar(out=mean[:, :], in0=mean[:, :], scalar1=c64[:, 10:11], scalar2=c64[:, 11:12], op0=ALU.mult, op1=ALU.add),
             r=[mean, c64], w=[mean])
        s.op("pool", lambda: nc.gpsimd.tensor_tensor(out=mean[:, :], in0=mean[:, :], in1=T["bonus"][:, :], op=ALU.add), r=[mean, T["bonus"]], w=[mean])
        o = ob.get()
        s.op("dve", lambda: nc.vector.tensor_tensor(out=o[:, :], in0=mean[:, :], in1=T["gT"][:, :], op=ALU.mult), r=[mean, T["gT"]], w=[o])
        s.dma("sp", D["o_rwkv"][:, t0:t0 + TB], o[:, :], r=[o], w=[(D["o_rwkv"], b)], is_output=True)
        yield 8.0


def build_B(cfg, has_vres, parts=("ret", "ssd", "rwkv", "mla")):
    S = cfg.S
    TB = min(512, S)
    Sq = S // 2
    nc = bass.Bass("TRN2", target_bir_lowering=False)
    es = contextlib.ExitStack()
    with es:
        s = Sched(nc, es)
        IN, OUT = "ExternalInput", "ExternalOutput"
        D = {}

        def di(name, shape, dt=F32):
            D[name] = s.dram(name, shape, dt, IN)

        def do(name, shape, dt=F32):
            D[name] = s.dram(name, shape, dt, OUT)
        di("ident", [128, 128])
        for n in ("ret_q", "ret_k", "ret_v"):
            di(n, [128, S])
        di("ret_mask", [128, 128]); di("ret_qdec", [128, 128]); di("ret_col", [128, 2])
        do("o_ret", [128, S])
        di("ssd_x", [64, S]); di("ssd_B", [128, S]); di("ssd_C", [128, S]); di("ssd_dt", [1, S])
        di("ssd_cwx", [64, 5]); di("ssd_cwB", [128, 5]); di("ssd_cwC", [128, 5]); di("ssd_scal", [128, 4]); di("ssd_negmask", [128, 128])
        do("o_ssd", [64, S])
        di("mla_kn", [128, S], BF16); di("mla_kr", [64, S], BF16); di("mla_qn", [128, Sq], BF16); di("mla_qr", [64, Sq], BF16)
        di("mla_v", [S, 128], BF16); di("mla_mask", [128, 256])
        do("o_mla", [Sq, 128])
        di("rw_A", [64, 5, S]); di("rw_G", [128, S]); di("rw_c64", [64, 16]); di("rw_c128", [128, 5])
        di("rw_w2", [64, 64]); di("rw_a2", [64, 64]); di("rw_g2", [128, 64]); di("rw_masks", [128, 4, 128])
        if has_vres:
            di("rw_V", [128, 4, S]); di("rw_vf", [64, S]); di("rw_v1", [128, 4, 32]); di("rw_v2", [32, 64])
        else:
            do("o_vf", [64, S])
        do("o_rwkv", [64, S])
        cn = mk_consts(s, nc)
        cn["ident"] = s.sb("ident_t", [128, 128], F32)
        s.dma("sp", cn["ident"][:], D["ident"][:], r=[D["ident"]], w=[cn["ident"]])
        ptiles = [s.ps("ps%d" % i, [128, 512]) for i in range(8)]
        pp = PsumPool(s, 0, tiles=ptiles[0:5])
        g2 = None
        if "mla" in parts:
            g2 = mixer_mla(s, nc, cn, S, D, ptiles[5:7], ptiles[7])
            next(g2)

        def stream1():
            if "ret" in parts:
                s.push_scope()
                yield from mixer_ret(s, nc, cn, pp, S, TB, D)
                s.pop_scope()
            if "ssd" in parts:
                s.push_scope()
                yield from mixer_ssd(s, nc, cn, pp, S, TB, D)
                s.pop_scope()
            if "rwkv" in parts:
                s.push_scope()
                yield from mixer_rwkv(s, nc, cn, pp, S, TB, D, has_vres)
                s.pop_scope()
        NSC = S // 128
        T1 = (4.5 * NSC if "ret" in parts else 0) + (12.0 * NSC if "ssd" in parts else 0) + (30.0 * NSC if "rwkv" in parts else 0)
        T2 = 3.2 * (NSC // 2) * (NSC // 2 + 1) / 4.0
        ratio = (T2 / T1) if T1 > 0 else 1e9
        a1 = a2 = 0.0
        for c1 in stream1():
            a1 += c1
            while g2 is not None and a2 < a1 * ratio:
                try:
                    a2 += next(g2)
                except StopIteration:
                    g2 = None
        while g2 is not None:
            try:
                next(g2)
            except StopIteration:
                g2 = None
        s.finish()
        print("build_B instrs", s.n_instr)
    return nc


def host_B_inputs(inp, l, A, d, cfg, v_first):
    S = cfg.S
    m = {}
    m["ident"] = np.eye(128, dtype=np.float32)
    hr = d // 2
    m["ret_q"], m["ret_k"], m["ret_v"] = A["rq"][hr], A["rk"][hr], A["rv"][hr]
    m.update(ret_consts(hr))
    sx = A["sx"].reshape(1024, S)
    g = d // 4
    m["ssd_x"] = sx[d * 64:(d + 1) * 64]
    m["ssd_B"] = sx[512 + g * 128:512 + (g + 1) * 128]
    m["ssd_C"] = sx[768 + g * 128:768 + (g + 1) * 128]
    m["ssd_dt"] = A["sdt"][d:d + 1]
    cw, cb = inp["ssm_conv_w"][l], inp["ssm_conv_b"][l]

    def cwp(ch):
        return np.concatenate([cw[:, ch].T, cb[ch][:, None]], axis=1).astype(np.float32)
    m["ssd_cwx"] = cwp(np.arange(d * 64, (d + 1) * 64))
    m["ssd_cwB"] = cwp(512 + g * 128 + np.arange(128))
    m["ssd_cwC"] = cwp(768 + g * 128 + np.arange(128))
    sc = np.zeros((128, 4), np.float32)
    sc[:, 0] = inp["ssm_dt_bias"][l][d]
    sc[:, 1] = inp["ssm_a_log"][l][d]
    sc[:, 2] = inp["ssm_d"][l][d]
    m["ssd_scal"] = sc
    m.update(ssd_consts())
    hm, par = d // 2, d % 2
    NKB = S // 128
    qsel = np.concatenate([np.arange(b * 128, (b + 1) * 128) for b in range(par, NKB, 2)])
    m["mla_kn"], m["mla_kr"] = A["mkn"][hm], A["mkr"][hm]
    m["mla_qn"], m["mla_qr"] = A["mqn"][hm][:, qsel], A["mqr"][hm][:, qsel]
    m["mla_v"] = A["mv"][:, hm * 128:(hm + 1) * 128]
    m["mla_mask"] = mla_masks(par)
    rwf = A["rw"].reshape(1792, S)
    hd = np.arange(d * 64, (d + 1) * 64)
    m["rw_A"] = np.stack([rwf[hd], rwf[512 + hd], rwf[1024 + hd], rwf[1536:1600], rwf[1600:1664]], axis=1)
    m["rw_G"] = rwf[1664:1792]
    mu = inp["rwkv_mu"][l]
    c64 = np.zeros((64, 16), np.float32)
    c64[:, 0], c64[:, 1], c64[:, 2], c64[:, 3], c64[:, 4] = mu[hd], mu[512 + hd], mu[1024 + hd], mu[1536:1600], mu[1600:1664]
    c64[:, 5], c64[:, 6] = inp["rwkv_w0"][l][hd], inp["rwkv_a0"][l][hd]
    c64[:, 7], c64[:, 8] = inp["rwkv_k_k"][l][hd], inp["rwkv_k_a"][l][hd]
    c64[:, 9] = inp["rwkv_r_k"][l][d]
    c64[:, 10], c64[:, 11] = inp["rwkv_ln_w"][l][hd], inp["rwkv_ln_b"][l][hd]
    if l > 0:
        c64[:, 12] = inp["rwkv_v0"][l - 1][hd]
    m["rw_c64"] = c64
    c128 = np.zeros((128, 5), np.float32)
    c128[:, 0] = mu[1664:1792]
    c128[:, 1:5] = mu[1024:1536].reshape(4, 128).T
    m["rw_c128"] = c128
    m["rw_w2"], m["rw_a2"], m["rw_g2"] = inp["rwkv_w2"][l][:, hd], inp["rwkv_a2"][l][:, hd], inp["rwkv_g2"][l][:, hd]
    m["rw_masks"] = rwkv_masks()
    if l > 0:
        m["rw_V"] = rwf[1024:1536].reshape(4, 128, S).transpose(1, 0, 2)
        m["rw_vf"] = v_first[d]
        m["rw_v1"] = fm(inp["rwkv_v1"][l - 1])
        m["rw_v2"] = inp["rwkv_v2"][l - 1][:, hd]
    return {k: np.ascontiguousarray(v) for k, v in m.items()}


WB_COLS = 8192
DEBUG_SKIP_WDMA = False


def build_C(cfg, debug=False):
    TPC, NT, NP = cfg.TPC, cfg.NT, cfg.NP
    nc = bass.Bass("TRN2", target_bir_lowering=False)
    es = contextlib.ExitStack()
    with es:
        s = Sched(nc, es)
        IN, OUT = "ExternalInput", "ExternalOutput"
        xT = s.dram("xT", [128, KC, TPC], F32, IN)
        oa = s.dram("oa", [4, 128, TPC], F32, IN)
        ob = s.dram("ob", [4, 128, TPC], F32, IN)
        oc = s.dram("oc", [4, 128, TPC], F32, IN)
        od = s.dram("od", [4, 128, TPC], F32, IN)
        cw = s.dram("cw", [128, 40], F32, IN)
        wC1 = s.dram("wC1", [128, KC, 1024], F32, IN)
        wG = s.dram("wG", [128, KC, 8192], F32, IN)
        wBr = s.dram("wBr", [128, 16, 2048], F32, IN)
        wOut = s.dram("wOut", [128, KC, 2048], F32, IN)
        wGU = s.dram("wGU", [128, KC, 2 * D_FF], F32, IN)
        wDn = s.dram("wDn", [128, FFT, 2048], F32, IN)
        o_xT = s.dram("o_xT", [128, KC, TPC], F32, OUT)

        cn = mk_consts(s, nc)
        pp = PsumPool(s, 8)
        cw_t = s.sb("cw_t", [128, 40], F32)
        s.dma("sp", cw_t[:], cw[:], r=[cw], w=[cw_t])
        x_res = s.sb("x_res", [128, KC, TPC], F32)
        for c in range(KC):
            s.dma("sp", x_res[:, c, :], xT[:, c, :], r=[xT], w=[(x_res, c)])
        hT = s.sb("hT", [128, KC, NT], BF16)
        wbs = Rot([s.sb("wbuf%d" % i, [128, WB_COLS], BF16) for i in range(3)])
        sqrot = Rot([s.sb("sq%d" % i, [128, NT], F32) for i in range(2)])
        rstd_t = s.sb("rstd_t", [128, NT], F32)
        rtmp = s.sb("rtmp", [128, NT], F32)

        def load_w(dram_t, c0, ncols, nk, kbase=0):
            wb = wbs.get()
            view = wb[:, 0:nk * ncols].rearrange("p (c n) -> p c n", c=nk)
            for k0 in range(0, nk, 8):
                k1 = min(nk, k0 + 8)
                if DEBUG_SKIP_WDMA and load_w.count >= 3:
                    continue
                s.dma("pool", view[:, k0:k1, :], dram_t[:, kbase + k0:kbase + k1, c0:c0 + ncols], r=[dram_t], w=[wb])
            load_w.count += 1
            return wb, view
        load_w.count = 0

        for p in range(NP):
            t0 = p * NT
            tsl = slice(t0, t0 + NT)

            def x_src(c):
                return x_res[:, c, tsl], [(x_res, c)]
            compute_hT(s, nc, cn, pp, x_src, Tn("cw_t", cw_t.t[:, 0:16]), hT, NT, sqrot, rstd_t, rtmp, True)

            s.push_scope()
            on_bf = s.sb("on_bf%d" % p, [128, 16, NT], BF16)
            merged = s.sb("merged%d" % p, [128, KC, NT], BF16)
            wbr = Rot([s.sb("wbr%d_%d" % (i, p), [128, 2048], BF16) for i in range(2)])
            ft = {n: s.sb("c_%s%d" % (n, p), [128, NT], F32) for n in ("x0", "x1", "x2", "x3", "sg", "mean", "var", "t1")}
            ft["gate"], ft["t2"], ft["macc"] = ft["sg"], ft["mean"], ft["var"]
            xin = [ft["x0"], ft["x1"], ft["x2"], ft["x3"]]
            wcur = {}

            def proj(col0, ps):
                wb1, w1v = wcur["w"]
                for c in range(KC):
                    s.op("pe", lambda: nc.tensor.matmul(ps[:, 0:NT], w1v[:, c, col0:col0 + 128], hT[:, c, 0:NT], start=(c == 0), stop=(c == KC - 1)),
                         r=[wb1, (hT, c)], w=[ps])
            wcur["w"] = load_w(wC1, 0, 512, KC)
            for h in range(4):
                x = xin[h % 2]
                s.dma("sp", x[:, :], oa[h, :, tsl], r=[oa], w=[x])
                psg = pp.get()
                proj(h * 128, psg)
                s.op("act", lambda: nc.scalar.activation(out=ft["sg"][:, :], in_=psg[:, 0:NT], func=AF.Silu), r=[psg], w=[ft["sg"]])
                sq = sqrot.get()
                s.op("act", lambda: nc.scalar.activation(out=sq[:, 0:NT], in_=x[:, :], func=AF.Square), r=[x], w=[sq])
                pm_ = pp.get()
                s.op("pe", lambda: nc.tensor.matmul(pm_[:, 0:NT], cn["ones_f"][:, :], x[:, :], start=True, stop=True), r=[cn["ones_f"], x], w=[pm_])
                pe_ = pp.get()
                s.op("pe", lambda: nc.tensor.matmul(pe_[:, 0:NT], cn["ones_f"][:, :], sq[:, 0:NT], start=True, stop=True), r=[cn["ones_f"], sq], w=[pe_])
                s.op("act", lambda: nc.scalar.activation(out=ft["mean"][:, :], in_=pm_[:, 0:NT], func=AF.Copy, scale=1.0 / 128), r=[pm_], w=[ft["mean"]])
                s.op("dve", lambda: nc.vector.tensor_tensor(out=ft["var"][:, :], in0=ft["mean"][:, :], in1=ft["mean"][:, :], op=ALU.mult), r=[ft["mean"]], w=[ft["var"]])
                s.op("dve", lambda: nc.vector.scalar_tensor_tensor(out=ft["var"][:, :], in0=pe_[:, 0:NT], scalar=1.0 / 128, in1=ft["var"][:, :],
                                                                   op0=ALU.mult, op1=ALU.subtract), r=[pe_, ft["var"]], w=[ft["var"]])
                s.op("act", lambda: nc.scalar.activation(out=ft["var"][:, :], in_=ft["var"][:, :], func=AF.Sqrt, bias=GN_EPS), r=[ft["var"]], w=[ft["var"]])
                s.op("dve", lambda: nc.vector.reciprocal(out=ft["var"][:, :], in_=ft["var"][:, :]), r=[ft["var"]], w=[ft["var"]])
                s.op("dve", lambda: nc.vector.tensor_tensor(out=ft["t1"][:, :], in0=x[:, :], in1=ft["mean"][:, :], op=ALU.subtract), r=[x, ft["mean"]], w=[ft["t1"]])
                s.op("dve", lambda: nc.vector.scalar_tensor_tensor(out=ft["t1"][:, :], in0=ft["t1"][:, :], scalar=cw_t[:, 32 + h:33 + h], in1=ft["var"][:, :],
                                                                   op0=ALU.mult, op1=ALU.mult), r=[ft["t1"], cw_t, ft["var"]], w=[ft["t1"]])
                s.op("dve", lambda: nc.vector.tensor_tensor(out=on_bf[:, h, :], in0=ft["t1"][:, :], in1=ft["sg"][:, :], op=ALU.mult),
                     r=[ft["t1"], ft["sg"]], w=[(on_bf, h)])
            wcur["w"] = load_w(wC1, 512, 512, KC)
            for g in range(2):
                for j in range(2):
                    i = g * 2 + j
                    x = xin[i]
                    s.dma("sp", x[:, :], ob[i, :, tsl], r=[ob], w=[x])
                    psz = pp.get()
                    proj(i * 128, psz)
                    s.op("act", lambda: nc.scalar.activation(out=ft["sg"][:, :], in_=psz[:, 0:NT], func=AF.Silu), r=[psz], w=[ft["sg"]])
                    s.op("dve", lambda: nc.vector.tensor_tensor(out=x[:, :], in0=x[:, :], in1=ft["sg"][:, :], op=ALU.mult), r=[x, ft["sg"]], w=[x])
                pss = pp.get()
                for j in range(2):
                    x = xin[g * 2 + j]
                    sq = sqrot.get()
                    s.op("act", lambda: nc.scalar.activation(out=sq[:, 0:NT], in_=x[:, :], func=AF.Square), r=[x], w=[sq])
                    s.op("pe", lambda: nc.tensor.matmul(pss[:, 0:NT], cn["ones_f"][:, :], sq[:, 0:NT], start=(j == 0), stop=(j == 1)),
                         r=[cn["ones_f"], sq], w=[pss])
                rstd_from_psum(s, nc, pss, 128, NT, 1.0 / 256, RMS_EPS, ft["var"], ft["mean"])
                for j in range(2):
                    i = g * 2 + j
                    x = xin[i]
                    s.op("dve", lambda: nc.vector.scalar_tensor_tensor(out=on_bf[:, 4 + i, :], in0=x[:, :], scalar=cw_t[:, 36 + i:37 + i], in1=ft["var"][:, :],
                                                                       op0=ALU.mult, op1=ALU.mult), r=[x, cw_t, ft["var"]], w=[(on_bf, 4 + i)])
            for i in range(4):
                s.dma("pool", on_bf[:, 8 + i, :], oc[i, :, tsl], r=[oc], w=[(on_bf, 8 + i)])
                s.dma("pool", on_bf[:, 12 + i, :], od[i, :, tsl], r=[od], w=[(on_bf, 12 + i)])
            for dt in range(16):
                wbg, wgv = load_w(wG, dt * 512, 512, KC)
                wb_ = wbr.get()
                s.dma("pool", wb_[:, :], wBr[:, dt, :], r=[wBr], w=[wb_])
                for n in range(4):
                    psg = pp.get()
                    for c in range(KC):
                        s.op("pe", lambda: nc.tensor.matmul(psg[:, 0:NT], wgv[:, c, n * 128:(n + 1) * 128], hT[:, c, 0:NT], start=(c == 0), stop=(c == KC - 1)),
                             r=[wbg, (hT, c)], w=[psg])
                    psb = pp.get()
                    for kc in range(4):
                        s.op("pe", lambda: nc.tensor.matmul(psb[:, 0:NT], wb_[:, (n * 4 + kc) * 128:(n * 4 + kc + 1) * 128], on_bf[:, n * 4 + kc, :],
                                                            start=(kc == 0), stop=(kc == 3)), r=[wb_, (on_bf, n * 4 + kc)], w=[psb])
                    s.op("act", lambda: nc.scalar.activation(out=ft["gate"][:, :], in_=psg[:, 0:NT], func=AF.Sigmoid), r=[psg], w=[ft["gate"]])
                    if n == 0:
                        s.op("dve", lambda: nc.vector.tensor_tensor(out=ft["macc"][:, :], in0=psb[:, 0:NT], in1=ft["gate"][:, :], op=ALU.mult),
                             r=[psb, ft["gate"]], w=[ft["macc"]])
                    else:
                        s.op("dve", lambda: nc.vector.tensor_tensor(out=ft["t2"][:, :], in0=psb[:, 0:NT], in1=ft["gate"][:, :], op=ALU.mult),
                             r=[psb, ft["gate"]], w=[ft["t2"]])
                        if n < 3:
                            s.op("dve", lambda: nc.vector.tensor_tensor(out=ft["macc"][:, :], in0=ft["macc"][:, :], in1=ft["t2"][:, :], op=ALU.add),
                                 r=[ft["macc"], ft["t2"]], w=[ft["macc"]])
                        else:
                            s.op("dve", lambda: nc.vector.tensor_tensor(out=merged[:, dt, :], in0=ft["macc"][:, :], in1=ft["t2"][:, :], op=ALU.add),
                                 r=[ft["macc"], ft["t2"]], w=[(merged, dt)])
            for g4 in range(4):
                wbo, wov = load_w(wOut, g4 * 512, 512, KC)
                for j in range(4):
                    d2 = g4 * 4 + j
                    ps = pp.get()
                    for c in range(KC):
                        s.op("pe", lambda: nc.tensor.matmul(ps[:, 0:NT], wov[:, c, j * 128:(j + 1) * 128], merged[:, c, :], start=(c == 0), stop=(c == KC - 1)),
                             r=[wbo, (merged, c)], w=[ps])
                    s.op("dve", lambda: nc.vector.tensor_tensor(out=x_res[:, d2, tsl], in0=x_res[:, d2, tsl], in1=ps[:, 0:NT], op=ALU.add),
                         r=[(x_res, d2), ps], w=[(x_res, d2)])
            if debug and p == 0:
                dbg1 = s.dram("dbg_on", [128, 16, NT], BF16, OUT)
                s.dma("sp", dbg1[:], on_bf[:], r=[on_bf], w=[dbg1], is_output=True)
                dbg2 = s.dram("dbg_mg", [128, 16, NT], BF16, OUT)
                s.dma("sp", dbg2[:], merged[:], r=[merged], w=[dbg2], is_output=True)
                dbg3 = s.dram("dbg_xm", [128, 16, NT], F32, OUT)
                for c in range(KC):
                    s.dma("sp", dbg3[:, c, :], x_res[:, c, tsl], r=[(x_res, c)], w=[(dbg3, c)], is_output=True)
            s.pop_scope()

            s.push_scope()
            compute_hT(s, nc, cn, pp, x_src, Tn("cw_t", cw_t.t[:, 16:32]), hT, NT, sqrot, rstd_t, rtmp, True)
            aT = s.sb("aT%d" % p, [128, FFT, NT], BF16)
            sgr = Rot([s.sb("f_sg%d_%d" % (i, p), [128, NT], F32) for i in range(2)])
            for gq in range(FFT // 2):
                wbf, wfv = load_w(wGU, gq * 512, 512, KC)
                for jj in range(2):
                    j = gq * 2 + jj
                    psg = pp.get()
                    psu = pp.get()
                    for (ps_, co) in ((psg, jj * 256), (psu, jj * 256 + 128)):
                        for c in range(KC):
                            s.op("pe", lambda: nc.tensor.matmul(ps_[:, 0:NT], wfv[:, c, co:co + 128], hT[:, c, 0:NT], start=(c == 0), stop=(c == KC - 1)),
                                 r=[wbf, (hT, c)], w=[ps_])
                    sg = sgr.get()
                    s.op("act", lambda: nc.scalar.activation(out=sg[:, :], in_=psg[:, 0:NT], func=AF.Silu), r=[psg], w=[sg])
                    s.op("dve", lambda: nc.vector.tensor_tensor(out=aT[:, j, :], in0=psu[:, 0:NT], in1=sg[:, :], op=ALU.mult), r=[psu, sg], w=[(aT, j)])
            HK = FFT // 2
            for g8 in range(8):
                wbd0, wdv0 = load_w(wDn, g8 * 256, 256, HK, 0)
                wbd1, wdv1 = load_w(wDn, g8 * 256, 256, HK, HK)
                for j in range(2):
                    d2 = g8 * 2 + j
                    ps = pp.get()
                    for c in range(FFT):
                        wbd, wdv, cc = (wbd0, wdv0, c) if c < HK else (wbd1, wdv1, c - HK)
                        s.op("pe", lambda: nc.tensor.matmul(ps[:, 0:NT], wdv[:, cc, j * 128:(j + 1) * 128], aT[:, c, :], start=(c == 0), stop=(c == FFT - 1)),
                             r=[wbd, (aT, c)], w=[ps])
                    s.op("dve", lambda: nc.vector.tensor_tensor(out=x_res[:, d2, tsl], in0=x_res[:, d2, tsl], in1=ps[:, 0:NT], op=ALU.add),
                         r=[(x_res, d2), ps], w=[(x_res, d2)])
            s.pop_scope()
        for c in range(KC):
            s.dma("sp", o_xT[:, c, :], x_res[:, c, :], r=[(x_res, c)], w=[(o_xT, c)], is_output=True)
        s.finish()
        print("build_C instrs", s.n_instr)
    return nc


def host_C_weights(inp, l):
    w = {}
    cwm = np.zeros((128, 40), np.float32)
    cwm[:, 0:16] = colvec(inp["norm1_w"][l])
    cwm[:, 16:32] = colvec(inp["norm2_w"][l])
    cwm[:, 32:36] = colvec(inp["ret_gn_w"][l])
    cwm[:, 36:40] = colvec(inp["ssm_norm_w"][l])
    w["cw"] = cwm
    win = inp["w_in"][l]
    w["wC1"] = fm(win[:, 1536:2560])
    gcols = np.concatenate([6216 + n * 2048 + dt * 128 + np.arange(128) for dt in range(16) for n in range(4)])
    w["wG"] = fm(win[:, gcols])
    wb = inp["w_branch"][l]
    t = wb.reshape(4, 4, 128, 16, 128)
    w["wBr"] = np.ascontiguousarray(t.transpose(2, 3, 0, 1, 4).reshape(128, 16, 2048))
    w["wOut"] = fm(inp["w_out"][l])
    gu = inp["ffn_w_gu"][l]
    fcols = np.concatenate([np.concatenate([j * 128 + np.arange(128), D_FF + j * 128 + np.arange(128)]) for j in range(FFT)])
    w["wGU"] = fm(gu[:, fcols])
    w["wDn"] = fm(inp["ffn_w_down"][l])
    return w


_NC_CACHE = {}


def _get_nc(kind, cfg, *args):
    key = (kind, cfg.S, cfg.depth) + tuple(args)
    if key not in _NC_CACHE:
        if kind == "A":
            _NC_CACHE[key] = build_A(cfg)
        elif kind == "B":
            _NC_CACHE[key] = build_B(cfg, *args)
        else:
            _NC_CACHE[key] = build_C(cfg)
    return _NC_CACHE[key]


def kernel(**inputs):
    inp = {k: np.asarray(v) for k, v in inputs.items()}
    x = inp["x"][0]
    S = x.shape[0]
    depth = inp["norm1_w"].shape[0]
    cfg = Cfg(S, depth)
    TPC = cfg.TPC
    sls = [slice(c * TPC, (c + 1) * TPC) for c in range(NCORES)]
    xT = [to_fm_tokens(x[sl].astype(np.float32)) for sl in sls]
    pos = [np.ascontiguousarray(inp["positions"][:, sl]).astype(np.int32) for sl in sls]
    v_first = None
    for l in range(depth):
        wA = host_A_weights(inp, l)
        resA = run_spmd(_get_nc("A", cfg), [dict(wA, xT=xT[c], pos=pos[c]) for c in range(NCORES)])
        A = {}
        for k_, n_ in (("rq", "o_rq"), ("rk", "o_rk"), ("rv", "o_rv"), ("sx", "o_sx"), ("sdt", "o_sdt"), ("rw", "o_rw"),
                       ("mqn", "o_mqn"), ("mqr", "o_mqr"), ("mkn", "o_mkn"), ("mkr", "o_mkr")):
            A[k_] = np.concatenate([np.asarray(r[n_]) for r in resA], axis=-1)
        A["mv"] = np.concatenate([np.asarray(r["o_mv"]) for r in resA], axis=0)
        del resA
        resB = run_spmd(_get_nc("B", cfg, l > 0), [host_B_inputs(inp, l, A, d, cfg, v_first) for d in range(NCORES)])
        del A
        if l == 0:
            v_first = [np.asarray(resB[d]["o_vf"]) for d in range(NCORES)]
        oa = np.stack([np.asarray(resB[2 * h]["o_ret"]) for h in range(4)])
        ob = np.concatenate([np.asarray(resB[d]["o_ssd"]) for d in range(NCORES)], axis=0).reshape(4, 128, S)
        od = np.concatenate([np.asarray(resB[d]["o_rwkv"]) for d in range(NCORES)], axis=0).reshape(4, 128, S)
        oc = np.zeros((4, 128, S), np.float32)
        for h in range(4):
            t = np.zeros((S // 128, 128, 128), np.float32)
            t[0::2] = np.asarray(resB[2 * h]["o_mla"]).reshape(-1, 128, 128)
            t[1::2] = np.asarray(resB[2 * h + 1]["o_mla"]).reshape(-1, 128, 128)
            oc[h] = t.reshape(S, 128).T
        del resB
        wC = host_C_weights(inp, l)
        resC = run_spmd(_get_nc("C", cfg), [dict(wC, xT=xT[c], oa=np.ascontiguousarray(oa[:, :, sls[c]]), ob=np.ascontiguousarray(ob[:, :, sls[c]]),
                                                oc=np.ascontiguousarray(oc[:, :, sls[c]]), od=np.ascontiguousarray(od[:, :, sls[c]]))
                                           for c in range(NCORES)])
        xT = [np.asarray(r["o_xT"]) for r in resC]
        del resC, wC, wA
    out = np.concatenate([t.transpose(1, 0, 2).reshape(D_MODEL, TPC).T for t in xT], axis=0)
    return np.ascontiguousarray(out[None]).astype(np.float32)
```

```python
import contextlib
import math
import numpy as np
import ml_dtypes
import concourse.bass as bass
import concourse.mybir as mybir
from concourse.bass_utils import run_bass_kernel_spmd

F32 = mybir.dt.float32
BF16 = mybir.dt.bfloat16
I32 = mybir.dt.int32
AF = mybir.ActivationFunctionType
ALU = mybir.AluOpType

NCORES = 8
D_MODEL = 2048
KC = D_MODEL // 128
D_FF = 5632
FFT = D_FF // 128
RMS_EPS = 1e-6
GN_EPS = 1e-5
RWKV_GN_EPS = 64e-5
N_IN = 14408


class Cfg:
    def __init__(self, S=8192, depth=4):
        self.S = S
        self.depth = depth
        self.TPC = S // NCORES
        self.NT = min(512, self.TPC)
        self.NP = self.TPC // self.NT


class Tn:
    def __init__(self, name, t):
        self.name = name
        self.t = t

    def __getitem__(self, k):
        return self.t[k]


class Sched:
    ND = 6
    SAME = {"pe": False, "act": True, "dve": True, "pool": True}

    def __init__(self, nc, es):
        self.nc = nc
        self.es = es
        self.eng = dict(pe=nc.tensor, act=nc.scalar, dve=nc.vector, pool=nc.gpsimd, sp=nc.sync)
        self.csem = {e: es.enter_context(nc.semaphore("c_" + e)) for e in ("pe", "act", "dve", "pool")}
        self.ccnt = {e: 0 for e in self.csem}
        self.waited = {e: {} for e in self.eng}
        self.dsem = {q: [es.enter_context(nc.semaphore("d_%s%d" % (q, i))) for i in range(self.ND)]
                     for q in ("sp", "pool", "act")}
        self.dcnt = {q: [0] * self.ND for q in self.dsem}
        self.drr = {q: 0 for q in self.dsem}
        self.state = {}
        self.scopes = []
        self.out_tokens = []
        self.n_instr = 0

    def sb(self, name, shape, dtype):
        es = self.scopes[-1] if self.scopes else self.es
        return Tn(name, es.enter_context(self.nc.sbuf_tensor(name, list(shape), dtype)))

    def push_scope(self):
        self.scopes.append(contextlib.ExitStack())

    def pop_scope(self):
        self.barrier()
        self.scopes.pop().close()

    def barrier(self):
        toks = [("c_" + e, self.csem[e], self.ccnt[e], "x") for e in self.csem if self.ccnt[e] > 0]
        for q in self.dsem:
            for i in range(self.ND):
                if self.dcnt[q][i] > 0:
                    toks.append(("d_%s%d" % (q, i), self.dsem[q][i], self.dcnt[q][i] * 16, "dma"))
        for e in self.eng:
            self._emit_waits(e, toks)

    def ps(self, name, shape, dtype=F32):
        return Tn(name, self.es.enter_context(self.nc.psum_tensor(name, list(shape), dtype)))

    def dram(self, name, shape, dtype, kind):
        return Tn(name, self.nc.dram_tensor(name, list(shape), dtype, kind=kind).ap())

    @staticmethod
    def _rk(r):
        if isinstance(r, tuple):
            return (r[0].name, r[1])
        return (r.name, "*")

    def _conf(self, key):
        name, k = key
        if k == "*":
            return [kk for kk in self.state if kk[0] == name]
        out = []
        if (name, k) in self.state:
            out.append((name, k))
        if (name, "*") in self.state:
            out.append((name, "*"))
        return out

    def _collect(self, reads, writes):
        toks = []
        for r in reads:
            for kk in self._conf(self._rk(r)):
                w = self.state[kk][0]
                if w is not None:
                    toks.append(w)
        for r in writes:
            for kk in self._conf(self._rk(r)):
                w, rds = self.state[kk]
                if w is not None:
                    toks.append(w)
                toks.extend(rds)
        return toks

    def _emit_waits(self, e, toks):
        best = {}
        for (sn, sh, val, pe) in toks:
            if pe == e and e in self.SAME and not self.SAME[e]:
                continue
            if self.waited[e].get(sn, 0) >= val:
                continue
            if sn not in best or best[sn][1] < val:
                best[sn] = (sh, val)
        for sn, (sh, val) in best.items():
            self.eng[e].wait_ge(sh, val)
            self.waited[e][sn] = val
            self.n_instr += 1

    def _update(self, tok, reads, writes):
        for r in writes:
            key = self._rk(r)
            if key[1] == "*":
                for kk in [kk for kk in self.state if kk[0] == key[0]]:
                    del self.state[kk]
            self.state[key] = [tok, []]
        for r in reads:
            key = self._rk(r)
            if key not in self.state:
                self.state[key] = [None, []]
            rds = self.state[key][1]
            rds[:] = [t for t in rds if t[0] != tok[0]]
            rds.append(tok)

    def op(self, e, fn, r=(), w=()):
        toks = self._collect(r, w)
        self._emit_waits(e, toks)
        ins = fn()
        self.ccnt[e] += 1
        ins.then_inc(self.csem[e], 1)
        tok = ("c_" + e, self.csem[e], self.ccnt[e], e)
        self._update(tok, r, w)
        self.n_instr += 1
        return tok

    def dma(self, q, out, in_, r=(), w=(), is_output=False):
        toks = self._collect(r, w)
        i = self.drr[q]
        self.drr[q] = (i + 1) % self.ND
        sn = "d_%s%d" % (q, i)
        sh = self.dsem[q][i]
        if self.dcnt[q][i] > 0:
            toks.append((sn, sh, self.dcnt[q][i] * 16, "dma"))
        self._emit_waits(q, toks)
        self.eng[q].dma_start(out=out, in_=in_).then_inc(sh, 16)
        self.dcnt[q][i] += 1
        tok = (sn, sh, self.dcnt[q][i] * 16, "dma")
        self._update(tok, r, w)
        if is_output:
            self.out_tokens.append(tok)
        self.n_instr += 1
        return tok

    def finish(self):
        self._emit_waits("sp", self.out_tokens)
        toks = [("c_" + e, self.csem[e], self.ccnt[e], e) for e in self.csem if self.ccnt[e] > 0]
        self._emit_waits("sp", toks)


class PsumPool:
    def __init__(self, s, n, width=512, tiles=None):
        self.tiles = tiles if tiles is not None else [s.ps("ps%d" % i, [128, width]) for i in range(n)]
        self.i = 0

    def get(self):
        t = self.tiles[self.i]
        self.i = (self.i + 1) % len(self.tiles)
        return t


class Rot:
    def __init__(self, tiles):
        self.tiles = tiles
        self.i = 0

    def get(self):
        t = self.tiles[self.i]
        self.i = (self.i + 1) % len(self.tiles)
        return t


def mk_consts(s, nc):
    c = {}
    c["ones_f"] = s.sb("ones_f", [128, 128], F32)
    s.op("pool", lambda: nc.gpsimd.memset(c["ones_f"][:], 1.0), w=[c["ones_f"]])
    c["ones_b"] = s.sb("ones_b", [128, 128], BF16)
    s.op("pool", lambda: nc.gpsimd.memset(c["ones_b"][:], 1.0), w=[c["ones_b"]])
    return c


def rstd_from_psum(s, nc, ps, P, N, scale, bias, out_t, tmp_t):
    s.op("act", lambda: nc.scalar.activation(out=tmp_t[0:P, 0:N], in_=ps[0:P, 0:N], func=AF.Sqrt,
                                             scale=scale, bias=bias), r=[ps], w=[tmp_t])
    s.op("dve", lambda: nc.vector.reciprocal(out=out_t[0:P, 0:N], in_=tmp_t[0:P, 0:N]), r=[tmp_t], w=[out_t])


def rope_tables(s, nc, pos_bc_i, TPC, inv_col, sgn_col, P, cos_t, sins_t, tmp):
    posf, ang, t, n, r = tmp
    INV2PI = 1.0 / (2.0 * math.pi)
    MAGIC = 12582912.0
    C1 = 6.28125
    C2 = 2.0 * math.pi - 6.28125
    PI_LO = 3.1415925
    s.op("dve", lambda: nc.vector.tensor_copy(out=posf[0:P, :], in_=pos_bc_i[0:P, :]), r=[pos_bc_i], w=[posf])
    s.op("dve", lambda: nc.vector.tensor_scalar(out=ang[0:P, :], in0=posf[0:P, :], scalar1=inv_col[0:P, 0:1],
                                                scalar2=None, op0=ALU.mult), r=[posf, inv_col], w=[ang])
    for which in ("sin", "cos"):
        off = 0.0 if which == "sin" else 0.25
        s.op("dve", lambda: nc.vector.tensor_scalar(out=t[0:P, :], in0=ang[0:P, :], scalar1=INV2PI, scalar2=off,
                                                    op0=ALU.mult, op1=ALU.add), r=[ang], w=[t])
        s.op("dve", lambda: nc.vector.tensor_scalar(out=n[0:P, :], in0=t[0:P, :], scalar1=MAGIC, scalar2=None,
                                                    op0=ALU.add), r=[t], w=[n])
        s.op("dve", lambda: nc.vector.tensor_scalar(out=n[0:P, :], in0=n[0:P, :], scalar1=MAGIC, scalar2=None,
                                                    op0=ALU.subtract), r=[n], w=[n])
        s.op("dve", lambda: nc.vector.scalar_tensor_tensor(out=r[0:P, :], in0=n[0:P, :], scalar=-C1, in1=ang[0:P, :],
                                                           op0=ALU.mult, op1=ALU.add), r=[n, ang], w=[r])
        s.op("dve", lambda: nc.vector.scalar_tensor_tensor(out=t[0:P, :], in0=n[0:P, :], scalar=-C2, in1=r[0:P, :],
                                                           op0=ALU.mult, op1=ALU.add), r=[n, r], w=[t])
        if which == "cos":
            s.op("dve", lambda: nc.vector.tensor_scalar(out=t[0:P, :], in0=t[0:P, :], scalar1=math.pi / 2, scalar2=None,
                                                        op0=ALU.add), r=[t], w=[t])
        s.op("dve", lambda: nc.vector.tensor_scalar(out=r[0:P, :], in0=t[0:P, :], scalar1=-PI_LO, scalar2=PI_LO,
                                                    op0=ALU.max, op1=ALU.min), r=[t], w=[r])
        if which == "sin":
            s.op("act", lambda: nc.scalar.activation(out=t[0:P, :], in_=r[0:P, :], func=AF.Sin), r=[r], w=[t])
            s.op("dve", lambda: nc.vector.tensor_scalar(out=sins_t[0:P, :], in0=t[0:P, :], scalar1=sgn_col[0:P, 0:1],
                                                        scalar2=None, op0=ALU.mult), r=[t, sgn_col], w=[sins_t])
        else:
            s.op("act", lambda: nc.scalar.activation(out=cos_t[0:P, :], in_=r[0:P, :], func=AF.Sin), r=[r], w=[cos_t])


def compute_hT(s, nc, cn, pp, x_src, nw, hT, NT, sq_rot, rstd_t, tmp_t, x_resident, col0=0):
    ps = pp.get()
    for c in range(KC):
        xa, xd = x_src(c)
        sq = sq_rot.get()
        s.op("act", lambda: nc.scalar.activation(out=sq[:, 0:NT], in_=xa, func=AF.Square), r=xd, w=[sq])
        s.op("pe", lambda: nc.tensor.matmul(ps[:, 0:NT], cn["ones_f"][:, :], sq[:, 0:NT], start=(c == 0), stop=(c == KC - 1)),
             r=[sq, cn["ones_f"]], w=[ps])
    rstd_from_psum(s, nc, ps, 128, NT, 1.0 / D_MODEL, RMS_EPS, rstd_t, tmp_t)
    for c in range(KC):
        xa, xd = x_src(c)
        s.op("dve", lambda: nc.vector.scalar_tensor_tensor(out=hT[:, c, col0:col0 + NT], in0=xa, scalar=nw[:, c:c + 1],
                                                           in1=rstd_t[:, 0:NT], op0=ALU.mult, op1=ALU.mult),
             r=xd + [nw, rstd_t], w=[(hT, c)])


def planA():
    tiles = []
    for h in range(4):
        for kind in ("rq", "rqp", "rk", "rkp", "rv"):
            tiles.append((kind, h, 128))
    for i in range(8):
        tiles.append(("sx", i, 128))
    tiles.append(("sdt", 0, 8))
    for i in range(4):
        tiles.append(("cq", i, 128))
    for i in range(2):
        tiles.append(("ckv", i, 128))
    tiles.append(("kr", 0, 64))
    tiles.append(("krp", 0, 64))
    for i in range(14):
        tiles.append(("rw", i, 128))
    groups = []
    cur = []
    curw = 0
    off = 0
    for t in tiles:
        if curw + t[2] > 512:
            groups.append((off - curw, curw, cur))
            cur = []
            curw = 0
        cur.append((t[0], t[1], t[2], curw))
        curw += t[2]
        off += t[2]
    groups.append((off - curw, curw, cur))
    return tiles, groups, off


def colsA():
    idx = []
    p64 = (np.arange(128) + 64) % 128
    p32 = (np.arange(64) + 32) % 64
    for h in range(4):
        q0 = 0 + h * 128
        k0 = 512 + h * 128
        v0 = 1024 + h * 128
        idx += list(q0 + np.arange(128)) + list(q0 + p64) + list(k0 + np.arange(128)) + list(k0 + p64) + list(v0 + np.arange(128))
    idx += list(2560 + np.arange(1024))
    idx += list(3584 + np.arange(8))
    idx += list(3592 + np.arange(512))
    idx += list(4104 + np.arange(256))
    idx += list(4360 + np.arange(64)) + list(4360 + p32)
    idx += list(4424 + np.arange(1792))
    return np.array(idx, dtype=np.int64)


def build_A(cfg):
    TPC, NT, NP = cfg.TPC, cfg.NT, cfg.NP
    tiles, groups, NCA = planA()
    nc = bass.Bass("TRN2", target_bir_lowering=False)
    es = contextlib.ExitStack()
    with es:
        s = Sched(nc, es)
        IN, OUT = "ExternalInput", "ExternalOutput"
        xT = s.dram("xT", [128, KC, TPC], F32, IN)
        pos = s.dram("pos", [1, TPC], I32, IN)
        n1w = s.dram("n1w", [128, KC], F32, IN)
        wA = s.dram("wA", [128, KC, NCA], F32, IN)
        ropec = s.dram("ropec", [128, 4], F32, IN)
        wqb = s.dram("wqb", [128, 4, 1024], F32, IN)
        wkvb = s.dram("wkvb", [128, 2, 1024], F32, IN)
        mlaw = s.dram("mlaw", [128, 16], F32, IN)
        o_rq = s.dram("o_rq", [4, 128, TPC], F32, OUT)
        o_rk = s.dram("o_rk", [4, 128, TPC], F32, OUT)
        o_rv = s.dram("o_rv", [4, 128, TPC], F32, OUT)
        o_sx = s.dram("o_sx", [8, 128, TPC], F32, OUT)
        o_sdt = s.dram("o_sdt", [8, TPC], F32, OUT)
        o_rw = s.dram("o_rw", [14, 128, TPC], F32, OUT)
        o_mqn = s.dram("o_mqn", [4, 128, TPC], BF16, OUT)
        o_mqr = s.dram("o_mqr", [4, 64, TPC], BF16, OUT)
        o_mkn = s.dram("o_mkn", [4, 128, TPC], BF16, OUT)
        o_mkr = s.dram("o_mkr", [4, 64, TPC], BF16, OUT)
        o_mv = s.dram("o_mv", [TPC, 512], BF16, OUT)

        cn = mk_consts(s, nc)
        pp = PsumPool(s, 8)
        n1w_t = s.sb("n1w_t", [128, KC], F32)
        s.dma("sp", n1w_t[:], n1w[:], r=[n1w], w=[n1w_t])
        ropec_t = s.sb("ropec_t", [128, 4], F32)
        s.dma("sp", ropec_t[:], ropec[:], r=[ropec], w=[ropec_t])
        mlaw_t = s.sb("mlaw_t", [128, 16], F32)
        s.dma("sp", mlaw_t[:], mlaw[:], r=[mlaw], w=[mlaw_t])
        wqb_t = s.sb("wqb_t", [128, 4, 1024], BF16)
        for c in range(4):
            s.dma("pool", wqb_t[:, c, :], wqb[:, c, :], r=[wqb], w=[wqb_t])
        wkvb_t = s.sb("wkvb_t", [128, 2, 1024], BF16)
        for c in range(2):
            s.dma("pool", wkvb_t[:, c, :], wkvb[:, c, :], r=[wkvb], w=[wkvb_t])

        pos_i = s.sb("pos_i", [128, TPC], I32)
        s.dma("sp", pos_i[:], pos[0:1, :].partition_broadcast(128), r=[pos], w=[pos_i])
        tmp5 = [s.sb("rt%d" % i, [128, TPC], F32) for i in range(5)]
        cos_r = s.sb("cos_r", [128, TPC], F32)
        sin_r = s.sb("sin_r", [128, TPC], F32)
        cos_m = s.sb("cos_m", [64, TPC], F32)
        sin_m = s.sb("sin_m", [64, TPC], F32)
        inv_r = Tn("ropec_t", ropec_t.t[:, 0:1])
        sgn_r = Tn("ropec_t", ropec_t.t[:, 1:2])
        inv_m = Tn("ropec_t", ropec_t.t[:, 2:3])
        sgn_m = Tn("ropec_t", ropec_t.t[:, 3:4])
        rope_tables(s, nc, pos_i, TPC, inv_r, sgn_r, 128, cos_r, sin_r, tmp5)
        rope_tables(s, nc, pos_i, TPC, inv_m, sgn_m, 64, cos_m, sin_m, tmp5)

        hT = s.sb("hT", [128, KC, NT], BF16)
        xrot = Rot([s.sb("xb%d" % i, [128, NT], F32) for i in range(4)])
        sqrot = Rot([s.sb("sq%d" % i, [128, NT], F32) for i in range(2)])
        rstd_t = s.sb("rstd_t", [128, NT], F32)
        rtmp = s.sb("rtmp", [128, NT], F32)
        wrot = Rot([s.sb("wb%d" % i, [128, KC, 512], BF16) for i in range(3)])
        stg = Rot([s.sb("stg%d" % i, [128, NT], F32) for i in range(4)])
        t1rot = Rot([s.sb("t1_%d" % i, [128, NT], F32) for i in range(2)])
        cqT = s.sb("cqT", [128, 4, NT], F32)
        ckvT = s.sb("ckvT", [128, 2, NT], F32)
        krT = s.sb("krT", [64, 2, NT], F32)
        cqn = s.sb("cqn", [128, 4, NT], BF16)
        ckvn = s.sb("ckvn", [128, 2, NT], BF16)
        bstg = Rot([s.sb("bstg%d" % i, [128, NT], BF16) for i in range(3)])
        vstg = Rot([s.sb("vstg%d" % i, [128, 512], BF16) for i in range(2)])
        rs2 = s.sb("rs2", [128, NT], F32)

        for p in range(NP):
            t0 = p * NT
            tsl = slice(t0, t0 + NT)

            def x_src(c):
                xb = xrot.get()
                s.dma("sp", xb[:, 0:NT], xT[:, c, tsl], r=[xT], w=[xb])
                return xb[:, 0:NT], [xb]
            compute_hT(s, nc, cn, pp, x_src, n1w_t, hT, NT, sqrot, rstd_t, rtmp, False)

            pend = None
            for (c0, gw, gt) in groups:
                wb = wrot.get()
                s.dma("pool", wb[:, :, 0:gw], wA[:, :, c0:c0 + gw], r=[wA], w=[wb])
                for (kind, idx, ncol, lo) in gt:
                    ps = pp.get()
                    for c in range(KC):
                        s.op("pe", lambda: nc.tensor.matmul(ps[0:ncol, 0:NT], wb[:, c, lo:lo + ncol], hT[:, c, 0:NT],
                                                            start=(c == 0), stop=(c == KC - 1)),
                             r=[wb, (hT, c)], w=[ps])
                    if kind in ("rq", "rk"):
                        sc = 1.0 if kind == "rq" else 128.0 ** -0.5
                        t1 = t1rot.get()
                        s.op("dve", lambda: nc.vector.scalar_tensor_tensor(out=t1[:, 0:NT], in0=ps[:, 0:NT], scalar=sc,
                                                                           in1=cos_r[:, tsl], op0=ALU.mult, op1=ALU.mult),
                             r=[ps, cos_r], w=[t1])
                        pend = t1
                    elif kind in ("rqp", "rkp"):
                        sc = 1.0 if kind == "rqp" else 128.0 ** -0.5
                        t2 = t1rot.get()
                        s.op("dve", lambda: nc.vector.scalar_tensor_tensor(out=t2[:, 0:NT], in0=ps[:, 0:NT], scalar=sc,
                                                                           in1=sin_r[:, tsl], op0=ALU.mult, op1=ALU.mult),
                             r=[ps, sin_r], w=[t2])
                        st = stg.get()
                        s.op("dve", lambda: nc.vector.tensor_tensor(out=st[:, 0:NT], in0=pend[:, 0:NT], in1=t2[:, 0:NT],
                                                                    op=ALU.add), r=[pend, t2], w=[st])
                        dst = o_rq if kind == "rqp" else o_rk
                        s.dma("sp", dst[idx, :, tsl], st[:, 0:NT], r=[st], w=[(dst, (idx, p))], is_output=True)
                    elif kind in ("rv", "sx", "rw", "sdt"):
                        st = stg.get()
                        s.op("act", lambda: nc.scalar.copy(out=st[0:ncol, 0:NT], in_=ps[0:ncol, 0:NT]), r=[ps], w=[st])
                        if kind == "sdt":
                            s.dma("sp", o_sdt[:, tsl], st[0:8, 0:NT], r=[st], w=[(o_sdt, p)], is_output=True)
                        else:
                            dst = {"rv": o_rv, "sx": o_sx, "rw": o_rw}[kind]
                            s.dma("sp", dst[idx, :, tsl], st[:, 0:NT], r=[st], w=[(dst, (idx, p))], is_output=True)
                    elif kind == "cq":
                        s.op("act", lambda: nc.scalar.copy(out=cqT[:, idx, 0:NT], in_=ps[:, 0:NT]), r=[ps], w=[(cqT, idx)])
                    elif kind == "ckv":
                        s.op("act", lambda: nc.scalar.copy(out=ckvT[:, idx, 0:NT], in_=ps[:, 0:NT]), r=[ps], w=[(ckvT, idx)])
                    elif kind in ("kr", "krp"):
                        j = 0 if kind == "kr" else 1
                        s.op("act", lambda: nc.scalar.copy(out=krT[:, j, 0:NT], in_=ps[0:64, 0:NT]), r=[ps], w=[(krT, j)])

            def rms_feat(src, nch, P, wcol0, dst, nfeat):
                psn = pp.get()
                for c in range(nch):
                    sq = sqrot.get()
                    s.op("act", lambda: nc.scalar.activation(out=sq[0:P, 0:NT], in_=src[0:P, c, 0:NT], func=AF.Square),
                         r=[(src, c)], w=[sq])
                    s.op("pe", lambda: nc.tensor.matmul(psn[:, 0:NT], cn["ones_f"][0:P, :], sq[0:P, 0:NT],
                                                        start=(c == 0), stop=(c == nch - 1)), r=[sq, cn["ones_f"]], w=[psn])
                rstd_from_psum(s, nc, psn, 128, NT, 1.0 / nfeat, RMS_EPS, rs2, rtmp)
                for c in range(nch):
                    s.op("dve", lambda: nc.vector.scalar_tensor_tensor(out=dst[:, c, 0:NT], in0=src[:, c, 0:NT],
                                                                       scalar=mlaw_t[:, wcol0 + c:wcol0 + c + 1],
                                                                       in1=rs2[:, 0:NT], op0=ALU.mult, op1=ALU.mult),
                         r=[(src, c), mlaw_t, rs2], w=[(dst, c)])
            rms_feat(cqT, 4, 128, 0, cqn, 512)
            rms_feat(ckvT, 2, 128, 4, ckvn, 256)

            def head_qk(is_q, h):
                if is_q:
                    wt, nkc, src = wqb_t, 4, cqn
                    cn0, cr0, cp0 = h * 256, h * 256 + 128, h * 256 + 192
                    wn, wr, wp = 6, 7, 8
                else:
                    wt, nkc, src = wkvb_t, 2, ckvn
                    cn0 = h * 128
                    wn, wr, wp = 9, 10, 11
                psn = pp.get()
                for c in range(nkc):
                    s.op("pe", lambda: nc.tensor.matmul(psn[:, 0:NT], wt[:, c, cn0:cn0 + 128], src[:, c, 0:NT],
                                                        start=(c == 0), stop=(c == nkc - 1)), r=[wt, (src, c)], w=[psn])
                nope = stg.get()
                s.op("act", lambda: nc.scalar.copy(out=nope[:, 0:NT], in_=psn[:, 0:NT]), r=[psn], w=[nope])
                if is_q:
                    psr = pp.get()
                    psp = pp.get()
                    for (pst, c00) in ((psr, cr0), (psp, cp0)):
                        for c in range(nkc):
                            s.op("pe", lambda: nc.tensor.matmul(pst[0:64, 0:NT], wt[:, c, c00:c00 + 64], src[:, c, 0:NT],
                                                                start=(c == 0), stop=(c == nkc - 1)), r=[wt, (src, c)], w=[pst])
                    rr = stg.get()
                    s.op("act", lambda: nc.scalar.copy(out=rr[0:64, 0:NT], in_=psr[0:64, 0:NT]), r=[psr], w=[rr])
                    rp = stg.get()
                    s.op("act", lambda: nc.scalar.copy(out=rp[0:64, 0:NT], in_=psp[0:64, 0:NT]), r=[psp], w=[rp])
                    rr_ap, rp_ap, rdeps = rr[0:64, 0:NT], rp[0:64, 0:NT], [rr, rp]
                else:
                    rr_ap, rp_ap, rdeps = krT[:, 0, 0:NT], krT[:, 1, 0:NT], [(krT, 0), (krT, 1)]
                pss = pp.get()
                sq = sqrot.get()
                s.op("act", lambda: nc.scalar.activation(out=sq[:, 0:NT], in_=nope[:, 0:NT], func=AF.Square), r=[nope], w=[sq])
                s.op("pe", lambda: nc.tensor.matmul(pss[:, 0:NT], cn["ones_f"][:, :], sq[:, 0:NT], start=True, stop=False),
                     r=[sq, cn["ones_f"]], w=[pss])
                sq2 = sqrot.get()
                s.op("act", lambda: nc.scalar.activation(out=sq2[0:64, 0:NT], in_=rr_ap, func=AF.Square), r=rdeps, w=[sq2])
                s.op("pe", lambda: nc.tensor.matmul(pss[:, 0:NT], cn["ones_f"][0:64, :], sq2[0:64, 0:NT], start=False, stop=True),
                     r=[sq2, cn["ones_f"]], w=[pss])
                if is_q:
                    rstd_from_psum(s, nc, pss, 128, NT, 1.0, 192.0 * RMS_EPS, rs2, rtmp)
                else:
                    rstd_from_psum(s, nc, pss, 128, NT, 1.0 / 192.0, RMS_EPS, rs2, rtmp)
                ob = bstg.get()
                s.op("dve", lambda: nc.vector.scalar_tensor_tensor(out=ob[:, 0:NT], in0=nope[:, 0:NT], scalar=mlaw_t[:, wn:wn + 1],
                                                                   in1=rs2[:, 0:NT], op0=ALU.mult, op1=ALU.mult),
                     r=[nope, mlaw_t, rs2], w=[ob])
                dstn = o_mqn if is_q else o_mkn
                s.dma("sp", dstn[h, :, tsl], ob[:, 0:NT], r=[ob], w=[(dstn, (h, p))], is_output=True)
                t1 = t1rot.get()
                s.op("dve", lambda: nc.vector.scalar_tensor_tensor(out=t1[0:64, 0:NT], in0=rr_ap, scalar=mlaw_t[0:64, wr:wr + 1],
                                                                   in1=cos_m[:, tsl], op0=ALU.mult, op1=ALU.mult),
                     r=rdeps + [mlaw_t, cos_m], w=[t1])
                t2 = t1rot.get()
                s.op("dve", lambda: nc.vector.scalar_tensor_tensor(out=t2[0:64, 0:NT], in0=rp_ap, scalar=mlaw_t[0:64, wp:wp + 1],
                                                                   in1=sin_m[:, tsl], op0=ALU.mult, op1=ALU.mult),
                     r=rdeps + [mlaw_t, sin_m], w=[t2])
                s.op("dve", lambda: nc.vector.tensor_tensor(out=t1[0:64, 0:NT], in0=t1[0:64, 0:NT], in1=t2[0:64, 0:NT], op=ALU.add),
                     r=[t1, t2], w=[t1])
                ob2 = bstg.get()
                s.op("dve", lambda: nc.vector.tensor_tensor(out=ob2[0:64, 0:NT], in0=t1[0:64, 0:NT], in1=rs2[0:64, 0:NT], op=ALU.mult),
                     r=[t1, rs2], w=[ob2])
                dstr = o_mqr if is_q else o_mkr
                s.dma("sp", dstr[h, :, tsl], ob2[0:64, 0:NT], r=[ob2], w=[(dstr, (h, p))], is_output=True)

            for h in range(4):
                head_qk(True, h)
                head_qk(False, h)
            for tt in range(NT // 128):
                psv = pp.get()
                for c in range(2):
                    s.op("pe", lambda: nc.tensor.matmul(psv[:, 0:512], ckvn[:, c, tt * 128:(tt + 1) * 128], wkvb_t[:, c, 512:1024],
                                                        start=(c == 0), stop=(c == 1)), r=[(ckvn, c), wkvb_t], w=[psv])
                vs = vstg.get()
                s.op("act", lambda: nc.scalar.copy(out=vs[:, :], in_=psv[:, 0:512]), r=[psv], w=[vs])
                s.dma("sp", o_mv[t0 + tt * 128:t0 + (tt + 1) * 128, :], vs[:, :], r=[vs], w=[(o_mv, (p, tt))], is_output=True)
        s.finish()
        print("build_A instrs", s.n_instr)
    return nc


def fm(a):
    K, N = a.shape
    return np.ascontiguousarray(a.reshape(K // 128, 128, N).transpose(1, 0, 2))


def colvec(v, n=None):
    v = np.asarray(v, dtype=np.float32)
    return np.ascontiguousarray(v.reshape(-1, 128).T)


def pad_rows(a, rows=128):
    out = np.zeros((rows,) + a.shape[1:], dtype=a.dtype)
    out[: a.shape[0]] = a
    return out


def rope_consts():
    inv64 = (1.0 / (np.float32(10000.0) ** (np.arange(0, 128, 2, dtype=np.float32) / np.float32(128)))).astype(np.float32)
    inv32 = (1.0 / (np.float32(10000.0) ** (np.arange(0, 64, 2, dtype=np.float32) / np.float32(64)))).astype(np.float32)
    rc = np.zeros((128, 4), np.float32)
    d = np.arange(128)
    rc[:, 0] = inv64[d % 64]
    rc[:, 1] = np.where(d < 64, -1.0, 1.0)
    rc[:64, 2] = inv32[np.arange(64) % 32]
    rc[:64, 3] = np.where(np.arange(64) < 32, -1.0, 1.0)
    return rc


def host_A_weights(inp, l):
    p32 = (np.arange(64) + 32) % 64
    w = {}
    w["wA"] = fm(inp["w_in"][l][:, colsA()])
    w["n1w"] = colvec(inp["norm1_w"][l])
    w["ropec"] = rope_consts()
    qb = inp["mla_w_qb"][l]
    cols = []
    for h in range(4):
        b = h * 192
        cols += list(b + np.arange(128)) + list(b + 128 + np.arange(64)) + list(b + 128 + p32)
    w["wqb"] = fm(qb[:, np.array(cols)])
    kvb = inp["mla_w_kvb"][l]
    cols = []
    for h in range(4):
        cols += list(h * 256 + np.arange(128))
    for h in range(4):
        cols += list(h * 256 + 128 + np.arange(128))
    w["wkvb"] = fm(kvb[:, np.array(cols)])
    m = np.zeros((128, 16), np.float32)
    m[:, 0:4] = colvec(inp["mla_q_a_norm_w"][l])
    m[:, 4:6] = colvec(inp["mla_kv_a_norm_w"][l])
    qn = inp["mla_q_norm_w"][l]
    kn = inp["mla_k_norm_w"][l]
    m[:, 6] = qn[0:128]
    m[:64, 7] = qn[128:192]
    m[:64, 8] = qn[128 + p32]
    m[:, 9] = kn[0:128]
    m[:64, 10] = kn[128:192]
    m[:64, 11] = kn[128 + p32]
    w["mlaw"] = m
    return w


def to_fm_tokens(x2d):
    T, Dd = x2d.shape
    return np.ascontiguousarray(x2d.T.reshape(Dd // 128, 128, T).transpose(1, 0, 2))


def run_spmd(nc, in_maps):
    res = run_bass_kernel_spmd(nc, in_maps, core_ids=list(range(NCORES)))
    return res.results


def ret_consts(h):
    lg = np.log1p(-np.exp2(np.float32(-5.0 - h))).astype(np.float64)
    m = np.arange(128)[:, None]
    l = np.arange(128)[None, :]
    cm, cl = m // 64, l // 64
    mask = np.where(cm == cl, np.exp(lg * np.abs(l - m)), np.where(cm < cl, np.exp(lg * (l - m)), 0.0))
    c = {}
    c["ret_mask"] = mask.astype(np.float32)
    col = np.zeros((128, 2), np.float32)
    col[:, 0] = np.exp(lg * (127 - np.arange(128)))
    col[:, 1] = np.exp(lg * 128)
    c["ret_col"] = col
    c["ret_qdec"] = np.tile(np.exp(lg * (np.arange(128) + 1.0))[None, :], (128, 1)).astype(np.float32)
    return c


def mixer_ret(s, nc, cn, pp, S, TB, D):
    NB = S // TB
    mask = s.sb("r_mask", [128, 128], F32)
    s.dma("sp", mask[:], D["ret_mask"][:], r=[D["ret_mask"]], w=[mask])
    qdec = s.sb("r_qdec", [128, 128], F32)
    s.dma("sp", qdec[:], D["ret_qdec"][:], r=[D["ret_qdec"]], w=[qdec])
    rcol = s.sb("r_col", [128, 2], F32)
    s.dma("sp", rcol[:], D["ret_col"][:], r=[D["ret_col"]], w=[rcol])
    qb = Rot([s.sb("r_q%d" % i, [128, TB], F32) for i in range(2)])
    kb = Rot([s.sb("r_k%d" % i, [128, TB], F32) for i in range(2)])
    vb = Rot([s.sb("r_v%d" % i, [128, TB], F32) for i in range(2)])
    ob = Rot([s.sb("r_o%d" % i, [128, TB], F32) for i in range(2)])
    St = [s.sb("r_S%d" % i, [128, 128], F32) for i in range(2)]
    s.op("pool", lambda: nc.gpsimd.memset(St[0][:], 0.0), w=[St[0]])
    NS = TB // 128
    ktm = [s.sb("r_ktm%d" % i, [128, 128], F32) for i in range(NS)]
    vtm = [s.sb("r_vtm%d" % i, [128, 128], F32) for i in range(NS)]
    pT = [s.sb("r_pT%d" % i, [128, 128], F32) for i in range(NS)]
    qd = [s.sb("r_qd%d" % i, [128, 128], F32) for i in range(NS)]
    kvs = [s.sb("r_kvs%d" % i, [128, 128], F32) for i in range(NS)]
    ident = cn["ident"]
    cnt = {"sc": 0}
    for b in range(NB):
        bs = slice(b * TB, (b + 1) * TB)
        q, k, v, o = qb.get(), kb.get(), vb.get(), ob.get()
        s.dma("sp", q[:], D["ret_q"][:, bs], r=[D["ret_q"]], w=[q])
        s.dma("sp", k[:], D["ret_k"][:, bs], r=[D["ret_k"]], w=[k])
        s.dma("sp", v[:], D["ret_v"][:, bs], r=[D["ret_v"]], w=[v])

        def sc_gen(j):
            cs_ = slice(j * 128, (j + 1) * 128)
            kt, vt, pt, qdt, kv = ktm[j], vtm[j], pT[j], qd[j], kvs[j]
            p1 = pp.get()
            s.op("pe", lambda: nc.tensor.transpose(p1[:, 0:128], k[:, cs_], ident[:, :]), r=[k, ident], w=[p1])
            s.op("act", lambda: nc.scalar.activation(out=kt[:, :], in_=p1[:, 0:128], func=AF.Copy, scale=rcol[:, 0:1]),
                 r=[p1, rcol], w=[kt])
            yield
            p2 = pp.get()
            s.op("pe", lambda: nc.tensor.transpose(p2[:, 0:128], v[:, cs_], ident[:, :]), r=[v, ident], w=[p2])
            s.op("dve", lambda: nc.vector.tensor_copy(out=vt[:, :], in_=p2[:, 0:128]), r=[p2], w=[vt])
            yield
            p3 = pp.get()
            s.op("pe", lambda: nc.tensor.matmul(p3[:, 0:128], k[:, cs_], q[:, cs_], start=True, stop=True), r=[k, q], w=[p3])
            s.op("dve", lambda: nc.vector.tensor_tensor(out=pt[:, :], in0=p3[:, 0:128], in1=mask[:, :], op=ALU.mult),
                 r=[p3, mask], w=[pt])
            s.op("pool", lambda: nc.gpsimd.tensor_tensor(out=qdt[:, :], in0=q[:, cs_], in1=qdec[:, :], op=ALU.mult),
                 r=[q, qdec], w=[qdt])
            yield
            p5 = pp.get()
            s.op("pe", lambda: nc.tensor.matmul(p5[:, 0:128], kt[:, :], vt[:, :], start=True, stop=True), r=[kt, vt], w=[p5])
            s.op("act", lambda: nc.scalar.copy(out=kv[:, :], in_=p5[:, 0:128]), r=[p5], w=[kv])
            yield
            Sold, Snew = St[cnt["sc"] % 2], St[(cnt["sc"] + 1) % 2]
            cnt["sc"] += 1
            p4 = pp.get()
            s.op("pe", lambda: nc.tensor.matmul(p4[:, 0:128], vt[:, :], pt[:, :], start=True, stop=False), r=[vt, pt], w=[p4])
            s.op("pe", lambda: nc.tensor.matmul(p4[:, 0:128], Sold[:, :], qdt[:, :], start=False, stop=True), r=[Sold, qdt], w=[p4])
            s.op("act", lambda: nc.scalar.copy(out=o[:, cs_], in_=p4[:, 0:128]), r=[p4], w=[o])
            s.op("dve", lambda: nc.vector.scalar_tensor_tensor(out=Snew[:, :], in0=Sold[:, :], scalar=rcol[:, 1:2], in1=kv[:, :],
                                                               op0=ALU.mult, op1=ALU.add), r=[Sold, rcol, kv], w=[Snew])
            yield
        gens = [sc_gen(j) for j in range(NS)]
        while gens:
            for g_ in list(gens):
                try:
                    next(g_)
                except StopIteration:
                    gens.remove(g_)
            yield 4.5 * NS / 6.0
        s.dma("sp", D["o_ret"][:, bs], o[:], r=[o], w=[(D["o_ret"], b)], is_output=True)


def ssd_consts():
    m = np.arange(128)[:, None]
    l = np.arange(128)[None, :]
    return {"ssd_negmask": np.where(l >= m, 0.0, -30000.0).astype(np.float32)}


def mixer_ssd(s, nc, cn, pp, S, TB, D):
    NB = S // TB
    ident = cn["ident"]
    negmask = s.sb("s_negmask", [128, 128], F32)
    s.dma("sp", negmask[:], D["ssd_negmask"][:], r=[D["ssd_negmask"]], w=[negmask])
    cwx = s.sb("s_cwx", [64, 5], F32)
    cwB = s.sb("s_cwB", [128, 5], F32)
    cwC = s.sb("s_cwC", [128, 5], F32)
    scal = s.sb("s_scal", [128, 4], F32)
    for t_, n_ in ((cwx, "ssd_cwx"), (cwB, "ssd_cwB"), (cwC, "ssd_cwC"), (scal, "ssd_scal")):
        s.dma("sp", t_[:], D[n_][:], r=[D[n_]], w=[t_])
    Acol = s.sb("s_A", [128, 1], F32)
    s.op("act", lambda: nc.scalar.activation(out=Acol[:, :], in_=scal[:, 1:2], func=AF.Exp), r=[scal], w=[Acol])
    s.op("dve", lambda: nc.vector.tensor_scalar(out=Acol[:, :], in0=Acol[:, :], scalar1=-1.0, scalar2=None, op0=ALU.mult),
         r=[Acol], w=[Acol])
    onesrow = s.sb("s_onesrow", [1, 128], F32)
    s.op("pool", lambda: nc.gpsimd.memset(onesrow[:], 1.0), w=[onesrow])
    negrow = s.sb("s_negrow", [1, 128], F32)
    s.op("pool", lambda: nc.gpsimd.memset(negrow[:], -1.0), w=[negrow])
    HW = TB + 3
    xin = Rot([s.sb("s_xin%d" % i, [64, HW], F32) for i in range(2)])
    Bin = Rot([s.sb("s_Bin%d" % i, [128, HW], F32) for i in range(2)])
    Cin = Rot([s.sb("s_Cin%d" % i, [128, HW], F32) for i in range(2)])
    dtin = Rot([s.sb("s_dtin%d" % i, [1, TB], F32) for i in range(2)])
    xc = s.sb("s_xc", [64, TB], F32)
    Bc = s.sb("s_Bc", [128, TB], F32)
    Cc = s.sb("s_Cc", [128, TB], F32)
    acc = Rot([s.sb("s_acc%d" % i, [128, TB], F32) for i in range(2)])
    r1 = s.sb("s_r1", [1, TB], F32)
    r2 = s.sb("s_r2", [1, TB], F32)
    dtr = s.sb("s_dtr", [1, TB], F32)
    adt = s.sb("s_adt", [1, TB], F32)
    csr = s.sb("s_csr", [1, TB], F32)
    ecs = s.sb("s_ecs", [1, TB], F32)
    ob = Rot([s.sb("s_o%d" % i, [64, TB], F32) for i in range(2)])
    Sst = [s.sb("s_S%d" % i, [128, 64], F32) for i in range(2)]
    s.op("pool", lambda: nc.gpsimd.memset(Sst[0][:], 0.0), w=[Sst[0]])
    NS = TB // 128
    cols = [s.sb("s_cols%d" % i, [128, 6], F32) for i in range(NS)]
    tmpm = [s.sb("s_tm%d" % i, [128, 128], F32) for i in range(NS)]
    decT = [s.sb("s_dec%d" % i, [128, 128], F32) for i in range(NS)]
    pT = [s.sb("s_pT%d" % i, [128, 128], F32) for i in range(NS)]
    xdt = [s.sb("s_xdt%d" % i, [128, 64], F32) for i in range(NS)]
    xck = [s.sb("s_xck%d" % i, [128, 64], F32) for i in range(NS)]
    Btm = [s.sb("s_Btm%d" % i, [128, 128], F32) for i in range(NS)]
    Cdec = [s.sb("s_Cdec%d" % i, [128, 128], F32) for i in range(NS)]
    dSs = [s.sb("s_dS%d" % i, [128, 64], F32) for i in range(NS)]
    cnt = {"sc": 0}
    for b in range(NB):
        t0 = b * TB
        xi, Bi, Ci, dti = xin.get(), Bin.get(), Cin.get(), dtin.get()
        for (buf, name, P) in ((xi, "ssd_x", 64), (Bi, "ssd_B", 128), (Ci, "ssd_C", 128)):
            if b == 0:
                s.op("pool", lambda: nc.gpsimd.memset(buf[0:P, 0:3], 0.0), w=[buf])
                s.dma("sp", buf[0:P, 3:HW], D[name][:, 0:TB], r=[D[name]], w=[buf])
            else:
                s.dma("sp", buf[0:P, :], D[name][:, t0 - 3:t0 + TB], r=[D[name]], w=[buf])
        s.dma("sp", dti[:], D["ssd_dt"][:, t0:t0 + TB], r=[D["ssd_dt"]], w=[dti])
        for (buf, cw, outt, P) in ((xi, cwx, xc, 64), (Bi, cwB, Bc, 128), (Ci, cwC, Cc, 128)):
            a = acc.get()
            s.op("dve", lambda: nc.vector.tensor_scalar(out=a[0:P, :], in0=buf[0:P, 0:TB], scalar1=cw[0:P, 0:1], scalar2=cw[0:P, 4:5],
                                                        op0=ALU.mult, op1=ALU.add), r=[buf, cw], w=[a])
            for kk in range(1, 4):
                s.op("dve", lambda: nc.vector.scalar_tensor_tensor(out=a[0:P, :], in0=buf[0:P, kk:kk + TB], scalar=cw[0:P, kk:kk + 1],
                                                                   in1=a[0:P, :], op0=ALU.mult, op1=ALU.add), r=[buf, cw, a], w=[a])
            s.op("act", lambda: nc.scalar.activation(out=outt[0:P, :], in_=a[0:P, :], func=AF.Silu), r=[a], w=[outt])
        s.op("dve", lambda: nc.vector.tensor_scalar(out=r1[:, :], in0=dti[:, :], scalar1=scal[0:1, 0:1], scalar2=None, op0=ALU.add),
             r=[dti, scal], w=[r1])
        s.op("dve", lambda: nc.vector.scalar_tensor_tensor(out=r2[:, :], in0=r1[:, :], scalar=-1.0, in1=r1[:, :], op0=ALU.mult, op1=ALU.max),
             r=[r1], w=[r2])
        s.op("act", lambda: nc.scalar.activation(out=r2[:, :], in_=r2[:, :], func=AF.Exp, scale=-1.0), r=[r2], w=[r2])
        s.op("act", lambda: nc.scalar.activation(out=r2[:, :], in_=r2[:, :], func=AF.Ln, bias=1.0), r=[r2], w=[r2])
        s.op("dve", lambda: nc.vector.scalar_tensor_tensor(out=dtr[:, :], in0=r1[:, :], scalar=0.0, in1=r2[:, :], op0=ALU.max, op1=ALU.add),
             r=[r1, r2], w=[dtr])
        s.op("dve", lambda: nc.vector.tensor_scalar(out=adt[:, :], in0=dtr[:, :], scalar1=Acol[0:1, 0:1], scalar2=None, op0=ALU.mult),
             r=[dtr, Acol], w=[adt])
        for j in range(TB // 128):
            cs_ = slice(j * 128, (j + 1) * 128)
            s.op("dve", lambda: nc.vector.tensor_tensor_scan(out=csr[:, cs_], data0=onesrow[:, :], data1=adt[:, cs_], initial=0.0,
                                                             op0=ALU.mult, op1=ALU.add), r=[onesrow, adt], w=[csr])
        s.op("act", lambda: nc.scalar.activation(out=ecs[:, :], in_=csr[:, :], func=AF.Exp), r=[csr], w=[ecs])
        yield 8.0
        o = ob.get()

        def sc_gen(j):
            cs_ = slice(j * 128, (j + 1) * 128)
            cl, tm, dc, pt, xd, xk, bt, cd, dS = cols[j], tmpm[j], decT[j], pT[j], xdt[j], xck[j], Btm[j], Cdec[j], dSs[j]
            pc = pp.get()
            s.op("pe", lambda: nc.tensor.matmul(pc[:, 0:1], dtr[:, cs_], onesrow[:, 0:1], start=True, stop=True), r=[dtr, onesrow], w=[pc])
            s.op("pe", lambda: nc.tensor.matmul(pc[:, 1:2], csr[:, cs_], onesrow[:, 0:1], start=True, stop=True), r=[csr, onesrow], w=[pc])
            e_ = j * 128 + 127
            s.op("pe", lambda: nc.tensor.matmul(pc[:, 2:3], onesrow[:, :], csr[:, e_:e_ + 1], start=True, stop=True), r=[csr, onesrow], w=[pc])
            s.op("dve", lambda: nc.vector.tensor_copy(out=cl[:, 0:3], in_=pc[:, 0:3]), r=[pc], w=[cl])
            s.op("act", lambda: nc.scalar.activation(out=cl[:, 3:4], in_=cl[:, 1:2], func=AF.Exp, scale=-1.0, bias=cl[:, 2:3]), r=[cl], w=[cl])
            s.op("act", lambda: nc.scalar.activation(out=cl[:, 4:5], in_=cl[:, 2:3], func=AF.Exp), r=[cl], w=[cl])
            yield
            pg = pp.get()
            s.op("pe", lambda: nc.tensor.matmul(pg[:, 0:128], onesrow[:, :], csr[:, cs_], start=True, stop=False), r=[csr, onesrow], w=[pg])
            s.op("pe", lambda: nc.tensor.matmul(pg[:, 0:128], csr[:, cs_], negrow[:, :], start=False, stop=True), r=[csr, negrow], w=[pg])
            s.op("dve", lambda: nc.vector.scalar_tensor_tensor(out=tm[:, :], in0=pg[:, 0:128], scalar=0.0, in1=negmask[:, :],
                                                               op0=ALU.min, op1=ALU.add), r=[pg, negmask], w=[tm])
            s.op("act", lambda: nc.scalar.activation(out=dc[:, :], in_=tm[:, :], func=AF.Exp), r=[tm], w=[dc])
            yield
            pb = pp.get()
            s.op("pe", lambda: nc.tensor.matmul(pb[:, 0:128], Bc[:, cs_], Cc[:, cs_], start=True, stop=True), r=[Bc, Cc], w=[pb])
            s.op("dve", lambda: nc.vector.tensor_tensor(out=pt[:, :], in0=pb[:, 0:128], in1=dc[:, :], op=ALU.mult), r=[pb, dc], w=[pt])
            yield
            px = pp.get()
            s.op("pe", lambda: nc.tensor.transpose(px[:, 0:64], xc[:, cs_], ident[0:64, 0:64]), r=[xc, ident], w=[px])
            s.op("act", lambda: nc.scalar.activation(out=xd[:, :], in_=px[:, 0:64], func=AF.Copy, scale=cl[:, 0:1]), r=[px, cl], w=[xd])
            s.op("dve", lambda: nc.vector.tensor_scalar(out=xk[:, :], in0=xd[:, :], scalar1=cl[:, 3:4], scalar2=None, op0=ALU.mult),
                 r=[xd, cl], w=[xk])
            yield
            pe_ = pp.get()
            s.op("pe", lambda: nc.tensor.matmul(pe_[:, 0:128], onesrow[:, :], ecs[:, cs_], start=True, stop=True), r=[ecs, onesrow], w=[pe_])
            s.op("dve", lambda: nc.vector.tensor_tensor(out=cd[:, :], in0=pe_[:, 0:128], in1=Cc[:, cs_], op=ALU.mult), r=[pe_, Cc], w=[cd])
            yield
            pB = pp.get()
            s.op("pe", lambda: nc.tensor.transpose(pB[:, 0:128], Bc[:, cs_], ident[:, :]), r=[Bc, ident], w=[pB])
            s.op("act", lambda: nc.scalar.copy(out=bt[:, :], in_=pB[:, 0:128]), r=[pB], w=[bt])
            yield
            pS = pp.get()
            s.op("pe", lambda: nc.tensor.matmul(pS[:, 0:64], bt[:, :], xk[:, :], start=True, stop=True), r=[bt, xk], w=[pS])
            s.op("act", lambda: nc.scalar.copy(out=dS[:, :], in_=pS[:, 0:64]), r=[pS], w=[dS])
            yield
            Sold, Snew = Sst[cnt["sc"] % 2], Sst[(cnt["sc"] + 1) % 2]
            cnt["sc"] += 1
            py = pp.get()
            s.op("pe", lambda: nc.tensor.matmul(py[0:64, 0:128], xd[:, :], pt[:, :], start=True, stop=False), r=[xd, pt], w=[py])
            s.op("pe", lambda: nc.tensor.matmul(py[0:64, 0:128], Sold[:, :], cd[:, :], start=False, stop=True), r=[Sold, cd], w=[py])
            s.op("dve", lambda: nc.vector.scalar_tensor_tensor(out=o[:, cs_], in0=xc[:, cs_], scalar=scal[0:64, 2:3], in1=py[0:64, 0:128],
                                                               op0=ALU.mult, op1=ALU.add), r=[xc, scal, py], w=[o])
            s.op("dve", lambda: nc.vector.scalar_tensor_tensor(out=Snew[:, :], in0=Sold[:, :], scalar=cl[:, 4:5], in1=dS[:, :],
                                                               op0=ALU.mult, op1=ALU.add), r=[Sold, cl, dS], w=[Snew])
            yield
        gens = [sc_gen(j) for j in range(NS)]
        while gens:
            for g_ in list(gens):
                try:
                    next(g_)
                except StopIteration:
                    gens.remove(g_)
            yield 10.0 * NS / 9.0
        s.dma("sp", D["o_ssd"][:, t0:t0 + TB], o[:], r=[o], w=[(D["o_ssd"], b)], is_output=True)


def mla_masks(par):
    k = np.arange(128)[:, None]
    q = np.arange(128)[None, :]
    diag = np.where((k >= 64) & (q < 64), 0.0, 1.0).astype(np.float32)
    if par == 0:
        mA, mB = diag, np.zeros((128, 128), np.float32)
    else:
        mA, mB = np.ones((128, 128), np.float32), diag
    return np.concatenate([mA, mB], axis=1)


def mixer_mla(s, nc, cn, S, D, psA, psOt):
    NKB = S // 128
    NQB = NKB // 2
    Sq = S // 2
    kn = s.sb("m_kn", [128, S], BF16)
    kr = s.sb("m_kr", [64, S], BF16)
    qn = s.sb("m_qn", [128, Sq], BF16)
    qr = s.sb("m_qr", [64, Sq], BF16)
    v = s.sb("m_v", [128, NKB, 130], BF16)
    msk = s.sb("m_msk", [128, 256], BF16)
    mskf = s.sb("m_mskf", [128, 256], F32)
    s.dma("sp", kn[:], D["mla_kn"][:], r=[D["mla_kn"]], w=[kn])
    s.dma("sp", kr[:], D["mla_kr"][:], r=[D["mla_kr"]], w=[kr])
    s.dma("sp", qn[:], D["mla_qn"][:], r=[D["mla_qn"]], w=[qn])
    s.dma("sp", qr[:], D["mla_qr"][:], r=[D["mla_qr"]], w=[qr])
    s.op("pool", lambda: nc.gpsimd.memset(v[:, :, 128:130], 1.0), w=[(v, "ones")])
    s.dma("sp", v[:, :, 0:128], D["mla_v"][:].rearrange("(b p) e -> p b e", p=128), r=[D["mla_v"]], w=[(v, "data")])
    s.dma("sp", mskf[:], D["mla_mask"][:], r=[D["mla_mask"]], w=[mskf])
    s.op("dve", lambda: nc.vector.tensor_copy(out=msk[:, :], in_=mskf[:, :]), r=[mskf], w=[msk])
    pT = Rot([s.sb("m_pT%d" % i, [128, 512], BF16) for i in range(3)])
    rc = Rot([s.sb("m_rc%d" % i, [128, 1], F32) for i in range(2)])
    ost = Rot([s.sb("m_o%d" % i, [128, 128], F32) for i in range(2)])
    yield 0.0
    groups = []
    for i in range(NQB):
        nkb = 2 * i + 2
        for g0 in range(0, nkb, 4):
            groups.append((i, g0, min(4, nkb - g0), nkb))

    def emit_scores(gi):
        i, g0, gn, nkb = groups[gi]
        ps = psA[gi % 2]
        qs = slice(i * 128, (i + 1) * 128)
        for j in range(gn):
            kb = g0 + j
            ks = slice(kb * 128, (kb + 1) * 128)
            s.op("pe", lambda: nc.tensor.matmul(ps[:, j * 128:(j + 1) * 128], kn[:, ks], qn[:, qs], start=True, stop=False),
                 r=[kn, qn], w=[ps])
            s.op("pe", lambda: nc.tensor.matmul(ps[:, j * 128:(j + 1) * 128], kr[:, ks], qr[:, qs], start=False, stop=True),
                 r=[kr, qr], w=[ps])
        pt = pT.get()
        s.op("act", lambda: nc.scalar.activation(out=pt[:, 0:gn * 128], in_=ps[:, 0:gn * 128], func=AF.Exp), r=[ps], w=[pt])
        for j in range(gn):
            kb = g0 + j
            if kb >= 2 * i:
                mo = (kb - 2 * i) * 128
                s.op("dve", lambda: nc.vector.tensor_tensor(out=pt[:, j * 128:(j + 1) * 128], in0=pt[:, j * 128:(j + 1) * 128],
                                                            in1=msk[:, mo:mo + 128], op=ALU.mult), r=[pt, msk], w=[pt])
        return pt

    def emit_pv(gi, pt):
        i, g0, gn, nkb = groups[gi]
        Ot = psOt[i % 2]
        for j in range(gn):
            kb = g0 + j
            s.op("pe", lambda: nc.tensor.matmul(Ot[:, 0:129], pt[:, j * 128:(j + 1) * 128], v[:, kb, 0:129],
                                                start=(kb == 0), stop=(kb == nkb - 1)), r=[pt, v], w=[Ot])
        if g0 + gn == nkb:
            r_ = rc.get()
            s.op("dve", lambda: nc.vector.reciprocal(out=r_[:, :], in_=Ot[:, 128:129]), r=[Ot], w=[r_])
            o = ost.get()
            s.op("act", lambda: nc.scalar.activation(out=o[:, :], in_=Ot[:, 0:128], func=AF.Copy, scale=r_[:, 0:1]), r=[Ot, r_], w=[o])
            s.dma("sp", D["o_mla"][i * 128:(i + 1) * 128, :], o[:, :], r=[o], w=[(D["o_mla"], i)], is_output=True)

    pts = {0: emit_scores(0)}
    for gi in range(len(groups)):
        if gi + 1 < len(groups):
            pts[gi + 1] = emit_scores(gi + 1)
        emit_pv(gi, pts.pop(gi))
        yield 0.8 * groups[gi][2]


def rwkv_masks():
    s_ = np.arange(128)[:, None]
    t_ = np.arange(128)[None, :]
    m = np.zeros((128, 4, 128), np.float32)
    m[:, 0, :] = (s_ < t_)
    m[:, 1, :] = (s_ > t_)
    m[:, 2, :] = (s_ <= t_)
    m[:, 3, :] = (s_ == t_)
    return m


def mixer_rwkv(s, nc, cn, pp, S, TB, D, has_vres):
    NB = S // TB
    NCH = TB // 128
    EH = math.exp(-0.5)
    ones = cn["ones_f"]
    ident = cn["ident"]
    msk = s.sb("w_msk", [128, 4, 128], F32)
    s.dma("sp", msk[:], D["rw_masks"][:], r=[D["rw_masks"]], w=[msk])
    c64 = s.sb("w_c64", [64, 16], F32)
    s.dma("sp", c64[:], D["rw_c64"][:], r=[D["rw_c64"]], w=[c64])
    c128 = s.sb("w_c128", [128, 5], F32)
    s.dma("sp", c128[:], D["rw_c128"][:], r=[D["rw_c128"]], w=[c128])
    w2 = s.sb("w_w2", [64, 64], F32)
    a2 = s.sb("w_a2", [64, 64], F32)
    g2 = s.sb("w_g2", [128, 64], F32)
    for t_, n_ in ((w2, "rw_w2"), (a2, "rw_a2"), (g2, "rw_g2")):
        s.dma("sp", t_[:], D[n_][:], r=[D[n_]], w=[t_])
    if has_vres:
        v1 = s.sb("w_v1", [128, 4, 32], F32)
        v2 = s.sb("w_v2", [32, 64], F32)
        s.dma("sp", v1[:], D["rw_v1"][:], r=[D["rw_v1"]], w=[v1])
        s.dma("sp", v2[:], D["rw_v2"][:], r=[D["rw_v2"]], w=[v2])
    s.op("dve", lambda: nc.vector.tensor_scalar(out=c64[:, 13:14], in0=c64[:, 8:9], scalar1=-1.0, scalar2=1.0, op0=ALU.mult, op1=ALU.add),
         r=[c64], w=[c64])
    HW = TB + 1
    Ain = Rot([s.sb("w_Ain%d" % i, [64, 5, HW], F32) for i in range(1)])
    Gin = Rot([s.sb("w_Gin%d" % i, [128, HW], F32) for i in range(2)])
    pm = s.sb("w_pm", [64, 5, TB], F32)
    pg = s.sb("w_pg", [128, TB], F32)
    dtmp = Rot([s.sb("w_dt%d" % i, [128, TB], F32) for i in range(2)])
    if has_vres:
        Vin = Rot([s.sb("w_Vin%d" % i, [128, 4, HW], F32) for i in range(1)])
        pv = s.sb("w_pv", [128, 4, TB], F32)
        vfin = Rot([s.sb("w_vf%d" % i, [64, TB], F32) for i in range(2)])
        t1s = s.sb("w_t1s", [32, TB], F32)
    names = ["sg", "asig", "gT", "kk", "kmod", "bvec", "lw", "cum", "epos", "eprev", "eneg", "Rt", "At", "Bt", "Kt", "bonus", "yT", "t64a", "t64b"]
    T = {n: s.sb("w_" + n, [64, TB], F32) for n in names}
    ob = Rot([s.sb("w_o%d" % i, [64, TB], F32) for i in range(2)])
    M = [s.sb("w_M%d" % i, [64, 64], F32) for i in range(2)]
    s.op("pool", lambda: nc.gpsimd.memset(M[0][:], 0.0), w=[M[0]])
    Mp = s.sb("w_Mp", [64, 64], F32)
    sqs = {}
    for n in ("P", "Q", "Z"):
        sqs[n] = [[s.sb("w_%s%d_%d" % (n, j, i), [128, 128], F32) for i in range(2)] for j in range(NCH)]
    for n in ("Zf", "NakT", "NrbT", "NrkT"):
        sqs[n] = [s.sb("w_%s%d" % (n, j), [128, 128], F32) for j in range(NCH)]
    tms = {n: [s.sb("w_%s%d" % (n, j), [128, 64], F32) for j in range(NCH)] for n in ("Btm", "Ktm", "Vtm", "W2")}
    tm = {n: Rot([s.sb("w_%s%d" % (n, i), [128, 64], F32) for i in range(3)]) for n in ("RHS", "U")}
    KVp = [s.sb("w_KVp%d" % j, [64, 64], F32) for j in range(NCH)]
    mi = 0
    for b in range(NB):
        t0 = b * TB
        A_, G_ = Ain.get(), Gin.get()
        loads = [(A_, "rw_A", True), (G_, "rw_G", False)]
        if has_vres:
            V_ = Vin.get()
            loads.append((V_, "rw_V", True))
        for (buf, name, three) in loads:
            if b == 0:
                if three:
                    s.op("pool", lambda: nc.gpsimd.memset(buf[:, :, 0:1], 0.0), w=[buf])
                    s.dma("sp", buf[:, :, 1:HW], D[name][:, :, 0:TB], r=[D[name]], w=[buf])
                else:
                    s.op("pool", lambda: nc.gpsimd.memset(buf[:, 0:1], 0.0), w=[buf])
                    s.dma("sp", buf[:, 1:HW], D[name][:, 0:TB], r=[D[name]], w=[buf])
            else:
                if three:
                    s.dma("sp", buf[:, :, :], D[name][:, :, t0 - 1:t0 + TB], r=[D[name]], w=[buf])
                else:
                    s.dma("sp", buf[:, :], D[name][:, t0 - 1:t0 + TB], r=[D[name]], w=[buf])
        for j in range(5):
            dd = dtmp.get()
            s.op("pool", lambda: nc.gpsimd.tensor_tensor(out=dd[0:64, :], in0=A_[:, j, 0:TB], in1=A_[:, j, 1:HW], op=ALU.subtract), r=[A_], w=[dd])
            s.op("dve", lambda: nc.vector.scalar_tensor_tensor(out=pm[:, j, :], in0=dd[0:64, :], scalar=c64[:, j:j + 1], in1=A_[:, j, 1:HW],
                                                               op0=ALU.mult, op1=ALU.add), r=[dd, c64, A_], w=[(pm, j)])
        dd = dtmp.get()
        s.op("pool", lambda: nc.gpsimd.tensor_tensor(out=dd[:, :], in0=G_[:, 0:TB], in1=G_[:, 1:HW], op=ALU.subtract), r=[G_], w=[dd])
        s.op("dve", lambda: nc.vector.scalar_tensor_tensor(out=pg[:, :], in0=dd[:, :], scalar=c128[:, 0:1], in1=G_[:, 1:HW],
                                                           op0=ALU.mult, op1=ALU.add), r=[dd, c128, G_], w=[pg])
        if has_vres:
            for j in range(4):
                dd = dtmp.get()
                s.op("pool", lambda: nc.gpsimd.tensor_tensor(out=dd[:, :], in0=V_[:, j, 0:TB], in1=V_[:, j, 1:HW], op=ALU.subtract), r=[V_], w=[dd])
                s.op("dve", lambda: nc.vector.scalar_tensor_tensor(out=pv[:, j, :], in0=dd[:, :], scalar=c128[:, 1 + j:2 + j], in1=V_[:, j, 1:HW],
                                                                   op0=ALU.mult, op1=ALU.add), r=[dd, c128, V_], w=[(pv, j)])
        pr, pk, pvh, pwl, pal = (pm[:, j, :] for j in range(5))
        R5 = [(pm, j) for j in range(5)]
        yield 8.0
        s.op("act", lambda: nc.scalar.activation(out=T["t64a"][:, :], in_=pwl, func=AF.Tanh), r=[R5[3]], w=[T["t64a"]])
        p_ = pp.get()
        s.op("pe", lambda: nc.tensor.matmul(p_[0:64, 0:TB], w2[:, :], T["t64a"][:, :], start=True, stop=True), r=[w2, T["t64a"]], w=[p_])
        s.op("act", lambda: nc.scalar.activation(out=T["sg"][:, :], in_=p_[0:64, 0:TB], func=AF.Sigmoid, bias=c64[:, 5:6]), r=[p_, c64], w=[T["sg"]])
        s.op("dve", lambda: nc.vector.tensor_scalar(out=T["lw"][:, :], in0=T["sg"][:, :], scalar1=-EH, scalar2=None, op0=ALU.mult), r=[T["sg"]], w=[T["lw"]])
        p_ = pp.get()
        s.op("pe", lambda: nc.tensor.matmul(p_[0:64, 0:TB], a2[:, :], pal, start=True, stop=True), r=[a2, R5[4]], w=[p_])
        s.op("act", lambda: nc.scalar.activation(out=T["asig"][:, :], in_=p_[0:64, 0:TB], func=AF.Sigmoid, bias=c64[:, 6:7]), r=[p_, c64], w=[T["asig"]])
        dd = dtmp.get()
        s.op("act", lambda: nc.scalar.activation(out=dd[:, :], in_=pg[:, :], func=AF.Sigmoid), r=[pg], w=[dd])
        p_ = pp.get()
        s.op("pe", lambda: nc.tensor.matmul(p_[0:64, 0:TB], g2[:, :], dd[:, :], start=True, stop=True), r=[g2, dd], w=[p_])
        s.op("act", lambda: nc.scalar.copy(out=T["gT"][:, :], in_=p_[0:64, 0:TB]), r=[p_], w=[T["gT"]])
        if has_vres:
            p_ = pp.get()
            for j in range(4):
                s.op("pe", lambda: nc.tensor.matmul(p_[0:32, 0:TB], v1[:, j, :], pv[:, j, :], start=(j == 0), stop=(j == 3)), r=[v1, (pv, j)], w=[p_])
            s.op("act", lambda: nc.scalar.copy(out=t1s[:, :], in_=p_[0:32, 0:TB]), r=[p_], w=[t1s])
            p_ = pp.get()
            s.op("pe", lambda: nc.tensor.matmul(p_[0:64, 0:TB], v2[:, :], t1s[:, :], start=True, stop=True), r=[v2, t1s], w=[p_])
            s.op("act", lambda: nc.scalar.activation(out=T["t64a"][:, :], in_=p_[0:64, 0:TB], func=AF.Sigmoid, bias=c64[:, 12:13]), r=[p_, c64], w=[T["t64a"]])
            vf = vfin.get()
            s.dma("sp", vf[:], D["rw_vf"][:, t0:t0 + TB], r=[D["rw_vf"]], w=[vf])
            s.op("dve", lambda: nc.vector.tensor_tensor(out=T["t64b"][:, :], in0=vf[:, :], in1=pvh, op=ALU.subtract), r=[vf, R5[2]], w=[T["t64b"]])
            s.op("dve", lambda: nc.vector.tensor_tensor(out=T["t64b"][:, :], in0=T["t64b"][:, :], in1=T["t64a"][:, :], op=ALU.mult), r=[T["t64b"], T["t64a"]], w=[T["t64b"]])
            s.op("dve", lambda: nc.vector.tensor_tensor(out=pvh, in0=pvh, in1=T["t64b"][:, :], op=ALU.add), r=[R5[2], T["t64b"]], w=[R5[2]])
        else:
            s.dma("sp", D["o_vf"][:, t0:t0 + TB], pvh, r=[R5[2]], w=[(D["o_vf"], b)], is_output=True)
        yield 8.0
        s.op("dve", lambda: nc.vector.tensor_scalar(out=T["kk"][:, :], in0=pk, scalar1=c64[:, 7:8], scalar2=None, op0=ALU.mult), r=[R5[1], c64], w=[T["kk"]])
        s.op("act", lambda: nc.scalar.activation(out=T["t64a"][:, :], in_=T["kk"][:, :], func=AF.Square), r=[T["kk"]], w=[T["t64a"]])
        p_ = pp.get()
        s.op("pe", lambda: nc.tensor.matmul(p_[0:64, 0:TB], ones[0:64, 0:64], T["t64a"][:, :], start=True, stop=True), r=[ones, T["t64a"]], w=[p_])
        s.op("act", lambda: nc.scalar.activation(out=T["t64b"][:, :], in_=p_[0:64, 0:TB], func=AF.Sqrt), r=[p_], w=[T["t64b"]])
        s.op("dve", lambda: nc.vector.tensor_scalar(out=T["t64b"][:, :], in0=T["t64b"][:, :], scalar1=1e-12, scalar2=None, op0=ALU.max), r=[T["t64b"]], w=[T["t64b"]])
        s.op("dve", lambda: nc.vector.reciprocal(out=T["t64b"][:, :], in_=T["t64b"][:, :]), r=[T["t64b"]], w=[T["t64b"]])
        s.op("dve", lambda: nc.vector.tensor_tensor(out=T["kk"][:, :], in0=T["kk"][:, :], in1=T["t64b"][:, :], op=ALU.mult), r=[T["kk"], T["t64b"]], w=[T["kk"]])
        s.op("dve", lambda: nc.vector.tensor_scalar(out=T["t64a"][:, :], in0=T["asig"][:, :], scalar1=c64[:, 8:9], scalar2=c64[:, 13:14], op0=ALU.mult, op1=ALU.add),
             r=[T["asig"], c64], w=[T["t64a"]])
        s.op("dve", lambda: nc.vector.tensor_tensor(out=T["kmod"][:, :], in0=pk, in1=T["t64a"][:, :], op=ALU.mult), r=[R5[1], T["t64a"]], w=[T["kmod"]])
        s.op("pool", lambda: nc.gpsimd.tensor_tensor(out=T["bvec"][:, :], in0=T["kk"][:, :], in1=T["asig"][:, :], op=ALU.mult), r=[T["kk"], T["asig"]], w=[T["bvec"]])
        yield 6.0
        for j in range(NCH):
            cs_ = slice(j * 128, (j + 1) * 128)
            s.op("dve", lambda: nc.vector.tensor_tensor_scan(out=T["cum"][:, cs_], data0=ones[0:64, 0:128], data1=T["lw"][:, cs_], initial=0.0,
                                                             op0=ALU.mult, op1=ALU.add), r=[ones, T["lw"]], w=[T["cum"]])
        s.op("act", lambda: nc.scalar.activation(out=T["epos"][:, :], in_=T["cum"][:, :], func=AF.Exp), r=[T["cum"]], w=[T["epos"]])
        s.op("act", lambda: nc.scalar.activation(out=T["eneg"][:, :], in_=T["cum"][:, :], func=AF.Exp, scale=-1.0), r=[T["cum"]], w=[T["eneg"]])
        s.op("pool", lambda: nc.gpsimd.tensor_tensor(out=T["t64a"][:, :], in0=T["cum"][:, :], in1=T["lw"][:, :], op=ALU.subtract), r=[T["cum"], T["lw"]], w=[T["t64a"]])
        s.op("act", lambda: nc.scalar.activation(out=T["eprev"][:, :], in_=T["t64a"][:, :], func=AF.Exp), r=[T["t64a"]], w=[T["eprev"]])
        s.op("dve", lambda: nc.vector.tensor_tensor(out=T["Rt"][:, :], in0=pr, in1=T["epos"][:, :], op=ALU.mult), r=[R5[0], T["epos"]], w=[T["Rt"]])
        s.op("dve", lambda: nc.vector.scalar_tensor_tensor(out=T["At"][:, :], in0=T["kk"][:, :], scalar=-1.0, in1=T["eprev"][:, :], op0=ALU.mult, op1=ALU.mult),
             r=[T["kk"], T["eprev"]], w=[T["At"]])
        s.op("pool", lambda: nc.gpsimd.tensor_tensor(out=T["Bt"][:, :], in0=T["bvec"][:, :], in1=T["eneg"][:, :], op=ALU.mult), r=[T["bvec"], T["eneg"]], w=[T["Bt"]])
        s.op("dve", lambda: nc.vector.tensor_tensor(out=T["Kt"][:, :], in0=T["kmod"][:, :], in1=T["eneg"][:, :], op=ALU.mult), r=[T["kmod"], T["eneg"]], w=[T["Kt"]])
        s.op("dve", lambda: nc.vector.scalar_tensor_tensor(out=T["t64b"][:, :], in0=pr, scalar=c64[:, 9:10], in1=T["kmod"][:, :], op0=ALU.mult, op1=ALU.mult),
             r=[R5[0], c64, T["kmod"]], w=[T["t64b"]])
        p_ = pp.get()
        s.op("pe", lambda: nc.tensor.matmul(p_[0:64, 0:TB], ones[0:64, 0:64], T["t64b"][:, :], start=True, stop=True), r=[ones, T["t64b"]], w=[p_])
        s.op("dve", lambda: nc.vector.tensor_tensor(out=T["bonus"][:, :], in0=p_[0:64, 0:TB], in1=pvh, op=ALU.mult), r=[p_, R5[2]], w=[T["bonus"]])

        yield 8.0
        ch = {}

        def chunk_prep(j):
            cs_ = slice(j * 128, (j + 1) * 128)
            At, Bt, Kt, Rt = T["At"][:, cs_], T["Bt"][:, cs_], T["Kt"][:, cs_], T["Rt"][:, cs_]

            def gram(lhs, lhs_t, rhs, rhs_t, mk, dst):
                pq = pp.get()
                s.op("pe", lambda: nc.tensor.matmul(pq[:, 0:128], lhs, rhs, start=True, stop=True), r=[lhs_t, rhs_t], w=[pq])
                s.op("dve", lambda: nc.vector.tensor_tensor(out=dst[:, :], in0=pq[:, 0:128], in1=msk[:, mk, :], op=ALU.mult), r=[pq, msk], w=[dst])

            def tpose(src_, src_t, dst):
                pq = pp.get()
                s.op("pe", lambda: nc.tensor.transpose(pq[:, 0:64], src_, ident[0:64, 0:64]), r=[src_t, ident], w=[pq])
                s.op("act", lambda: nc.scalar.copy(out=dst[:, :], in_=pq[:, 0:64]), r=[pq], w=[dst])
            Pb, Qb, Zb = sqs["P"][j], sqs["Q"][j], sqs["Z"][j]
            P, Q, Z = Pb[0], Qb[0], Zb[0]
            NakT, NrbT, NrkT = sqs["NakT"][j], sqs["NrbT"][j], sqs["NrkT"][j]
            Btm, Ktm, Vtm, W2 = tms["Btm"][j], tms["Ktm"][j], tms["Vtm"][j], tms["W2"][j]
            kvp = KVp[j]
            gram(Bt, T["Bt"], At, T["At"], 0, P)
            yield
            gram(At, T["At"], Bt, T["Bt"], 1, Q)
            yield
            gram(Kt, T["Kt"], At, T["At"], 0, NakT)
            yield
            gram(Bt, T["Bt"], Rt, T["Rt"], 2, NrbT)
            yield
            gram(Kt, T["Kt"], Rt, T["Rt"], 2, NrkT)
            s.op("pool", lambda: nc.gpsimd.tensor_tensor(out=Z[:, :], in0=P[:, :], in1=msk[:, 3, :], op=ALU.add), r=[P, msk], w=[Z])
            yield
            tpose(Bt, T["Bt"], Btm)
            yield
            tpose(Kt, T["Kt"], Ktm)
            yield
            tpose(pm[:, 2, cs_], R5[2], Vtm)
            yield
            pq = pp.get()
            s.op("pe", lambda: nc.tensor.matmul(pq[:, 0:64], NakT[:, :], Vtm[:, :], start=True, stop=True), r=[NakT, Vtm], w=[pq])
            s.op("act", lambda: nc.scalar.copy(out=W2[:, :], in_=pq[:, 0:64]), r=[pq], w=[W2])
            yield
            e_ = j * 128 + 127
            pL = T["epos"][:, e_:e_ + 1]
            pq = pp.get()
            s.op("pe", lambda: nc.tensor.matmul(pq[0:64, 0:64], Ktm[:, :], Vtm[:, :], start=True, stop=True), r=[Ktm, Vtm], w=[pq])
            s.op("act", lambda: nc.scalar.activation(out=kvp[:, :], in_=pq[0:64, 0:64], func=AF.Copy, scale=pL), r=[pq, T["epos"]], w=[kvp])
            yield
            for lv in range(6):
                Qn = Qb[(lv + 1) % 2]
                pq = pp.get()
                s.op("pe", lambda: nc.tensor.matmul(pq[:, 0:128], P[:, :], Q[:, :], start=True, stop=True), r=[P, Q], w=[pq])
                s.op("act", lambda: nc.scalar.copy(out=Qn[:, :], in_=pq[:, 0:128]), r=[pq], w=[Qn])
                yield
                if lv < 5:
                    Pn = Pb[(lv + 1) % 2]
                    pq2 = pp.get()
                    s.op("pe", lambda: nc.tensor.matmul(pq2[:, 0:128], Q[:, :], P[:, :], start=True, stop=True), r=[P, Q], w=[pq2])
                    s.op("pool" if False else "dve", lambda: nc.vector.tensor_copy(out=Pn[:, :], in_=pq2[:, 0:128]), r=[pq2], w=[Pn])
                    yield
                pz = pp.get()
                s.op("pe", lambda: nc.tensor.matmul(pz[:, 0:128], Qn[:, :], Z[:, :], start=True, stop=True), r=[Qn, Z], w=[pz])
                Zn = Zb[(lv + 1) % 2] if lv < 5 else sqs["Zf"][j]
                s.op("dve", lambda: nc.vector.tensor_tensor(out=Zn[:, :], in0=pz[:, 0:128], in1=Z[:, :], op=ALU.add), r=[pz, Z], w=[Zn])
                yield
                Z = Zn
                Q = Qn
                if lv < 5:
                    P = Pn
            ch[j] = dict(Z=Z, NrbT=NrbT, NrkT=NrkT, Btm=Btm, Vtm=Vtm, W2=W2, kvp=kvp, pL=pL)

        gens = [chunk_prep(j) for j in range(NCH)]
        while gens:
            for g_ in list(gens):
                try:
                    next(g_)
                except StopIteration:
                    gens.remove(g_)
            yield 2.0

        for j in range(NCH):
            cs_ = slice(j * 128, (j + 1) * 128)
            At, Rt = T["At"][:, cs_], T["Rt"][:, cs_]
            c_ = ch[j]
            Z, NrbT, NrkT, Btm, Vtm, W2, kvp, pL = (c_[k] for k in ("Z", "NrbT", "NrkT", "Btm", "Vtm", "W2", "kvp", "pL"))
            Mo, Mn = M[mi % 2], M[(mi + 1) % 2]
            mi += 1
            s.op("dve", lambda: nc.vector.scalar_tensor_tensor(out=Mp[:, :], in0=Mo[:, :], scalar=pL, in1=kvp[:, :], op0=ALU.mult, op1=ALU.add),
                 r=[Mo, T["epos"], kvp], w=[Mp])
            p1 = pp.get()
            s.op("pe", lambda: nc.tensor.matmul(p1[:, 0:64], At, Mo[:, :], start=True, stop=True), r=[T["At"], Mo], w=[p1])
            RHS = tm["RHS"].get()
            s.op("dve", lambda: nc.vector.tensor_tensor(out=RHS[:, :], in0=p1[:, 0:64], in1=W2[:, :], op=ALU.add), r=[p1, W2], w=[RHS])
            yield 1.25
            p2 = pp.get()
            s.op("pe", lambda: nc.tensor.matmul(p2[:, 0:64], Z[:, :], RHS[:, :], start=True, stop=True), r=[Z, RHS], w=[p2])
            U = tm["U"].get()
            s.op("act", lambda: nc.scalar.copy(out=U[:, :], in_=p2[:, 0:64]), r=[p2], w=[U])
            yield 1.25
            p3 = pp.get()
            s.op("pe", lambda: nc.tensor.matmul(p3[0:64, 0:64], Btm[:, :], U[:, :], start=True, stop=True), r=[Btm, U], w=[p3])
            s.op("dve", lambda: nc.vector.scalar_tensor_tensor(out=Mn[:, :], in0=p3[0:64, 0:64], scalar=pL, in1=Mp[:, :], op0=ALU.mult, op1=ALU.add),
                 r=[p3, T["epos"], Mp], w=[Mn])
            yield 1.25
            p4 = pp.get()
            s.op("pe", lambda: nc.tensor.matmul(p4[0:64, 0:128], Mo[:, :], Rt, start=True, stop=False), r=[Mo, T["Rt"]], w=[p4])
            s.op("pe", lambda: nc.tensor.matmul(p4[0:64, 0:128], U[:, :], NrbT[:, :], start=False, stop=False), r=[U, NrbT], w=[p4])
            s.op("pe", lambda: nc.tensor.matmul(p4[0:64, 0:128], Vtm[:, :], NrkT[:, :], start=False, stop=True), r=[Vtm, NrkT], w=[p4])
            s.op("act", lambda: nc.scalar.copy(out=T["yT"][:, cs_], in_=p4[0:64, 0:128]), r=[p4], w=[T["yT"]])
            yield 1.25

        y = T["yT"]
        s.op("act", lambda: nc.scalar.activation(out=T["t64a"][:, :], in_=y[:, :], func=AF.Square), r=[y], w=[T["t64a"]])
        pm_ = pp.get()
        s.op("pe", lambda: nc.tensor.matmul(pm_[0:64, 0:TB], ones[0:64, 0:64], y[:, :], start=True, stop=True), r=[ones, y], w=[pm_])
        pe_ = pp.get()
        s.op("pe", lambda: nc.tensor.matmul(pe_[0:64, 0:TB], ones[0:64, 0:64], T["t64a"][:, :], start=True, stop=True), r=[ones, T["t64a"]], w=[pe_])
        mean = T["t64b"]
        s.op("act", lambda: nc.scalar.activation(out=mean[:, :], in_=pm_[0:64, 0:TB], func=AF.Copy, scale=1.0 / 64), r=[pm_], w=[mean])
        s.op("dve", lambda: nc.vector.tensor_tensor(out=T["t64a"][:, :], in0=mean[:, :], in1=mean[:, :], op=ALU.mult), r=[mean], w=[T["t64a"]])
        s.op("dve", lambda: nc.vector.scalar_tensor_tensor(out=T["t64a"][:, :], in0=pe_[0:64, 0:TB], scalar=1.0 / 64, in1=T["t64a"][:, :], op0=ALU.mult, op1=ALU.subtract),
             r=[pe_, T["t64a"]], w=[T["t64a"]])
        s.op("act", lambda: nc.scalar.activation(out=T["t64a"][:, :], in_=T["t64a"][:, :], func=AF.Sqrt, bias=RWKV_GN_EPS), r=[T["t64a"]], w=[T["t64a"]])
        s.op("dve", lambda: nc.vector.reciprocal(out=T["t64a"][:, :], in_=T["t64a"][:, :]), r=[T["t64a"]], w=[T["t64a"]])
        s.op("dve", lambda: nc.vector.tensor_tensor(out=mean[:, :], in0=y[:, :], in1=mean[:, :], op=ALU.subtract), r=[y, mean], w=[mean])
        s.op("dve", lambda: nc.vector.tensor_tensor(out=mean[:, :], in0=mean[:, :], in1=T["t64a"][:, :], op=ALU.mult), r=[mean, T["t64a"]], w=[mean])
        s.op("dve", lambda: nc.vector.tensor_scalar(out=mean[:, :], in0=mean[:, :], scalar1=c64[:, 10:11], scalar2=c64[:, 11:12], op0=ALU.mult, op1=ALU.add),
             r=[mean, c64], w=[mean])
        s.op("pool", lambda: nc.gpsimd.tensor_tensor(out=mean[:, :], in0=mean[:, :], in1=T["bonus"][:, :], op=ALU.add), r=[mean, T["bonus"]], w=[mean])
        o = ob.get()
        s.op("dve", lambda: nc.vector.tensor_tensor(out=o[:, :], in0=mean[:, :], in1=T["gT"][:, :], op=ALU.mult), r=[mean, T["gT"]], w=[o])
        s.dma("sp", D["o_rwkv"][:, t0:t0 + TB], o[:, :], r=[o], w=[(D["o_rwkv"], b)], is_output=True)
        yield 8.0


def build_B(cfg, has_vres, parts=("ret", "ssd", "rwkv", "mla")):
    S = cfg.S
    TB = min(512, S)
    Sq = S // 2
    nc = bass.Bass("TRN2", target_bir_lowering=False)
    es = contextlib.ExitStack()
    with es:
        s = Sched(nc, es)
        IN, OUT = "ExternalInput", "ExternalOutput"
        D = {}

        def di(name, shape, dt=F32):
            D[name] = s.dram(name, shape, dt, IN)

        def do(name, shape, dt=F32):
            D[name] = s.dram(name, shape, dt, OUT)
        di("ident", [128, 128])
        for n in ("ret_q", "ret_k", "ret_v"):
            di(n, [128, S])
        di("ret_mask", [128, 128]); di("ret_qdec", [128, 128]); di("ret_col", [128, 2])
        do("o_ret", [128, S])
        di("ssd_x", [64, S]); di("ssd_B", [128, S]); di("ssd_C", [128, S]); di("ssd_dt", [1, S])
        di("ssd_cwx", [64, 5]); di("ssd_cwB", [128, 5]); di("ssd_cwC", [128, 5]); di("ssd_scal", [128, 4]); di("ssd_negmask", [128, 128])
        do("o_ssd", [64, S])
        di("mla_kn", [128, S], BF16); di("mla_kr", [64, S], BF16); di("mla_qn", [128, Sq], BF16); di("mla_qr", [64, Sq], BF16)
        di("mla_v", [S, 128], BF16); di("mla_mask", [128, 256])
        do("o_mla", [Sq, 128])
        di("rw_A", [64, 5, S]); di("rw_G", [128, S]); di("rw_c64", [64, 16]); di("rw_c128", [128, 5])
        di("rw_w2", [64, 64]); di("rw_a2", [64, 64]); di("rw_g2", [128, 64]); di("rw_masks", [128, 4, 128])
        if has_vres:
            di("rw_V", [128, 4, S]); di("rw_vf", [64, S]); di("rw_v1", [128, 4, 32]); di("rw_v2", [32, 64])
        else:
            do("o_vf", [64, S])
        do("o_rwkv", [64, S])
        cn = mk_consts(s, nc)
        cn["ident"] = s.sb("ident_t", [128, 128], F32)
        s.dma("sp", cn["ident"][:], D["ident"][:], r=[D["ident"]], w=[cn["ident"]])
        ptiles = [s.ps("ps%d" % i, [128, 512]) for i in range(8)]
        pp = PsumPool(s, 0, tiles=ptiles[0:4])
        NSC = S // 128
        g1 = None
        if "rwkv" in parts:
            g1 = mixer_rwkv(s, nc, cn, pp, S, TB, D, has_vres)
            a1 = next(g1)
        else:
            a1 = 0.0

        def stream2():
            if "mla" in parts:
                s.push_scope()
                yield from mixer_mla(s, nc, cn, S, D, ptiles[4:6], ptiles[6:8])
                s.pop_scope()
            pp2 = PsumPool(s, 0, tiles=ptiles[4:8])
            if "ret" in parts:
                s.push_scope()
                yield from mixer_ret(s, nc, cn, pp2, S, TB, D)
                s.pop_scope()
            if "ssd" in parts:
                s.push_scope()
                yield from mixer_ssd(s, nc, cn, pp2, S, TB, D)
                s.pop_scope()
        T1 = 116.0 * (S // TB) if "rwkv" in parts else 0.0
        T2 = ((0.8 * (NSC // 2) * (NSC // 2 + 1)) if "mla" in parts else 0.0) + (4.5 * NSC if "ret" in parts else 0.0) \
            + ((12.0 * NSC) if "ssd" in parts else 0.0)
        g2 = stream2()
        a2 = 0.0
        while g1 is not None or g2 is not None:
            if g2 is None or (g1 is not None and a1 / max(T1, 1e-9) <= a2 / max(T2, 1e-9)):
                try:
                    a1 += next(g1)
                except StopIteration:
                    g1 = None
            else:
                try:
                    a2 += next(g2)
                except StopIteration:
                    g2 = None
        s.finish()
        print("build_B instrs", s.n_instr)
    return nc


def host_B_inputs(inp, l, A, d, cfg, v_first):
    S = cfg.S
    m = {}
    m["ident"] = np.eye(128, dtype=np.float32)
    hr = d // 2
    m["ret_q"], m["ret_k"], m["ret_v"] = A["rq"][hr], A["rk"][hr], A["rv"][hr]
    m.update(ret_consts(hr))
    sx = A["sx"].reshape(1024, S)
    g = d // 4
    m["ssd_x"] = sx[d * 64:(d + 1) * 64]
    m["ssd_B"] = sx[512 + g * 128:512 + (g + 1) * 128]
    m["ssd_C"] = sx[768 + g * 128:768 + (g + 1) * 128]
    m["ssd_dt"] = A["sdt"][d:d + 1]
    cw, cb = inp["ssm_conv_w"][l], inp["ssm_conv_b"][l]

    def cwp(ch):
        return np.concatenate([cw[:, ch].T, cb[ch][:, None]], axis=1).astype(np.float32)
    m["ssd_cwx"] = cwp(np.arange(d * 64, (d + 1) * 64))
    m["ssd_cwB"] = cwp(512 + g * 128 + np.arange(128))
    m["ssd_cwC"] = cwp(768 + g * 128 + np.arange(128))
    sc = np.zeros((128, 4), np.float32)
    sc[:, 0] = inp["ssm_dt_bias"][l][d]
    sc[:, 1] = inp["ssm_a_log"][l][d]
    sc[:, 2] = inp["ssm_d"][l][d]
    m["ssd_scal"] = sc
    m.update(ssd_consts())
    hm, par = d // 2, d % 2
    NKB = S // 128
    qsel = np.concatenate([np.arange(b * 128, (b + 1) * 128) for b in range(par, NKB, 2)])
    m["mla_kn"], m["mla_kr"] = A["mkn"][hm], A["mkr"][hm]
    m["mla_qn"], m["mla_qr"] = A["mqn"][hm][:, qsel], A["mqr"][hm][:, qsel]
    m["mla_v"] = A["mv"][:, hm * 128:(hm + 1) * 128]
    m["mla_mask"] = mla_masks(par)
    rwf = A["rw"].reshape(1792, S)
    hd = np.arange(d * 64, (d + 1) * 64)
    m["rw_A"] = np.stack([rwf[hd], rwf[512 + hd], rwf[1024 + hd], rwf[1536:1600], rwf[1600:1664]], axis=1)
    m["rw_G"] = rwf[1664:1792]
    mu = inp["rwkv_mu"][l]
    c64 = np.zeros((64, 16), np.float32)
    c64[:, 0], c64[:, 1], c64[:, 2], c64[:, 3], c64[:, 4] = mu[hd], mu[512 + hd], mu[1024 + hd], mu[1536:1600], mu[1600:1664]
    c64[:, 5], c64[:, 6] = inp["rwkv_w0"][l][hd], inp["rwkv_a0"][l][hd]
    c64[:, 7], c64[:, 8] = inp["rwkv_k_k"][l][hd], inp["rwkv_k_a"][l][hd]
    c64[:, 9] = inp["rwkv_r_k"][l][d]
    c64[:, 10], c64[:, 11] = inp["rwkv_ln_w"][l][hd], inp["rwkv_ln_b"][l][hd]
    if l > 0:
        c64[:, 12] = inp["rwkv_v0"][l - 1][hd]
    m["rw_c64"] = c64
    c128 = np.zeros((128, 5), np.float32)
    c128[:, 0] = mu[1664:1792]
    c128[:, 1:5] = mu[1024:1536].reshape(4, 128).T
    m["rw_c128"] = c128
    m["rw_w2"], m["rw_a2"], m["rw_g2"] = inp["rwkv_w2"][l][:, hd], inp["rwkv_a2"][l][:, hd], inp["rwkv_g2"][l][:, hd]
    m["rw_masks"] = rwkv_masks()
    if l > 0:
        m["rw_V"] = rwf[1024:1536].reshape(4, 128, S).transpose(1, 0, 2)
        m["rw_vf"] = v_first[d]
        m["rw_v1"] = fm(inp["rwkv_v1"][l - 1])
        m["rw_v2"] = inp["rwkv_v2"][l - 1][:, hd]
    return {k: np.ascontiguousarray(v) for k, v in m.items()}


WB_COLS = 8192
DEBUG_SKIP_WDMA = False


def build_C(cfg, debug=False):
    TPC, NT, NP = cfg.TPC, cfg.NT, cfg.NP
    nc = bass.Bass("TRN2", target_bir_lowering=False)
    es = contextlib.ExitStack()
    with es:
        s = Sched(nc, es)
        IN, OUT = "ExternalInput", "ExternalOutput"
        xT = s.dram("xT", [128, KC, TPC], F32, IN)
        oa = s.dram("oa", [4, 128, TPC], F32, IN)
        ob = s.dram("ob", [4, 128, TPC], F32, IN)
        oc = s.dram("oc", [4, 128, TPC], F32, IN)
        od = s.dram("od", [4, 128, TPC], F32, IN)
        cw = s.dram("cw", [128, 40], F32, IN)
        wC1 = s.dram("wC1", [128, KC, 1024], F32, IN)
        wG = s.dram("wG", [128, KC, 8192], F32, IN)
        wBr = s.dram("wBr", [128, 16, 2048], F32, IN)
        wOut = s.dram("wOut", [128, KC, 2048], F32, IN)
        wGU = s.dram("wGU", [128, KC, 2 * D_FF], F32, IN)
        wDn = s.dram("wDn", [128, FFT, 2048], F32, IN)
        o_xT = s.dram("o_xT", [128, KC, TPC], F32, OUT)

        cn = mk_consts(s, nc)
        pp = PsumPool(s, 8)
        cw_t = s.sb("cw_t", [128, 40], F32)
        s.dma("sp", cw_t[:], cw[:], r=[cw], w=[cw_t])
        x_res = s.sb("x_res", [128, KC, TPC], F32)
        for c in range(KC):
            s.dma("sp", x_res[:, c, :], xT[:, c, :], r=[xT], w=[(x_res, c)])
        wbs = Rot([s.sb("wbuf%d" % i, [128, WB_COLS], BF16) for i in range(3)])
        sqrot = Rot([s.sb("sq%d" % i, [128, NT], F32) for i in range(2)])
        rstd_t = s.sb("rstd_t", [128, NT], F32)
        rtmp = s.sb("rtmp", [128, NT], F32)

        def load_w(dram_t, c0, ncols, nk, kbase=0):
            wb = wbs.get()
            view = wb[:, 0:nk * ncols].rearrange("p (c n) -> p c n", c=nk)
            for k0 in range(0, nk, 8):
                k1 = min(nk, k0 + 8)
                if DEBUG_SKIP_WDMA and load_w.count >= 3:
                    continue
                s.dma("pool", view[:, k0:k1, :], dram_t[:, kbase + k0:kbase + k1, c0:c0 + ncols], r=[dram_t], w=[wb])
            load_w.count += 1
            return wb, view
        load_w.count = 0

        for p in range(NP):
            t0 = p * NT
            tsl = slice(t0, t0 + NT)

            def x_src(c):
                return x_res[:, c, tsl], [(x_res, c)]
            s.push_scope()
            hT = s.sb("hT%d" % p, [128, KC, NT], BF16)
            compute_hT(s, nc, cn, pp, x_src, Tn("cw_t", cw_t.t[:, 0:16]), hT, NT, sqrot, rstd_t, rtmp, True)
            on_bf = s.sb("on_bf%d" % p, [128, 16, NT], BF16)
            merged = s.sb("merged%d" % p, [128, KC, NT], BF16)
            wbr = Rot([s.sb("wbr%d_%d" % (i, p), [128, 2048], BF16) for i in range(2)])
            ft = {n: s.sb("c_%s%d" % (n, p), [128, NT], F32) for n in ("x0", "x1", "x2", "x3", "sg", "mean", "var", "t1")}
            ft["gate"], ft["t2"], ft["macc"] = ft["sg"], ft["mean"], ft["var"]
            xin = [ft["x0"], ft["x1"], ft["x2"], ft["x3"]]
            wcur = {}

            def proj(col0, ps):
                wb1, w1v = wcur["w"]
                for c in range(KC):
                    s.op("pe", lambda: nc.tensor.matmul(ps[:, 0:NT], w1v[:, c, col0:col0 + 128], hT[:, c, 0:NT], start=(c == 0), stop=(c == KC - 1)),
                         r=[wb1, (hT, c)], w=[ps])
            wcur["w"] = load_w(wC1, 0, 512, KC)
            for h in range(4):
                x = xin[h % 2]
                s.dma("sp", x[:, :], oa[h, :, tsl], r=[oa], w=[x])
                psg = pp.get()
                proj(h * 128, psg)
                s.op("act", lambda: nc.scalar.activation(out=ft["sg"][:, :], in_=psg[:, 0:NT], func=AF.Silu), r=[psg], w=[ft["sg"]])
                sq = sqrot.get()
                s.op("act", lambda: nc.scalar.activation(out=sq[:, 0:NT], in_=x[:, :], func=AF.Square), r=[x], w=[sq])
                pm_ = pp.get()
                s.op("pe", lambda: nc.tensor.matmul(pm_[:, 0:NT], cn["ones_f"][:, :], x[:, :], start=True, stop=True), r=[cn["ones_f"], x], w=[pm_])
                pe_ = pp.get()
                s.op("pe", lambda: nc.tensor.matmul(pe_[:, 0:NT], cn["ones_f"][:, :], sq[:, 0:NT], start=True, stop=True), r=[cn["ones_f"], sq], w=[pe_])
                s.op("act", lambda: nc.scalar.activation(out=ft["mean"][:, :], in_=pm_[:, 0:NT], func=AF.Copy, scale=1.0 / 128), r=[pm_], w=[ft["mean"]])
                s.op("dve", lambda: nc.vector.tensor_tensor(out=ft["var"][:, :], in0=ft["mean"][:, :], in1=ft["mean"][:, :], op=ALU.mult), r=[ft["mean"]], w=[ft["var"]])
                s.op("dve", lambda: nc.vector.scalar_tensor_tensor(out=ft["var"][:, :], in0=pe_[:, 0:NT], scalar=1.0 / 128, in1=ft["var"][:, :],
                                                                   op0=ALU.mult, op1=ALU.subtract), r=[pe_, ft["var"]], w=[ft["var"]])
                s.op("act", lambda: nc.scalar.activation(out=ft["var"][:, :], in_=ft["var"][:, :], func=AF.Sqrt, bias=GN_EPS), r=[ft["var"]], w=[ft["var"]])
                s.op("dve", lambda: nc.vector.reciprocal(out=ft["var"][:, :], in_=ft["var"][:, :]), r=[ft["var"]], w=[ft["var"]])
                s.op("dve", lambda: nc.vector.tensor_tensor(out=ft["t1"][:, :], in0=x[:, :], in1=ft["mean"][:, :], op=ALU.subtract), r=[x, ft["mean"]], w=[ft["t1"]])
                s.op("dve", lambda: nc.vector.scalar_tensor_tensor(out=ft["t1"][:, :], in0=ft["t1"][:, :], scalar=cw_t[:, 32 + h:33 + h], in1=ft["var"][:, :],
                                                                   op0=ALU.mult, op1=ALU.mult), r=[ft["t1"], cw_t, ft["var"]], w=[ft["t1"]])
                s.op("dve", lambda: nc.vector.tensor_tensor(out=on_bf[:, h, :], in0=ft["t1"][:, :], in1=ft["sg"][:, :], op=ALU.mult),
                     r=[ft["t1"], ft["sg"]], w=[(on_bf, h)])
            wcur["w"] = load_w(wC1, 512, 512, KC)
            for g in range(2):
                for j in range(2):
                    i = g * 2 + j
                    x = xin[i]
                    s.dma("sp", x[:, :], ob[i, :, tsl], r=[ob], w=[x])
                    psz = pp.get()
                    proj(i * 128, psz)
                    s.op("act", lambda: nc.scalar.activation(out=ft["sg"][:, :], in_=psz[:, 0:NT], func=AF.Silu), r=[psz], w=[ft["sg"]])
                    s.op("dve", lambda: nc.vector.tensor_tensor(out=x[:, :], in0=x[:, :], in1=ft["sg"][:, :], op=ALU.mult), r=[x, ft["sg"]], w=[x])
                pss = pp.get()
                for j in range(2):
                    x = xin[g * 2 + j]
                    sq = sqrot.get()
                    s.op("act", lambda: nc.scalar.activation(out=sq[:, 0:NT], in_=x[:, :], func=AF.Square), r=[x], w=[sq])
                    s.op("pe", lambda: nc.tensor.matmul(pss[:, 0:NT], cn["ones_f"][:, :], sq[:, 0:NT], start=(j == 0), stop=(j == 1)),
                         r=[cn["ones_f"], sq], w=[pss])
                rstd_from_psum(s, nc, pss, 128, NT, 1.0 / 256, RMS_EPS, ft["var"], ft["mean"])
                for j in range(2):
                    i = g * 2 + j
                    x = xin[i]
                    s.op("dve", lambda: nc.vector.scalar_tensor_tensor(out=on_bf[:, 4 + i, :], in0=x[:, :], scalar=cw_t[:, 36 + i:37 + i], in1=ft["var"][:, :],
                                                                       op0=ALU.mult, op1=ALU.mult), r=[x, cw_t, ft["var"]], w=[(on_bf, 4 + i)])
            for i in range(4):
                s.dma("pool", on_bf[:, 8 + i, :], oc[i, :, tsl], r=[oc], w=[(on_bf, 8 + i)])
                s.dma("pool", on_bf[:, 12 + i, :], od[i, :, tsl], r=[od], w=[(on_bf, 12 + i)])
            for dt in range(16):
                wbg, wgv = load_w(wG, dt * 512, 512, KC)
                wb_ = wbr.get()
                s.dma("pool", wb_[:, :], wBr[:, dt, :], r=[wBr], w=[wb_])
                for n in range(4):
                    psg = pp.get()
                    for c in range(KC):
                        s.op("pe", lambda: nc.tensor.matmul(psg[:, 0:NT], wgv[:, c, n * 128:(n + 1) * 128], hT[:, c, 0:NT], start=(c == 0), stop=(c == KC - 1)),
                             r=[wbg, (hT, c)], w=[psg])
                    psb = pp.get()
                    for kc in range(4):
                        s.op("pe", lambda: nc.tensor.matmul(psb[:, 0:NT], wb_[:, (n * 4 + kc) * 128:(n * 4 + kc + 1) * 128], on_bf[:, n * 4 + kc, :],
                                                            start=(kc == 0), stop=(kc == 3)), r=[wb_, (on_bf, n * 4 + kc)], w=[psb])
                    s.op("act", lambda: nc.scalar.activation(out=ft["gate"][:, :], in_=psg[:, 0:NT], func=AF.Sigmoid), r=[psg], w=[ft["gate"]])
                    if n == 0:
                        s.op("dve", lambda: nc.vector.tensor_tensor(out=ft["macc"][:, :], in0=psb[:, 0:NT], in1=ft["gate"][:, :], op=ALU.mult),
                             r=[psb, ft["gate"]], w=[ft["macc"]])
                    else:
                        s.op("dve", lambda: nc.vector.tensor_tensor(out=ft["t2"][:, :], in0=psb[:, 0:NT], in1=ft["gate"][:, :], op=ALU.mult),
                             r=[psb, ft["gate"]], w=[ft["t2"]])
                        if n < 3:
                            s.op("dve", lambda: nc.vector.tensor_tensor(out=ft["macc"][:, :], in0=ft["macc"][:, :], in1=ft["t2"][:, :], op=ALU.add),
                                 r=[ft["macc"], ft["t2"]], w=[ft["macc"]])
                        else:
                            s.op("dve", lambda: nc.vector.tensor_tensor(out=merged[:, dt, :], in0=ft["macc"][:, :], in1=ft["t2"][:, :], op=ALU.add),
                                 r=[ft["macc"], ft["t2"]], w=[(merged, dt)])
            for g4 in range(4):
                wbo, wov = load_w(wOut, g4 * 512, 512, KC)
                for j in range(4):
                    d2 = g4 * 4 + j
                    ps = pp.get()
                    for c in range(KC):
                        s.op("pe", lambda: nc.tensor.matmul(ps[:, 0:NT], wov[:, c, j * 128:(j + 1) * 128], merged[:, c, :], start=(c == 0), stop=(c == KC - 1)),
                             r=[wbo, (merged, c)], w=[ps])
                    s.op("dve", lambda: nc.vector.tensor_tensor(out=x_res[:, d2, tsl], in0=x_res[:, d2, tsl], in1=ps[:, 0:NT], op=ALU.add),
                         r=[(x_res, d2), ps], w=[(x_res, d2)])
            if debug and p == 0:
                dbg1 = s.dram("dbg_on", [128, 16, NT], BF16, OUT)
                s.dma("sp", dbg1[:], on_bf[:], r=[on_bf], w=[dbg1], is_output=True)
                dbg2 = s.dram("dbg_mg", [128, 16, NT], BF16, OUT)
                s.dma("sp", dbg2[:], merged[:], r=[merged], w=[dbg2], is_output=True)
                dbg3 = s.dram("dbg_xm", [128, 16, NT], F32, OUT)
                for c in range(KC):
                    s.dma("sp", dbg3[:, c, :], x_res[:, c, tsl], r=[(x_res, c)], w=[(dbg3, c)], is_output=True)
            s.pop_scope()

        s.push_scope()
        h2T = s.sb("h2T", [128, KC, TPC], BF16)
        for p in range(NP):
            tsl = slice(p * NT, (p + 1) * NT)

            def x_src2(c, tsl=tsl):
                return x_res[:, c, tsl], [(x_res, c)]
            compute_hT(s, nc, cn, pp, x_src2, Tn("cw_t", cw_t.t[:, 16:32]), h2T, NT, sqrot, rstd_t, rtmp, True, col0=p * NT)
        HF = FFT // 2
        aT = s.sb("aT", [128, HF, TPC], BF16)
        sgr = Rot([s.sb("f_sg%d" % i, [128, NT], F32) for i in range(2)])
        for half in range(2):
            for gq in range(HF // 2):
                wbf, wfv = load_w(wGU, (half * (HF // 2) + gq) * 512, 512, KC)
                for jj in range(2):
                    jl = gq * 2 + jj
                    for p in range(NP):
                        tsl = slice(p * NT, (p + 1) * NT)
                        psg = pp.get()
                        psu = pp.get()
                        for (ps_, co) in ((psg, jj * 256), (psu, jj * 256 + 128)):
                            for c in range(KC):
                                s.op("pe", lambda: nc.tensor.matmul(ps_[:, 0:NT], wfv[:, c, co:co + 128], h2T[:, c, tsl], start=(c == 0), stop=(c == KC - 1)),
                                     r=[wbf, (h2T, c)], w=[ps_])
                        sg = sgr.get()
                        s.op("act", lambda: nc.scalar.activation(out=sg[:, :], in_=psg[:, 0:NT], func=AF.Silu), r=[psg], w=[sg])
                        s.op("dve", lambda: nc.vector.tensor_tensor(out=aT[:, jl, tsl], in0=psu[:, 0:NT], in1=sg[:, :], op=ALU.mult), r=[psu, sg], w=[(aT, jl)])
            for g8 in range(8):
                wbd, wdv = load_w(wDn, g8 * 256, 256, HF, half * HF)
                for j in range(2):
                    d2 = g8 * 2 + j
                    for p in range(NP):
                        tsl = slice(p * NT, (p + 1) * NT)
                        ps = pp.get()
                        for c in range(HF):
                            s.op("pe", lambda: nc.tensor.matmul(ps[:, 0:NT], wdv[:, c, j * 128:(j + 1) * 128], aT[:, c, tsl], start=(c == 0), stop=(c == HF - 1)),
                                 r=[wbd, (aT, c)], w=[ps])
                        s.op("dve", lambda: nc.vector.tensor_tensor(out=x_res[:, d2, tsl], in0=x_res[:, d2, tsl], in1=ps[:, 0:NT], op=ALU.add),
                             r=[(x_res, d2), ps], w=[(x_res, d2)])
        s.pop_scope()
        for c in range(KC):
            s.dma("sp", o_xT[:, c, :], x_res[:, c, :], r=[(x_res, c)], w=[(o_xT, c)], is_output=True)
        s.finish()
        print("build_C instrs", s.n_instr)
    return nc


def host_C_weights(inp, l):
    w = {}
    cwm = np.zeros((128, 40), np.float32)
    cwm[:, 0:16] = colvec(inp["norm1_w"][l])
    cwm[:, 16:32] = colvec(inp["norm2_w"][l])
    cwm[:, 32:36] = colvec(inp["ret_gn_w"][l])
    cwm[:, 36:40] = colvec(inp["ssm_norm_w"][l])
    w["cw"] = cwm
    win = inp["w_in"][l]
    w["wC1"] = fm(win[:, 1536:2560])
    gcols = np.concatenate([6216 + n * 2048 + dt * 128 + np.arange(128) for dt in range(16) for n in range(4)])
    w["wG"] = fm(win[:, gcols])
    wb = inp["w_branch"][l]
    t = wb.reshape(4, 4, 128, 16, 128)
    w["wBr"] = np.ascontiguousarray(t.transpose(2, 3, 0, 1, 4).reshape(128, 16, 2048))
    w["wOut"] = fm(inp["w_out"][l])
    gu = inp["ffn_w_gu"][l]
    fcols = np.concatenate([np.concatenate([j * 128 + np.arange(128), D_FF + j * 128 + np.arange(128)]) for j in range(FFT)])
    w["wGU"] = fm(gu[:, fcols])
    w["wDn"] = fm(inp["ffn_w_down"][l])
    return w


_NC_CACHE = {}


def _get_nc(kind, cfg, *args):
    key = (kind, cfg.S, cfg.depth) + tuple(args)
    if key not in _NC_CACHE:
        if kind == "A":
            _NC_CACHE[key] = build_A(cfg)
        elif kind == "B":
            _NC_CACHE[key] = build_B(cfg, *args)
        else:
            _NC_CACHE[key] = build_C(cfg)
    return _NC_CACHE[key]


def kernel(**inputs):
    inp = {k: np.asarray(v) for k, v in inputs.items()}
    x = inp["x"][0]
    S = x.shape[0]
    depth = inp["norm1_w"].shape[0]
    cfg = Cfg(S, depth)
    TPC = cfg.TPC
    sls = [slice(c * TPC, (c + 1) * TPC) for c in range(NCORES)]
    xT = [to_fm_tokens(x[sl].astype(np.float32)) for sl in sls]
    pos = [np.ascontiguousarray(inp["positions"][:, sl]).astype(np.int32) for sl in sls]
    v_first = None
    for l in range(depth):
        wA = host_A_weights(inp, l)
        resA = run_spmd(_get_nc("A", cfg), [dict(wA, xT=xT[c], pos=pos[c]) for c in range(NCORES)])
        A = {}
        for k_, n_ in (("rq", "o_rq"), ("rk", "o_rk"), ("rv", "o_rv"), ("sx", "o_sx"), ("sdt", "o_sdt"), ("rw", "o_rw"),
                       ("mqn", "o_mqn"), ("mqr", "o_mqr"), ("mkn", "o_mkn"), ("mkr", "o_mkr")):
            A[k_] = np.concatenate([np.asarray(r[n_]) for r in resA], axis=-1)
        A["mv"] = np.concatenate([np.asarray(r["o_mv"]) for r in resA], axis=0)
        del resA
        resB = run_spmd(_get_nc("B", cfg, l > 0), [host_B_inputs(inp, l, A, d, cfg, v_first) for d in range(NCORES)])
        del A
        if l == 0:
            v_first = [np.asarray(resB[d]["o_vf"]) for d in range(NCORES)]
        oa = np.stack([np.asarray(resB[2 * h]["o_ret"]) for h in range(4)])
        ob = np.concatenate([np.asarray(resB[d]["o_ssd"]) for d in range(NCORES)], axis=0).reshape(4, 128, S)
        od = np.concatenate([np.asarray(resB[d]["o_rwkv"]) for d in range(NCORES)], axis=0).reshape(4, 128, S)
        oc = np.zeros((4, 128, S), np.float32)
        for h in range(4):
            t = np.zeros((S // 128, 128, 128), np.float32)
            t[0::2] = np.asarray(resB[2 * h]["o_mla"]).reshape(-1, 128, 128)
            t[1::2] = np.asarray(resB[2 * h + 1]["o_mla"]).reshape(-1, 128, 128)
            oc[h] = t.reshape(S, 128).T
        del resB
        wC = host_C_weights(inp, l)
        resC = run_spmd(_get_nc("C", cfg), [dict(wC, xT=xT[c], oa=np.ascontiguousarray(oa[:, :, sls[c]]), ob=np.ascontiguousarray(ob[:, :, sls[c]]),
                                                oc=np.ascontiguousarray(oc[:, :, sls[c]]), od=np.ascontiguousarray(od[:, :, sls[c]]))
                                           for c in range(NCORES)])
        xT = [np.asarray(r["o_xT"]) for r in resC]
        del resC, wC, wA
    out = np.concatenate([t.transpose(1, 0, 2).reshape(D_MODEL, TPC).T for t in xT], axis=0)
    return np.ascontiguousarray(out[None]).astype(np.float32)
```
